# Optimizing a Trainium2 kernel written in Bass

```python
import math
import jax
import jax.numpy as jnp
from jax import lax
import numpy as np

D_MODEL = 1024
BATCH = 4
SEQ = 4096
DEPTH = 2
DEC_BATCH = 32
DEC_SEQ = 32
PAST_LEN = 2048

CHUNK = 64
N_META = 16
N_MIXERS = 2
N_GLA_LAYERS = (DEPTH + 1) // 2
N_MLA_LAYERS = DEPTH // 2
D_FF = 2816
RMS_EPS = 1e-6
GLA_HEADS = 4
GLA_KEY_DIM = D_MODEL // 2
GLA_VALUE_DIM = D_MODEL
GLA_DK = GLA_KEY_DIM // GLA_HEADS
GLA_DV = GLA_VALUE_DIM // GLA_HEADS
GLA_GATE_RANK = 16
GLA_GATE_NORMALIZER = 16.0
GLA_BLOCK = CHUNK
GLA_IN_DIM = 2 * GLA_KEY_DIM + 2 * GLA_VALUE_DIM + GLA_GATE_RANK
MLA_HEADS = 8
MLA_Q_RANK = 384
MLA_KV_RANK = 256
MLA_NOPE = 128
MLA_ROPE = 64
MLA_V = 128
MLA_DOWN_DIM = MLA_Q_RANK + MLA_KV_RANK + MLA_ROPE
ROPE_THETA = 10000.0
Q_BLOCK = 128

kernel_name = "hybrid_gla_mla_macaron_stream_step"


def rmsnorm(x, w):
    x32 = x.astype(jnp.float32)
    y = x32 * lax.rsqrt(jnp.mean(x32 * x32, axis=-1, keepdims=True) + RMS_EPS)
    return (y * w.astype(jnp.float32)).astype(x.dtype)


def swiglu_half(h, norm_w, w_gate, w_up, w_down):
    hn = rmsnorm(h, norm_w)
    return h + 0.5 * ((jax.nn.silu(hn @ w_gate) * (hn @ w_up)) @ w_down)


def apply_rope(x, pos):
    half = x.shape[-1] // 2
    inv = ROPE_THETA ** (-jnp.arange(half, dtype=jnp.float32) / half)
    ang = pos[:, None] * inv[None, :]
    cos = jnp.cos(ang)[None, :, None, :]
    sin = jnp.sin(ang)[None, :, None, :]
    x32 = x.astype(jnp.float32)
    x1, x2 = x32[..., :half], x32[..., half:]
    return jnp.concatenate([x1 * cos - x2 * sin, x2 * cos + x1 * sin], axis=-1).astype(x.dtype)


def gla_mixer(xn, s0, w_in, w_gate_up, b_gate, head_norm, w_out):
    B, L, _ = xn.shape
    proj = xn @ w_in
    q, k, v, r, gd = jnp.split(proj, [GLA_KEY_DIM, 2 * GLA_KEY_DIM, 2 * GLA_KEY_DIM + GLA_VALUE_DIM,
                                      2 * GLA_KEY_DIM + 2 * GLA_VALUE_DIM], axis=-1)
    glog = jax.nn.log_sigmoid((gd @ w_gate_up + b_gate).astype(jnp.float32)) / GLA_GATE_NORMALIZER
    pad = (-L) % GLA_BLOCK
    nb = (L + pad) // GLA_BLOCK

    def blocks(t, hd):
        t = t.astype(jnp.float32).reshape(B, L, GLA_HEADS, hd)
        t = jnp.pad(t, ((0, 0), (pad, 0), (0, 0), (0, 0)))
        return t.reshape(B, nb, GLA_BLOCK, GLA_HEADS, hd).transpose(1, 0, 3, 2, 4)

    qb = blocks(q, GLA_DK) * (GLA_DK ** -0.5)
    kb = blocks(k, GLA_DK)
    vb = blocks(v, GLA_DV)
    gb = jnp.cumsum(blocks(glog, GLA_DK), axis=-2)
    causal = jnp.tril(jnp.ones((GLA_BLOCK, GLA_BLOCK), dtype=bool))

    def step(S, inp):
        qc, kc, vc, gc = inp
        qg = qc * jnp.exp(gc)
        a = jnp.einsum('bhid,bhjd->bhij', qg, kc * jnp.exp(-gc))
        a = jnp.where(causal, a, 0.0)
        o = jnp.einsum('bhij,bhjv->bhiv', a, vc) + jnp.einsum('bhid,bhdv->bhiv', qg, S)
        glast = gc[:, :, -1, :]
        kd = kc * jnp.exp(glast[:, :, None, :] - gc)
        S = S * jnp.exp(glast)[..., None] + jnp.einsum('bhjd,bhjv->bhdv', kd, vc)
        return S, o

    S, o = lax.scan(step, s0.astype(jnp.float32), (qb, kb, vb, gb))
    o = o.transpose(1, 0, 3, 2, 4).reshape(B, nb * GLA_BLOCK, GLA_HEADS, GLA_DV)[:, pad:]
    o = rmsnorm(o, head_norm) * jax.nn.silu(r.astype(jnp.float32).reshape(B, L, GLA_HEADS, GLA_DV))
    y = o.reshape(B, L, GLA_VALUE_DIM).astype(xn.dtype) @ w_out
    return y, S.astype(xn.dtype)


def chunk_causal_attention(q_lat, q_rope, k_lat, k_rope, q_ids, k_ids):
    B, Lq, H, R = q_lat.shape
    qblk = min(Q_BLOCK, Lq)
    pad = (-Lq) % qblk
    nb = (Lq + pad) // qblk
    q_ids = jnp.pad(q_ids, (0, pad), constant_values=np.iinfo(np.int32).max)
    scale = (MLA_NOPE + MLA_ROPE) ** -0.5

    def to_blocks(t):
        t = jnp.pad(t, ((0, 0), (0, pad), (0, 0), (0, 0)))
        return t.reshape(B, nb, qblk, H, t.shape[-1]).transpose(1, 0, 2, 3, 4)

    def attend(blk):
        ql, qr, qi = blk
        s = (jnp.einsum('bqhr,bkr->bhqk', ql, k_lat).astype(jnp.float32)
             + jnp.einsum('bqhe,bke->bhqk', qr, k_rope).astype(jnp.float32)) * scale
        mask = (k_ids[None, :] <= qi[:, None])[None, None]
        p = jax.nn.softmax(jnp.where(mask, s, -jnp.inf), axis=-1).astype(k_lat.dtype)
        return jnp.einsum('bhqk,bkr->bqhr', p, k_lat)

    o = lax.map(attend, (to_blocks(q_lat), to_blocks(q_rope), q_ids.reshape(nb, qblk)))
    return o.transpose(1, 0, 2, 3, 4).reshape(B, nb * qblk, H, R)[:, :Lq]


def mla_mixer(xn, pos, q_ids, k_ids, past_lat, past_rope, w_down, q_norm, w_uq, kv_norm, w_uk, w_uv, w_out):
    B, L, _ = xn.shape
    cq, ckv, kr = jnp.split(xn @ w_down, [MLA_Q_RANK, MLA_Q_RANK + MLA_KV_RANK], axis=-1)
    q = (rmsnorm(cq, q_norm) @ w_uq).reshape(B, L, MLA_HEADS, MLA_NOPE + MLA_ROPE)
    q_nope = q[..., :MLA_NOPE]
    q_rope = apply_rope(q[..., MLA_NOPE:], pos)
    ckv = rmsnorm(ckv, kv_norm)
    kr = apply_rope(kr[:, :, None, :], pos)[:, :, 0]
    q_lat = jnp.einsum('blhn,rhn->blhr', q_nope, w_uk)
    if past_lat is None:
        keys_lat, keys_rope = ckv, kr
    else:
        keys_lat = jnp.concatenate([past_lat.astype(ckv.dtype), ckv], axis=1)
        keys_rope = jnp.concatenate([past_rope.astype(kr.dtype), kr], axis=1)
    o_lat = chunk_causal_attention(q_lat, q_rope, keys_lat, keys_rope, q_ids, k_ids)
    o = jnp.einsum('blhr,rhv->blhv', o_lat, w_uv).reshape(B, L, MLA_HEADS * MLA_V)
    return o @ w_out, ckv, kr


def setup_inputs(seed: int = 0) -> dict:
    key = jax.random.key(seed)
    ks = jax.random.split(key, 32)
    f32 = jnp.float32

    def nrm(k, shape, scale):
        return jax.random.normal(k, shape, f32) * scale

    def gain(k, shape):
        return 1.0 + 0.02 * jax.random.normal(k, shape, f32)

    return {
        "x_prompt": nrm(ks[0], (BATCH, SEQ, D_MODEL), 1.0),
        "x_sample": nrm(ks[1], (DEC_BATCH, DEC_SEQ, D_MODEL), 1.0),
        "state_gla": nrm(ks[2], (N_GLA_LAYERS, DEC_BATCH, GLA_HEADS, GLA_DK, GLA_DV), 1.0),
        "cache_mla_latent": nrm(ks[3], (N_MLA_LAYERS, DEC_BATCH, PAST_LEN, MLA_KV_RANK), 1.0),
        "cache_mla_rope": nrm(ks[4], (N_MLA_LAYERS, DEC_BATCH, PAST_LEN, MLA_ROPE), 1.0),
        "meta_tokens": nrm(ks[5], (N_META, D_MODEL), 1.0),
        "ffn1_norm": gain(ks[6], (DEPTH, D_MODEL)),
        "ffn1_w_gate": nrm(ks[7], (DEPTH, D_MODEL, D_FF), D_MODEL ** -0.5),
        "ffn1_w_up": nrm(ks[8], (DEPTH, D_MODEL, D_FF), D_MODEL ** -0.5),
        "ffn1_w_down": nrm(ks[9], (DEPTH, D_FF, D_MODEL), D_FF ** -0.5),
        "mix_norm": gain(ks[10], (DEPTH, D_MODEL)),
        "gla_w_in": nrm(ks[11], (N_GLA_LAYERS, D_MODEL, GLA_IN_DIM), D_MODEL ** -0.5),
        "gla_w_gate_up": nrm(ks[12], (N_GLA_LAYERS, GLA_GATE_RANK, GLA_KEY_DIM), GLA_GATE_RANK ** -0.5),
        "gla_b_gate": nrm(ks[13], (N_GLA_LAYERS, GLA_KEY_DIM), 0.01),
        "gla_head_norm": gain(ks[14], (N_GLA_LAYERS, GLA_DV)),
        "gla_w_out": nrm(ks[15], (N_GLA_LAYERS, GLA_VALUE_DIM, D_MODEL), GLA_VALUE_DIM ** -0.5),
        "mla_w_down": nrm(ks[16], (N_MLA_LAYERS, D_MODEL, MLA_DOWN_DIM), D_MODEL ** -0.5),
        "mla_q_norm": gain(ks[17], (N_MLA_LAYERS, MLA_Q_RANK)),
        "mla_w_uq": nrm(ks[18], (N_MLA_LAYERS, MLA_Q_RANK, MLA_HEADS * (MLA_NOPE + MLA_ROPE)), MLA_Q_RANK ** -0.5),
        "mla_kv_norm": gain(ks[19], (N_MLA_LAYERS, MLA_KV_RANK)),
        "mla_w_uk": nrm(ks[20], (N_MLA_LAYERS, MLA_KV_RANK, MLA_HEADS, MLA_NOPE), MLA_KV_RANK ** -0.5),
        "mla_w_uv": nrm(ks[21], (N_MLA_LAYERS, MLA_KV_RANK, MLA_HEADS, MLA_V), MLA_KV_RANK ** -0.5),
        "mla_w_out": nrm(ks[22], (N_MLA_LAYERS, MLA_HEADS * MLA_V, D_MODEL), (MLA_HEADS * MLA_V) ** -0.5),
        "ffn2_norm": gain(ks[23], (DEPTH, D_MODEL)),
        "ffn2_w_gate": nrm(ks[24], (DEPTH, D_MODEL, D_FF), D_MODEL ** -0.5),
        "ffn2_w_up": nrm(ks[25], (DEPTH, D_MODEL, D_FF), D_MODEL ** -0.5),
        "ffn2_w_down": nrm(ks[26], (DEPTH, D_FF, D_MODEL), D_FF ** -0.5),
        "final_norm": gain(ks[27], (D_MODEL,)),
    }


def reference(x_prompt, x_sample, state_gla, cache_mla_latent, cache_mla_rope, meta_tokens,
              ffn1_norm, ffn1_w_gate, ffn1_w_up, ffn1_w_down, mix_norm,
              gla_w_in, gla_w_gate_up, gla_b_gate, gla_head_norm, gla_w_out,
              mla_w_down, mla_q_norm, mla_w_uq, mla_kv_norm, mla_w_uk, mla_w_uv, mla_w_out,
              ffn2_norm, ffn2_w_gate, ffn2_w_up, ffn2_w_down, final_norm):
    Bp, n_frames, _ = x_prompt.shape
    Bs, Ls, _ = x_sample.shape
    past = cache_mla_latent.shape[2]
    hp = jnp.concatenate([jnp.broadcast_to(meta_tokens[None].astype(x_prompt.dtype), (Bp, N_META, D_MODEL)),
                          x_prompt], axis=1)
    Lp = hp.shape[1]
    pos_p = jnp.arange(Lp, dtype=jnp.float32)
    ids_p = jnp.concatenate([jnp.full((N_META,), -1, jnp.int32),
                             jnp.arange(n_frames, dtype=jnp.int32) // CHUNK])
    hs = x_sample
    frame_s = past + jnp.arange(Ls, dtype=jnp.int32)
    pos_s = frame_s.astype(jnp.float32)
    ids_s = frame_s // CHUNK
    kids_s = jnp.concatenate([jnp.arange(past, dtype=jnp.int32) // CHUNK, ids_s])

    gla_sp, gla_ss, lat_p, rope_p, lat_s, rope_s = [], [], [], [], [], []
    for i in range(DEPTH):
        hp = swiglu_half(hp, ffn1_norm[i], ffn1_w_gate[i], ffn1_w_up[i], ffn1_w_down[i])
        hs = swiglu_half(hs, ffn1_norm[i], ffn1_w_gate[i], ffn1_w_up[i], ffn1_w_down[i])
        xp = rmsnorm(hp, mix_norm[i])
        xs = rmsnorm(hs, mix_norm[i])
        j = i // N_MIXERS
        if i % N_MIXERS == 0:
            s0 = jnp.zeros((Bp, GLA_HEADS, GLA_DK, GLA_DV), xp.dtype)
            dp, sp = gla_mixer(xp, s0, gla_w_in[j], gla_w_gate_up[j], gla_b_gate[j], gla_head_norm[j], gla_w_out[j])
            ds, ss = gla_mixer(xs, state_gla[j], gla_w_in[j], gla_w_gate_up[j], gla_b_gate[j], gla_head_norm[j], gla_w_out[j])
            gla_sp.append(sp)
            gla_ss.append(ss)
        else:
            dp, cp, rp = mla_mixer(xp, pos_p, ids_p, ids_p, None, None, mla_w_down[j], mla_q_norm[j], mla_w_uq[j],
                                   mla_kv_norm[j], mla_w_uk[j], mla_w_uv[j], mla_w_out[j])
            ds, cs, rs = mla_mixer(xs, pos_s, ids_s, kids_s, cache_mla_latent[j], cache_mla_rope[j], mla_w_down[j],
                                   mla_q_norm[j], mla_w_uq[j], mla_kv_norm[j], mla_w_uk[j], mla_w_uv[j], mla_w_out[j])
            lat_p.append(cp)
            rope_p.append(rp)
            lat_s.append(cs)
            rope_s.append(rs)
        hp = hp + dp
        hs = hs + ds
        hp = swiglu_half(hp, ffn2_norm[i], ffn2_w_gate[i], ffn2_w_up[i], ffn2_w_down[i])
        hs = swiglu_half(hs, ffn2_norm[i], ffn2_w_gate[i], ffn2_w_up[i], ffn2_w_down[i])

    y_prompt = rmsnorm(hp[:, N_META:], final_norm)
    y_sample = rmsnorm(hs, final_norm)
    return (y_prompt, y_sample, jnp.stack(gla_sp, 0), jnp.stack(gla_ss, 0),
            jnp.stack(lat_p, 0), jnp.stack(rope_p, 0), jnp.stack(lat_s, 0), jnp.stack(rope_s, 0))
```

```python
import numpy as np
from contextlib import ExitStack
import concourse.bass as bass
import concourse.mybir as mybir
from concourse.bass_utils import run_bass_kernel_spmd

F32 = mybir.dt.float32
BF16 = mybir.dt.bfloat16
AF = mybir.ActivationFunctionType
ALU = mybir.AluOpType

ENGS = ("tensor", "vector", "scalar", "gpsimd", "sync")

D = 1024
DFF = 2816
NF = 2048
NMETA = 16
NSAMP = 128
T = NF + NMETA + NSAMP
C_META = NF
C_SAMP = NF + NMETA
TILES = [(0, 512), (512, 512), (1024, 512), (1536, 512), (2048, 144)]
EPS = 1e-6
PAST = 2048
NEG = -30000.0
NCONST = 2560


class Buf:
    __slots__ = ("name", "lw", "lw_eng", "rd")

    def __init__(self, name):
        self.name = name
        self.lw = None
        self.lw_eng = None
        self.rd = {}


class Prog:
    def __init__(self, nc, stack, n_dma_sems=24, epoch=1000):
        self.nc = nc
        self.stack = stack
        self.ops = {e: [] for e in ENGS}
        self.sig = {e: 0 for e in ENGS}
        self.ep = {e: 0 for e in ENGS}
        self.sems = {}
        self.waited = {e: {} for e in ENGS}
        self.epoch = epoch
        self.n_dma = n_dma_sems
        self.dma_cnt = [0] * n_dma_sems
        self.dma_last = [None] * n_dma_sems
        self.dma_rr = 0
        self.nops = 0
        self.old = []
        self.async_idx = set()

    def _sem(self, name):
        if name not in self.sems:
            self.sems[name] = self.stack.enter_context(self.nc.semaphore(name))
        return name

    def _cursem(self, eng):
        if self.sig[eng] >= self.epoch:
            self.old.append((f"s_{eng}_{self.ep[eng]}", self.sig[eng]))
            self.ep[eng] += 1
            self.sig[eng] = 0
        return self._sem(f"s_{eng}_{self.ep[eng]}")

    def _wait(self, eng, tok):
        sem, val = tok
        if self.waited[eng].get(sem, 0) >= val:
            return
        self.waited[eng][sem] = val
        self.ops[eng].append(("wait", sem, val))

    def _deps(self, eng, me, reads, writes):
        for b in reads:
            if b.lw is not None:
                self._wait(eng, b.lw)
        for b in writes:
            if b.lw is not None and b.lw_eng != me:
                self._wait(eng, b.lw)
            for teng, tok in b.rd.items():
                if teng != me:
                    self._wait(eng, tok)

    def _post(self, me, tok, reads, writes):
        for b in reads:
            b.rd[me] = tok
        for b in writes:
            b.lw = tok
            b.lw_eng = me
            b.rd = {}

    def op(self, eng, fn, reads=(), writes=(), signal=True):
        mysem = self._cursem(eng)
        self._deps(eng, eng, reads, writes)
        tok = (mysem, self.sig[eng] + 1)
        if signal:
            self.sig[eng] += 1
        self.ops[eng].append(("op", fn, mysem if signal else None, 1))
        self._post(eng, tok, reads, writes)
        self.nops += 1
        return tok

    def dma(self, q, fn, reads=(), writes=(), inc=16, asyn=False):
        i = self.dma_rr
        self.dma_rr = (i + 1) % self.n_dma
        sem = self._sem(f"s_dma_{i}")
        me = f"dma{i}"
        if self.dma_last[i] is not None:
            self._wait(q, self.dma_last[i])
        self._deps(q, me, reads, writes)
        self.dma_cnt[i] += inc
        tok = (sem, self.dma_cnt[i])
        self.dma_last[i] = tok
        if asyn:
            self.async_idx.add(i)
        else:
            self.async_idx.discard(i)
        self.ops[q].append(("op", fn, sem, inc))
        self._post(me, tok, reads, writes)
        self.nops += 1
        return tok

    def barrier(self, skip_async=False):
        toks = list(self.old)
        for e in ENGS:
            if self.sig[e] > 0:
                toks.append((f"s_{e}_{self.ep[e]}", self.sig[e]))
        for i in range(self.n_dma):
            if self.dma_last[i] is not None and not (skip_async and i in self.async_idx):
                toks.append(self.dma_last[i])
        for e in ENGS:
            for t in toks:
                self._wait(e, t)

    def emit(self):
        nc = self.nc
        with nc.Block() as block:
            for e in ENGS:
                items = self.ops[e]

                def body(engh, items=items):
                    for it in items:
                        if it[0] == "wait":
                            engh.wait_ge(self.sems[it[1]], it[2])
                        else:
                            ins = it[1](engh)
                            if it[2] is not None:
                                ins.then_inc(self.sems[it[2]], it[3])

                getattr(block, e)(body)
        self.ops = {e: [] for e in ENGS}


class K:
    pass


_UID = [0]


def _sb(k, st, name, shape, dt):
    _UID[0] += 1
    name = f"{name}_{_UID[0]}"
    t = st.enter_context(k.nc.sbuf_tensor(name, shape, dt))
    return t, Buf(name)


def mm(k, out, lhsT, rhs, start, stop, reads, writes, signal=True):
    k.P.op("tensor", lambda e: e.matmul(out, lhsT=lhsT, rhs=rhs, start=start, stop=stop),
           reads=reads, writes=writes, signal=signal)


def act(k, out, in_, func, reads, writes, scale=None, bias=None, eng="scalar"):
    kw = {}
    if scale is not None:
        kw["scale"] = scale
    if bias is not None:
        kw["bias"] = bias
    k.P.op("scalar", lambda e: e.activation(out=out, in_=in_, func=func, **kw), reads=reads, writes=writes)


def tt(k, out, in0, in1, op, reads, writes, eng="vector"):
    k.P.op(eng, lambda e: e.tensor_tensor(out=out, in0=in0, in1=in1, op=op), reads=reads, writes=writes)


def ts(k, out, in0, s1, s2, op0, op1, reads, writes, eng="vector"):
    if op1 is None:
        k.P.op(eng, lambda e: e.tensor_scalar(out=out, in0=in0, scalar1=s1, scalar2=None, op0=op0),
               reads=reads, writes=writes)
    else:
        k.P.op(eng, lambda e: e.tensor_scalar(out=out, in0=in0, scalar1=s1, scalar2=s2, op0=op0, op1=op1),
               reads=reads, writes=writes)


def stt(k, out, in0, scalar, in1, op0, op1, reads, writes):
    k.P.op("vector", lambda e: e.scalar_tensor_tensor(out=out, in0=in0, scalar=scalar, in1=in1, op0=op0, op1=op1),
           reads=reads, writes=writes)


def cp(k, out, in_, reads, writes, eng="vector"):
    k.P.op(eng, lambda e: e.tensor_copy(out=out, in_=in_), reads=reads, writes=writes)


def dma(k, out, in_, reads, writes, q="sync", asyn=False):
    k.P.dma(q, lambda e: e.dma_start(out=out, in_=in_), reads=reads, writes=writes, asyn=asyn)


def next_ps(k):
    i = k.ps_rr
    k.ps_rr = (i + 1) % k.ps_n
    return k.ps[i], k.psb[i]


def rms_rstd(k, src, Bsrc, nchunk, nfeat, rstd, Brstd, sq, Bsq, col0=0, tiles=TILES):
    for (t0, tn) in tiles:
        ps, Bps = next_ps(k)
        for c in range(nchunk):
            s, Bs = sq[c % 2], Bsq[c % 2]
            sb16 = s[:, :].bitcast(BF16)
            act(k, sb16[:, :tn], src[:, c, col0 + t0:col0 + t0 + tn], AF.Square, [Bsrc], [Bs])
            mm(k, ps[:, :tn], k.ones_b[:, :], sb16[:, :tn], c == 0, c == nchunk - 1, [Bs, k.Bconst], [Bps],
               signal=True)
        ts(k, rstd[:, t0:t0 + tn], ps[:, :tn], 1.0 / nfeat, EPS, ALU.mult, ALU.add, [Bps], [Brstd])
        act(k, rstd[:, t0:t0 + tn], rstd[:, t0:t0 + tn], AF.Ln, [Brstd], [Brstd])
        act(k, rstd[:, t0:t0 + tn], rstd[:, t0:t0 + tn], AF.Exp, [Brstd], [Brstd], scale=-0.5)


def make_xn(k, nw, rstd, Brstd, sq, Bsq):
    for ti, (t0, tn) in enumerate(TILES):
        rms_rstd(k, k.h, k.Bh, 8, D, rstd, Brstd, sq, Bsq, tiles=[(t0, tn)])
        for c in range(8):
            stt(k, k.xn[:, c, t0:t0 + tn], k.h[:, c, t0:t0 + tn], nw[:, c:c + 1], rstd[:, t0:t0 + tn],
                ALU.mult, ALU.mult, [k.Bh, Brstd, k.Bconst], [k.Bxn, k.Bxnt[ti]])


def ffn_phase(k, li, which, final=False):
    P = k.P
    wg_d = k.din[f"ffn{which}_w_gate"][li].rearrange("(kc p) f -> p kc f", p=128)
    wu_d = k.din[f"ffn{which}_w_up"][li].rearrange("(kc p) f -> p kc f", p=128)
    wd_d = k.din[f"ffn{which}_w_down"][li].rearrange("(fc p) d -> p fc d", p=128)
    nw = k.normw[f"ffn{which}_{li}"]
    with ExitStack() as st:
        wg = [_sb(k, st, f"wg{i}", [128, 8, 512], BF16) for i in range(2)]
        wu = [_sb(k, st, f"wu{i}", [128, 8, 512], BF16) for i in range(2)]
        wd = [_sb(k, st, f"wd{i}", [128, 4, D], BF16) for i in range(2)]
        a, Ba = _sb(k, st, "ffn_a", [128, 4, T], BF16)
        rstd, Brstd = _sb(k, st, "ffn_rstd", [128, T], F32)
        sq0, Bsq0 = _sb(k, st, "ffn_sq0", [128, 512], F32)
        sq1, Bsq1 = _sb(k, st, "ffn_sq1", [128, 512], F32)
        sg = [_sb(k, st, f"ffn_sg{i}", [128, 512], F32) for i in range(2)]
        groups = [(0, 4), (4, 4), (8, 4), (12, 4), (16, 4), (20, 2)]

        def load(gi):
            f0, nf = groups[gi]
            b = gi % 2
            dma(k, wg[b][0][:, :, 0:nf * 128], wg_d[:, :, f0 * 128:(f0 + nf) * 128], [k.Bw], [wg[b][1]], q="gpsimd")
            dma(k, wu[b][0][:, :, 0:nf * 128], wu_d[:, :, f0 * 128:(f0 + nf) * 128], [k.Bw], [wu[b][1]], q="gpsimd")
            dma(k, wd[b][0][:, 0:nf, :], wd_d[:, f0:f0 + nf, :], [k.Bw], [wd[b][1]], q="gpsimd")

        load(0)
        make_xn(k, nw, rstd, Brstd, [sq0, sq1], [Bsq0, Bsq1])
        for gi, (f0, nf) in enumerate(groups):
            if gi + 1 < len(groups):
                load(gi + 1)
            b = gi % 2
            for fc in range(nf):
                for ti, (t0, tn) in enumerate(TILES):
                    pg, Bpg = next_ps(k)
                    pu, Bpu = next_ps(k)
                    for kc in range(8):
                        mm(k, pg[:, :tn], wg[b][0][:, kc, fc * 128:(fc + 1) * 128], k.xn[:, kc, t0:t0 + tn],
                           kc == 0, kc == 7, [wg[b][1], k.Bxnt[ti]], [Bpg], signal=(kc == 7))
                    for kc in range(8):
                        mm(k, pu[:, :tn], wu[b][0][:, kc, fc * 128:(fc + 1) * 128], k.xn[:, kc, t0:t0 + tn],
                           kc == 0, kc == 7, [wu[b][1], k.Bxnt[ti]], [Bpu], signal=(kc == 7))
                    s, Bs = sg[ti % 2]
                    act(k, s[:, :tn], pg[:, :tn], AF.Silu, [Bpg], [Bs])
                    tt(k, a[:, fc, t0:t0 + tn], s[:, :tn], pu[:, :tn], ALU.mult, [Bs, Bpu], [Ba])
            if final and gi == len(groups) - 1:
                fnw = k.normw["final"]
                yi = [0]

                def down_tile(t0, tn):
                    for d in range(8):
                        pd, Bpd = next_ps(k)
                        for fc in range(nf):
                            mm(k, pd[:, :tn], wd[b][0][:, fc, d * 128:(d + 1) * 128], a[:, fc, t0:t0 + tn],
                               fc == 0, fc == nf - 1, [wd[b][1], Ba], [Bpd], signal=(fc == nf - 1))
                        stt(k, k.h[:, d, t0:t0 + tn], pd[:, :tn], 0.5, k.h[:, d, t0:t0 + tn], ALU.mult, ALU.add,
                            [Bpd, k.Bh], [k.Bh])

                def norm_out(t0, tn):
                    rms_rstd(k, k.h, k.Bh, 8, D, rstd, Brstd, [sq0, sq1], [Bsq0, Bsq1], tiles=[(t0, tn)])
                    for c in range(8):
                        y, By = sg[yi[0] % 2]
                        yi[0] += 1
                        stt(k, y[:, :tn], k.h[:, c, t0:t0 + tn], fnw[:, c:c + 1], rstd[:, t0:t0 + tn], ALU.mult,
                            ALU.mult, [k.Bh, Brstd, k.Bconst], [By])
                        dma(k, k.dout["o_y"][:, c, t0:t0 + tn], y[:, :tn], [By], [k.Bw],
                            q="sync" if yi[0] % 2 else "gpsimd")

                down_tile(*TILES[0])
                for ti2, (t0, tn) in enumerate(TILES):
                    if ti2 + 1 < len(TILES):
                        down_tile(*TILES[ti2 + 1])
                    norm_out(t0, tn)
                continue
            for d in range(8):
                for (t0, tn) in TILES:
                    pd, Bpd = next_ps(k)
                    for fc in range(nf):
                        mm(k, pd[:, :tn], wd[b][0][:, fc, d * 128:(d + 1) * 128], a[:, fc, t0:t0 + tn],
                           fc == 0, fc == nf - 1, [wd[b][1], Ba], [Bpd], signal=(fc == nf - 1))
                    stt(k, k.h[:, d, t0:t0 + tn], pd[:, :tn], 0.5, k.h[:, d, t0:t0 + tn], ALU.mult, ALU.add,
                        [Bpd, k.Bh], [k.Bh])
        P.barrier()
        P.emit()


class NS:
    pass


def ps_view4(ps):
    return ps[:, :].rearrange("p (h n) -> p h n", h=4)


def gla_prep(k, W, Z, c0, n):
    xs = lambda kc: k.xn[:, kc, c0:c0 + n]
    ps, Bps = next_ps(k)
    for kc in range(8):
        mm(k, ps[0:16, :n], W.wgd[:, kc, :], xs(kc), kc == 0, kc == 7, [W.Bwgd, k.Bxn], [Bps], signal=(kc == 7))
    cp(k, Z.gd[0:16, :n], ps[0:16, :n], [Bps], [Z.Bgd])
    for half in range(2):
        psv, Bpsv = next_ps(k)
        for kc in range(8):
            mm(k, psv[:n, :], xs(kc), W.wv[:, kc, half * 512:(half + 1) * 512], kc == 0, kc == 7,
               [W.Bwv, k.Bxn], [Bpsv], signal=(kc == 7))
        act(k, Z.vt[:n, half * 512:(half + 1) * 512], psv[:n, :], AF.Copy, [Bpsv], [Z.Bvt])
    ps2, Bps2 = next_ps(k)
    mm(k, ps2[:n, :], Z.gd[0:17, :n], W.wgu[0:17, :], True, True, [Z.Bgd, W.Bwgu], [Bps2])
    act(k, Z.graw[:n, :], ps2[:n, :], AF.Exp, [Bps2], [Z.Bgraw], scale=-1.0)
    act(k, Z.graw[:n, :], Z.graw[:n, :], AF.Ln, [Z.Bgraw], [Z.Bgraw], bias=1.0)
    psq, Bpsq = next_ps(k)
    psq_v = ps_view4(psq)
    for h in range(4):
        for kc in range(8):
            mm(k, psq_v[:, h, :n], W.wq[:, kc, h * 128:(h + 1) * 128], xs(kc), kc == 0, kc == 7,
               [W.Bwq, k.Bxn], [Bpsq], signal=(kc == 7 and h == 3))
    psk, Bpsk = next_ps(k)
    psk_v = ps_view4(psk)
    for h in range(4):
        for kc in range(8):
            mm(k, psk_v[:, h, :n], W.wk[:, kc, h * 128:(h + 1) * 128], xs(kc), kc == 0, kc == 7,
               [W.Bwk, k.Bxn], [Bpsk], signal=(kc == 7 and h == 3))
    psg, Bpsg = next_ps(k)
    psg_v = ps_view4(psg)
    for h in range(4):
        mm(k, psg_v[:, h, :n], Z.graw[:n, h * 128:(h + 1) * 128], k.TRI[64][:n, :n], True, True,
           [Z.Bgraw, k.Bconst], [Bpsg], signal=(h == 3))
    psw, Bpsw = next_ps(k)
    mm(k, psw[:n, :], k.REV[64][:n, :n], Z.graw[:n, :], True, True, [Z.Bgraw, k.Bconst], [Bpsw])
    act(k, Z.egc[:, :, :n], psg_v[:, :, :n], AF.Exp, [Bpsg], [Z.Begc])
    act(k, Z.engc[:, :, :n], psg_v[:, :, :n], AF.Exp, [Bpsg], [Z.Bengc], scale=-1.0)
    act(k, Z.eW[:n, :], psw[:n, :], AF.Exp, [Bpsw], [Z.BeW])
    pskt, Bpskt = next_ps(k)
    for kc in range(8):
        mm(k, pskt[:n, :], xs(kc), W.wk[:, kc, :], kc == 0, kc == 7, [W.Bwk, k.Bxn], [Bpskt], signal=(kc == 7))
    stt(k, Z.qg[:, :, :n], psq_v[:, :, :n], 128.0 ** -0.5, Z.egc[:, :, :n], ALU.mult, ALU.mult,
        [Bpsq, Z.Begc], [Z.Bqg])
    tt(k, Z.kg[:, :, :n], psk_v[:, :, :n], Z.engc[:, :, :n], ALU.mult, [Bpsk, Z.Bengc], [Z.Bkg])
    tt(k, Z.kd[:n, :], pskt[:n, :], Z.eW[:n, :], ALU.mult, [Bpskt, Z.BeW], [Z.Bkd])
    psa, Bpsa = next_ps(k)
    psa_v = ps_view4(psa)
    for h in range(4):
        mm(k, psa_v[:n, h, :n], Z.kg[:, h, :n], Z.qg[:, h, :n], True, True, [Z.Bkg, Z.Bqg], [Bpsa],
           signal=(h == 3))
    tt(k, Z.atm[:n, :, :n], psa_v[:n, :, :n], k.MASK4[64][:n, :, :n], ALU.mult, [Bpsa, k.Bconst], [Z.Batm])


def gla_blocks(k, W, Z, c0, n, nb, L, o_dst, Bo, ecum_b0=None):
    xs = None
    pso = [ps_view4(k.pso[0]), ps_view4(k.pso[1])]
    for bl in range(nb):
        b0 = bl * L
        for h in range(4):
            for vc in range(2):
                c = h * 2 + vc
                out = pso[c // 4][:, c % 4, b0:b0 + L]
                mm(k, out, W.Sbf[:, h, vc * 128:(vc + 1) * 128], Z.qg[:, h, b0:b0 + L], True, False,
                   [W.BSbf[h], Z.Bqg], [k.Bpso[c // 4]], signal=False)
                mm(k, out, Z.vt[:n, h * 256 + vc * 128:h * 256 + (vc + 1) * 128], Z.atm[:n, h, b0:b0 + L],
                   False, True, [Z.Bvt, Z.Batm], [k.Bpso[c // 4]], signal=True)
            pss, Bpss = next_ps(k)
            mm(k, pss[:, 0:256], Z.kd[b0:b0 + L, h * 128:(h + 1) * 128], Z.vt[b0:b0 + L, h * 256:(h + 1) * 256],
               True, True, [Z.Bkd, Z.Bvt], [Bpss])
            stt(k, W.S[:, h, :], W.S[:, h, :], Z.egc[:, h, b0 + L - 1:b0 + L], pss[:, 0:256], ALU.mult, ALU.add,
                [W.BS[h], Z.Begc, Bpss], [W.BS[h]])
            act(k, W.Sbf[:, h, :], W.S[:, h, :], AF.Copy, [W.BS[h]], [W.BSbf[h]])
            if ecum_b0 is not None:
                gb = ecum_b0 + bl
                ts(k, Z.qt[:, h, b0:b0 + L], Z.qg[:, h, b0:b0 + L], W.ecum[:, gb, h:h + 1], None,
                   ALU.mult, None, [Z.Bqg, W.Becum], [Z.Bqt])
        if ecum_b0 is not None:
            gb = ecum_b0 + bl
            tt(k, W.ecum[:, gb + 1, :], W.ecum[:, gb, :], Z.egc[:, :, b0 + L - 1], ALU.mult,
               [W.Becum, Z.Begc], [W.Becum])
    for half in range(2):
        cp(k, o_dst[:, half * 4:(half + 1) * 4, :n], pso[half][:, :, :n], [k.Bpso[half]], [Bo])
    if ecum_b0 is not None:
        dma(k, k.qtl_scr[:, :, c0:c0 + n], Z.qt[:, :, :n], [Z.Bqt], [k.Bqtl], q="sync")


def gla_subtile(k, W, c0, n, nb, L, o_dst, Bo, ecum_b0=None):
    W.par ^= 1
    Z = W.sets[W.par]
    gla_prep(k, W, Z, c0, n)
    gla_blocks(k, W, Z, c0, n, nb, L, o_dst, Bo, ecum_b0)


def gla_post_r(k, W, c0, n, sr8, Bsr8):
    for c in range(8):
        psr, Bpsr = next_ps(k)
        for kc in range(8):
            mm(k, psr[:, :n], W.wr[:, kc, c * 128:(c + 1) * 128], k.xn[:, kc, c0:c0 + n], kc == 0, kc == 7,
               [W.Bwr, k.Bxn], [Bpsr], signal=(kc == 7))
        act(k, sr8[:, c, :n], psr[:, :n], AF.Silu, [Bpsr], [Bsr8])


def gla_post_norm(k, W, o, Bo, n, sr8, Bsr8):
    for c in range(8):
        act(k, W.sq8[:, c, :n], o[:, c, :n], AF.Square, [Bo], [W.Bsq8])
    for j in range(2):
        psn, Bpsn = next_ps(k)
        psn_v = psn[:, :].rearrange("p (h n) -> p h n", h=2)
        for hh in range(2):
            h = 2 * j + hh
            for vc in range(2):
                mm(k, psn_v[:, hh, :n], k.ones_b[:, :], W.sq8[:, h * 2 + vc, :n], vc == 0, vc == 1,
                   [W.Bsq8, k.Bconst], [Bpsn], signal=(vc == 1 and hh == 1))
        ts(k, W.rstd[:, 2 * j:2 * j + 2, :n], psn_v[:, :, :n], 1.0 / 256.0, EPS, ALU.mult, ALU.add, [Bpsn], [W.Brstd1])
    act(k, W.rstd[:, :, :n], W.rstd[:, :, :n], AF.Ln, [W.Brstd1], [W.Brstd1])
    act(k, W.rstd[:, :, :n], W.rstd[:, :, :n], AF.Exp, [W.Brstd1], [W.Brstd1], scale=-0.5)
    for c in range(8):
        h, vc = c // 2, c % 2
        stt(k, W.tmp[:, :n], o[:, c, :n], k.hnorm[:, vc:vc + 1], W.rstd[:, h, :n], ALU.mult, ALU.mult,
            [Bo, W.Brstd1, k.Bconst], [W.Btmp])
        tt(k, W.y[:, c, :n], W.tmp[:, :n], sr8[:, c, :n], ALU.mult, [W.Btmp, Bsr8], [W.By])


def gla_post_out(k, W, c0, n):
    for d in range(8):
        psd, Bpsd = next_ps(k)
        for c in range(8):
            mm(k, psd[:, :n], W.wout[:, c, d * 128:(d + 1) * 128], W.y[:, c, :n], c == 0, c == 7,
               [W.Bwout, W.By], [Bpsd], signal=(c == 7))
        tt(k, k.h[:, d, c0:c0 + n], k.h[:, d, c0:c0 + n], psd[:, :n], ALU.add, [k.Bh, Bpsd], [k.Bh])


def gla_phase(k):
    P = k.P
    nc = k.nc
    win = k.din["gla_w_in"][0].rearrange("(kc p) f -> p kc f", p=128)
    wout_d = k.din["gla_w_out"][0].rearrange("(kc p) d -> p kc d", p=128)
    o_scr = nc.dram_tensor("gla_o_scr", [128, 8, NF], F32, kind="Internal").ap()
    cc_in = nc.dram_tensor("gla_cc_in", [128, 1024], F32, kind="Internal").ap()
    cc_out = nc.dram_tensor("gla_cc_out", [256, 1024], F32, kind="Internal").ap()
    Bscr, Bccin, Bccout = Buf("o_scr"), Buf("ccin"), Buf("ccout")
    k.ps_n = 6
    k.ps_rr = 0
    k.pso = [k.ps[6], k.ps[7]]
    k.Bpso = [k.psb[6], k.psb[7]]
    with ExitStack() as st:
        k.qtl_scr = nc.dram_tensor("gla_qtl_scr", [128, 4, NF], BF16, kind="Internal").ap()
        k.Bqtl = Buf("qtl_scr")
        osm, Bosm = _sb(k, st, "g_osm", [128, 8, 144], F32)
        W = NS()
        W.S, _ = _sb(k, st, "g_S", [128, 4, 256], F32)
        W.Sbf, _ = _sb(k, st, "g_Sbf", [128, 4, 256], BF16)
        W.BS = [Buf(f"S{h}") for h in range(4)]
        W.BSbf = [Buf(f"Sbf{h}") for h in range(4)]
        W.ecum, W.Becum = _sb(k, st, "g_ecum", [128, 33, 4], F32)
        Sloc, BSloc = _sb(k, st, "g_Sloc", [128, 4, 256], F32)
        with ExitStack() as s0:
            rstd, Brstd = _sb(k, s0, "g_rstd", [128, T], F32)
            sq0, Bsq0 = _sb(k, s0, "g_sq0", [128, 512], F32)
            sq1, Bsq1 = _sb(k, s0, "g_sq1", [128, 512], F32)
            make_xn(k, k.normw["mix_0"], rstd, Brstd, [sq0, sq1], [Bsq0, Bsq1])
            P.barrier()
            P.emit()
        with ExitStack() as p1:
            W.wq, W.Bwq = _sb(k, p1, "g_wq", [128, 8, 512], BF16)
            W.wk, W.Bwk = _sb(k, p1, "g_wk", [128, 8, 512], BF16)
            W.wv, W.Bwv = _sb(k, p1, "g_wv", [128, 8, 1024], BF16)
            W.wgd, W.Bwgd = _sb(k, p1, "g_wgd", [128, 8, 16], BF16)
            W.wgu, W.Bwgu = _sb(k, p1, "g_wgu", [32, 512], F32)
            W.par = 0
            W.sets = []
            for i in range(2):
                Z = NS()
                Z.gd, Z.Bgd = _sb(k, p1, f"g_gd{i}", [32, 128], F32)
                Z.graw, Z.Bgraw = _sb(k, p1, f"g_graw{i}", [128, 512], F32)
                Z.eW, Z.BeW = _sb(k, p1, f"g_eW{i}", [128, 512], F32)
                Z.egc, Z.Begc = _sb(k, p1, f"g_egc{i}", [128, 4, 128], F32)
                Z.engc, Z.Bengc = _sb(k, p1, f"g_engc{i}", [128, 4, 128], F32)
                Z.qg, Z.Bqg = _sb(k, p1, f"g_qg{i}", [128, 4, 128], BF16)
                Z.kg, Z.Bkg = _sb(k, p1, f"g_kg{i}", [128, 4, 128], BF16)
                Z.kd, Z.Bkd = _sb(k, p1, f"g_kd{i}", [128, 512], BF16)
                Z.vt, Z.Bvt = _sb(k, p1, f"g_vt{i}", [128, 1024], BF16)
                Z.atm, Z.Batm = _sb(k, p1, f"g_atm{i}", [128, 4, 128], BF16)
                Z.qt, Z.Bqt = _sb(k, p1, f"g_qt{i}", [128, 4, 128], BF16)
                W.sets.append(Z)
            osb = [_sb(k, p1, f"g_osb{i}", [128, 8, 128], F32) for i in range(2)]
            dma(k, W.wq[:, :, :], win[:, :, 0:512], [k.Bw], [W.Bwq], q="gpsimd")
            dma(k, W.wk[:, :, :], win[:, :, 512:1024], [k.Bw], [W.Bwk], q="gpsimd")
            dma(k, W.wv[:, :, :], win[:, :, 1024:2048], [k.Bw], [W.Bwv], q="gpsimd")
            dma(k, W.wgd[:, :, :], win[:, :, 3072:3088], [k.Bw], [W.Bwgd], q="gpsimd")
            dma(k, W.wgu[0:16, :], k.din["gla_w_gate_up"][0], [k.Bw], [W.Bwgu], q="sync")
            dma(k, W.wgu[16:17, :], k.din["gla_b_gate"][0:1, :], [k.Bw], [W.Bwgu], q="sync")
            for Z in W.sets:
                P.op("vector", lambda e, Z=Z: e.memset(Z.gd[:, :], 1.0), writes=[Z.Bgd])
            P.op("vector", lambda e: e.memset(W.S[:, :, :], 0.0), writes=W.BS)
            P.op("vector", lambda e: e.memset(W.Sbf[:, :, :], 0.0), writes=W.BSbf)
            P.op("vector", lambda e: e.memset(W.ecum[:, :, :], 1.0), writes=[W.Becum])
            gla_subtile(k, W, C_META, NMETA, 1, NMETA, osm[:, :, 0:NMETA], Bosm)
            for h in range(4):
                ts(k, W.S[:, h, :], W.S[:, h, :], k.fA, None, ALU.mult, None, [W.BS[h], k.Bconst], [W.BS[h]])
                act(k, W.Sbf[:, h, :], W.S[:, h, :], AF.Copy, [W.BS[h]], [W.BSbf[h]])
            for sti in range(16):
                ob, Bob = osb[sti % 2]
                gla_subtile(k, W, sti * 128, 128, 2, 64, ob, Bob, ecum_b0=2 * sti)
                dma(k, o_scr[:, :, sti * 128:(sti + 1) * 128], ob[:, :, :], [Bob], [Bscr], q="sync")
            for h in range(4):
                dma(k, cc_in[:, h * 256:(h + 1) * 256], W.S[:, h, :], [W.BS[h]], [Bccin], q="gpsimd")
            P.dma("gpsimd", lambda e: e.collective_compute("AllGather", ALU.bypass,
                                                           replica_groups=[[0, 1], [2, 3], [4, 5], [6, 7]],
                                                           ins=[cc_in], outs=[cc_out]),
                  reads=[Bccin], writes=[Bccout], inc=1)
            for h in range(4):
                cp(k, Sloc[:, h, :], W.S[:, h, :], [W.BS[h]], [BSloc])
            for sq_i in range(4):
                for h in range(4):
                    dma(k, W.S[:, h, :], k.din["state"][sq_i, h], [k.Bw], [W.BS[h]], q="sync")
                    act(k, W.Sbf[:, h, :], W.S[:, h, :], AF.Copy, [W.BS[h]], [W.BSbf[h]])
                cs = NMETA + 32 * sq_i
                gla_subtile(k, W, C_SAMP + 32 * sq_i, 32, 1, 32, osm[:, :, cs:cs + 32], Bosm)
                for h in range(4):
                    dma(k, k.dout["o_ss"][sq_i, h], W.S[:, h, :], [W.BS[h]], [k.Bw], q="sync")
            P.barrier()
            P.emit()
        with ExitStack() as p2:
            W2 = NS()
            W2.wr, W2.Bwr = _sb(k, p2, "g_wr", [128, 8, 1024], BF16)
            W2.wout, W2.Bwout = _sb(k, p2, "g_wout", [128, 8, 1024], BF16)
            W2.rstd, _ = _sb(k, p2, "g2_rstd", [128, 4, 256], F32)
            W2.Brstd = [Buf(f"g2_rstd{h}") for h in range(4)]
            W2.tmp, W2.Btmp = _sb(k, p2, "g2_tmp", [128, 256], F32)
            W2.y, W2.By = _sb(k, p2, "g2_y", [128, 8, 256], BF16)
            W2.sq8, W2.Bsq8 = _sb(k, p2, "g2_sq8", [128, 8, 256], BF16)
            W2.Brstd1 = Buf("g2_rstd_all")
            sr8s = [_sb(k, p2, f"g2_sr8{i}", [128, 8, 256], BF16) for i in range(2)]
            o2 = [_sb(k, p2, f"g2_o{i}", [128, 8, 256], F32) for i in range(2)]
            qts = [_sb(k, p2, f"g2_qtl{i}", [128, 4, 256], BF16) for i in range(2)]
            Srv, BSrv = _sb(k, p2, "g2_Srv", [128, 4, 256], F32)
            Srb, BSrb = _sb(k, p2, "g2_Srb", [128, 4, 256], BF16)
            dma(k, W2.wr[:, :, :], win[:, :, 2048:3072], [k.Bw], [W2.Bwr], q="gpsimd")
            dma(k, W2.wout[:, :, :], wout_d, [k.Bw], [W2.Bwout], q="gpsimd")
            dma(k, Srv[:, :, :], cc_out[0:128, :].rearrange("p (h v) -> p h v", h=4), [Bccout], [BSrv], q="sync")
            ts(k, Srv[:, :, :], Srv[:, :, :], k.fB, None, ALU.mult, None, [BSrv, k.Bconst], [BSrv])
            cp(k, Srb[:, :, :], Srv[:, :, :], [BSrv], [BSrb])
            for h in range(4):
                stt(k, Srv[:, h, :], Srv[:, h, :], W.ecum[:, 32, h:h + 1], Sloc[:, h, :], ALU.mult, ALU.add,
                    [BSrv, W.Becum, BSloc], [BSrv])
            dma(k, k.dout["o_sp"].rearrange("h p v -> p h v"), Srv[:, :, :], [BSrv], [k.Bw], q="sync")
            tiles2 = [(C_META, 144, False)] + [(ti * 256, 256, True) for ti in range(8)]

            def load2(i):
                c0, n, corr = tiles2[i]
                if corr:
                    ob, Bob = o2[i % 2]
                    qt, Bqt = qts[i % 2]
                    dma(k, ob[:, :, :], o_scr[:, :, c0:c0 + 256], [Bscr], [Bob], q="sync")
                    dma(k, qt[:, :, :], k.qtl_scr[:, :, c0:c0 + 256], [k.Bqtl], [Bqt], q="sync")

            load2(0)
            gla_post_r(k, W2, tiles2[0][0], tiles2[0][1], *sr8s[0])
            for i, (c0, n, corr) in enumerate(tiles2):
                if i + 1 < len(tiles2):
                    load2(i + 1)
                if corr:
                    ob, Bob = o2[i % 2]
                    qt, Bqt = qts[i % 2]
                    for h in range(4):
                        for vc in range(2):
                            c = h * 2 + vc
                            psc, Bpsc = next_ps(k)
                            mm(k, psc[:, :256], Srb[:, h, vc * 128:(vc + 1) * 128], qt[:, h, :], True, True,
                               [BSrb, Bqt], [Bpsc])
                            tt(k, ob[:, c, :], ob[:, c, :], psc[:, :256], ALU.add, [Bob, Bpsc], [Bob])
                else:
                    ob, Bob = osm, Bosm
                gla_post_norm(k, W2, ob, Bob, n, *sr8s[i % 2])
                if i + 1 < len(tiles2):
                    gla_post_r(k, W2, tiles2[i + 1][0], tiles2[i + 1][1], *sr8s[(i + 1) % 2])
                gla_post_out(k, W2, c0, n)
            P.barrier()
            P.emit()
    k.ps_n = 8
    k.ps_rr = 0


SCALE = 192.0 ** -0.5


def attend(k, M, qlat, qr, nq, keysets, olat, Bolat, Bq, poset):
    po0, po1, pden, Bpo = poset
    nks = len(keysets)

    def scores(i):
        (klT, krT, ktok, nk, bias, q0, maskfix, Bk) = keysets[i]
        pS, BpS = next_ps(k)
        q_0, q_1, q_r = qlat(0), qlat(1), qr
        if q0 > 0:
            q_0, q_1, q_r = q_0[:, q0:nq], q_1[:, q0:nq], q_r[:, q0:nq]
        mm(k, pS[:nk, q0:nq], klT(0), q_0, True, False, Bk + Bq, [BpS], signal=False)
        mm(k, pS[:nk, q0:nq], klT(1), q_1, False, False, Bk + Bq, [BpS], signal=False)
        mm(k, pS[:nk, q0:nq], krT, q_r, False, True, Bk + Bq, [BpS], signal=True)
        return pS, BpS

    def finish(i, pS, BpS):
        (klT, krT, ktok, nk, bias, q0, maskfix, Bk) = keysets[i]
        pT, BpT = M.pT[i % 3]
        if bias is None:
            act(k, pT[:nk, q0:nq], pS[:nk, q0:nq], AF.Exp, [BpS], [BpT], scale=SCALE)
        else:
            act(k, pT[:nk, q0:nq], pS[:nk, q0:nq], AF.Exp, [BpS, k.Bconst], [BpT], scale=SCALE, bias=bias)
        if maskfix:
            k.P.op("vector", lambda e, pT=pT, q0=q0: e.memset(pT[64:128, q0:q0 + 64], 0.0), writes=[BpT])
        first, last = (i == 0), (i == nks - 1)
        mm(k, po0[:, q0:nq], ktok[:nk, 0:128], pT[:nk, q0:nq], first, last, Bk + [BpT], [Bpo[0]], signal=False)
        mm(k, po1[:, q0:nq], ktok[:nk, 128:256], pT[:nk, q0:nq], first, last, Bk + [BpT], [Bpo[1]], signal=True)
        acc, Bacc = M.acc[0]
        tt(k, acc[:nk, q0:nq], acc[:nk, q0:nq], pT[:nk, q0:nq], ALU.add, [Bacc, BpT], [Bacc],
           eng="vector")

    k.P.op("vector", lambda e: e.memset(M.acc[0][0][:, :nq], 0.0), writes=[M.acc[0][1]])
    LOOK = 2
    pend = [scores(i) for i in range(min(LOOK, nks))]
    for i in range(nks):
        if i + LOOK < nks:
            pend.append(scores(i + LOOK))
        finish(i, *pend.pop(0))
    mm(k, pden[:, :nq], k.ones_f[:, :], M.acc[0][0][:, :nq], True, True, [M.acc[0][1], k.Bconst], [Bpo[2]], signal=True)
    act(k, M.oraw[:, 0, :nq], po0[:, :nq], AF.Copy, [Bpo[0]], [M.Boraw])
    act(k, M.oraw[:, 1, :nq], po1[:, :nq], AF.Copy, [Bpo[1]], [M.Boraw])
    act(k, M.rden[:, :nq], pden[:, :nq], AF.Copy, [Bpo[2]], [M.Brden])
    k.P.op("vector", lambda e: e.reciprocal(out=M.rden[:, :nq], in_=M.rden[:, :nq]), reads=[M.Brden], writes=[M.Brden])
    tt(k, olat[:, 0, :nq], M.oraw[:, 0, :nq], M.rden[:, :nq], ALU.mult, [M.Boraw, M.Brden], [Bolat])
    tt(k, olat[:, 1, :nq], M.oraw[:, 1, :nq], M.rden[:, :nq], ALU.mult, [M.Boraw, M.Brden], [Bolat])


def mla_phase(k):
    P = k.P
    nc = k.nc
    wdn_d = k.din["mla_w_down"][0].rearrange("(kc p) f -> p kc f", p=128)
    wuq_d = k.din["mla_w_uq"][0].rearrange("(kc p) f -> p kc f", p=128)
    wuv_d = k.din["mla_w_uv"][0].rearrange("(rc p) h v -> p rc h v", p=128)
    wo_d = k.din["mla_w_out"][0].rearrange("(c p) d -> p c d", p=128)
    h_scr = nc.dram_tensor("mla_h_scr", [128, 8, T], F32, kind="Internal").ap()
    cc_ins = [nc.dram_tensor(f"mla_cc_in{i}", [128, 2560], BF16, kind="Internal").ap() for i in range(4)]
    cc_outs = [nc.dram_tensor(f"mla_cc_out{i}", [256, 2560], BF16, kind="Internal").ap() for i in range(4)]
    Bhs, Bccin, Bccout = Buf("h_scr"), Buf("ccin2"), Buf("ccout2")
    k.ps_n = 5
    k.ps_rr = 0
    with ExitStack() as st:
        M = NS()
        Bprev = Buf("prev")
        kropeT, BkropeT = _sb(k, st, "m_kropeT", [128, T], BF16)
        pkropeT, _ = _sb(k, st, "m_pkropeT", [128, NF], BF16)
        qlat_s, Bqs = _sb(k, st, "m_qlat_s", [128, 2, 4, 8, 32], BF16)
        qr_s, _ = _sb(k, st, "m_qr_s", [128, 4, 8, 32], BF16)
        M.pT = [_sb(k, st, f"m_pT{i}", [128, 512], BF16) for i in range(3)]
        M.oraw, M.Boraw = _sb(k, st, "m_oraw", [128, 2, 512], F32)
        M.acc = [_sb(k, st, f"m_acc{i}", [128, 512], F32) for i in range(2)]
        M.rden, M.Brden = _sb(k, st, "m_rden", [128, 512], F32)
        olat, Bolat = _sb(k, st, "m_olat", [128, 2, 512], BF16)
        olat2, Bolat2 = _sb(k, st, "m_olat2", [128, 2, 512], BF16)
        t1, Bt1 = _sb(k, st, "m_t1", [32, 512], F32)
        t2, Bt2 = _sb(k, st, "m_t2", [32, 512], F32)
        wuq, Bwuq = _sb(k, st, "m_wuq", [128, 3, 1536], BF16)
        wukT, BwukT = _sb(k, st, "m_wukT", [128, 8, 256], BF16)
        wuv, Bwuv = _sb(k, st, "m_wuv", [128, 2, 8, 128], BF16)
        sw = ExitStack()
        wdn, Bwdn = _sb(k, sw, "m_wdn", [128, 8, 704], BF16)
        dma(k, wdn[:, :, :], wdn_d, [k.Bw], [Bwdn], q="gpsimd")
        dma(k, wuq[:, :, :], wuq_d, [k.Bw], [Bwuq], q="gpsimd")
        dma(k, wukT[:, :, :], k.din["w_ukT"], [k.Bw], [BwukT], q="gpsimd")
        dma(k, wuv[:, :, :, :], wuv_d, [k.Bw], [Bwuv], q="gpsimd")
        P.op("vector", lambda e: e.memset(kropeT[64:128, :], 0.0), writes=[BkropeT])
        P.op("vector", lambda e: e.memset(pkropeT[64:128, :], 0.0), writes=[Bprev])
        with ExitStack() as s0:
            rstd, Brstd = _sb(k, s0, "m_rstd", [128, T], F32)
            sq0, Bsq0 = _sb(k, s0, "m_sq0", [128, 512], F32)
            sq1, Bsq1 = _sb(k, s0, "m_sq1", [128, 512], F32)
            for c in range(8):
                dma(k, h_scr[:, c, :], k.h[:, c, :], [k.Bh], [Bhs], q="sync")
            make_xn(k, k.normw["mix_1"], rstd, Brstd, [sq0, sq1], [Bsq0, Bsq1])
            P.barrier()
            P.emit()
        flat = k.h[:, :, :].rearrange("p c t -> p (c t)")
        off = [0]

        def carve(nf32):
            a = flat[:, off[0]:off[0] + nf32]
            off[0] += nf32
            assert off[0] <= 8 * T
            return a

        tab = carve(2 * T).rearrange("p (a t) -> p a t", a=2)
        Btab = Buf("tab")
        klatT = carve(T).bitcast(BF16).rearrange("p (a t) -> p a t", a=2)
        BklatT = Buf("klatT")
        ktok = carve(21 * 128).bitcast(BF16).rearrange("p (a r) -> p a r", a=21)
        Bktok = Buf("ktok")
        cqn = carve(3 * T // 2).bitcast(BF16).rearrange("p (a t) -> p a t", a=3)
        Bcqn = Buf("cqn")
        pklatT = carve(2048).bitcast(BF16).rearrange("p (a t) -> p a t", a=2)
        pktok = carve(2048).bitcast(BF16).rearrange("p (a r) -> p a r", a=16)
        dma(k, tab[0:32, :, :], k.din["rope"], [k.Bw], [Btab], q="sync")
        ov = k.xn
        GROUPS = [(i * 128, 128) for i in range(16)] + [(C_META, 16)] + [(C_SAMP + 32 * i, 32) for i in range(4)]

        def rope(psa, Bpsa, psb, Bpsb, dst, Bdst, t0, tn):
            cos, sin = tab[0:32, 0, t0:t0 + tn], tab[0:32, 1, t0:t0 + tn]
            tt(k, t1[:, :tn], psa[0:32, :tn], cos, ALU.mult, [Bpsa, Btab], [Bt1])
            tt(k, t2[:, :tn], psb[0:32, :tn], sin, ALU.mult, [Bpsb, Btab], [Bt2])
            tt(k, dst[0:32, t0:t0 + tn], t1[:, :tn], t2[:, :tn], ALU.subtract, [Bt1, Bt2], [Bdst])
            tt(k, t1[:, :tn], psb[0:32, :tn], cos, ALU.mult, [Bpsb, Btab], [Bt1])
            tt(k, t2[:, :tn], psa[0:32, :tn], sin, ALU.mult, [Bpsa, Btab], [Bt2])
            tt(k, dst[32:64, t0:t0 + tn], t1[:, :tn], t2[:, :tn], ALU.add, [Bt1, Bt2], [Bdst])

        with ExitStack() as sa:
            cq, Bcq = _sb(k, sa, "m_cq", [128, 3, 512], F32)
            ckv, Bckv = _sb(k, sa, "m_ckv", [128, 2, 512], F32)
            klf, Bklf = _sb(k, sa, "m_klf", [128, 2, 512], F32)
            krf, Bkrf = _sb(k, sa, "m_krf", [64, T], F32)
            sqa = [_sb(k, sa, f"m_sqa{i}", [128, 512], F32) for i in range(2)]
            rs, Brs = _sb(k, sa, "m_rs", [128, 512], F32)
            stg = [_sb(k, sa, f"m_stg{i}", [128, 256], F32) for i in range(2)]
            stgr = [_sb(k, sa, f"m_stgr{i}", [128, 64], F32) for i in range(2)]
            for (t0, tn) in TILES:
                pss = []
                for c in range(5):
                    ps, Bps = next_ps(k)
                    for kc in range(8):
                        mm(k, ps[:, :tn], wdn[:, kc, c * 128:(c + 1) * 128], k.xn[:, kc, t0:t0 + tn], kc == 0, kc == 7,
                           [Bwdn, k.Bxn], [Bps], signal=(kc == 7))
                    pss.append((ps, Bps))
                for c in range(3):
                    act(k, cq[:, c, :tn], pss[c][0][:, :tn], AF.Copy, [pss[c][1]], [Bcq])
                for c in range(2):
                    act(k, ckv[:, c, :tn], pss[3 + c][0][:, :tn], AF.Copy, [pss[3 + c][1]], [Bckv])
                rms_rstd(k, cq, Bcq, 3, 384, rs, Brs, [sqa[0][0], sqa[1][0]], [sqa[0][1], sqa[1][1]],
                         tiles=[(0, tn)])
                psa, Bpsa = next_ps(k)
                psb, Bpsb = next_ps(k)
                for kc in range(8):
                    mm(k, psa[0:32, :tn], wdn[:, kc, 640:672], k.xn[:, kc, t0:t0 + tn], kc == 0, kc == 7,
                       [Bwdn, k.Bxn], [Bpsa], signal=(kc == 7))
                for kc in range(8):
                    mm(k, psb[0:32, :tn], wdn[:, kc, 672:704], k.xn[:, kc, t0:t0 + tn], kc == 0, kc == 7,
                       [Bwdn, k.Bxn], [Bpsb], signal=(kc == 7))
                for c in range(3):
                    stt(k, cqn[:, c, t0:t0 + tn], cq[:, c, :tn], k.qnorm[:, c:c + 1], rs[:, :tn], ALU.mult, ALU.mult,
                        [Bcq, Brs, k.Bconst], [Bcqn])
                rms_rstd(k, ckv, Bckv, 2, 256, rs, Brs, [sqa[0][0], sqa[1][0]], [sqa[0][1], sqa[1][1]],
                         tiles=[(0, tn)])
                for c in range(2):
                    stt(k, klf[:, c, :tn], ckv[:, c, :tn], k.kvnorm[:, c:c + 1], rs[:, :tn], ALU.mult, ALU.mult,
                        [Bckv, Brs, k.Bconst], [Bklf])
                    cp(k, klatT[:, c, t0:t0 + tn], klf[:, c, :tn], [Bklf], [BklatT])
                rope(psa, Bpsa, psb, Bpsb, krf, Bkrf, t0, tn)
                cp(k, kropeT[0:64, t0:t0 + tn], krf[:, t0:t0 + tn], [Bkrf], [BkropeT])
                for gi, (g0, gn) in enumerate(GROUPS):
                    if not (t0 <= g0 < t0 + tn):
                        continue
                    pst, Bpst = next_ps(k)
                    for c in range(2):
                        k.P.op("tensor", lambda e, c=c, g0=g0, gn=gn, pst=pst, t0=t0: e.transpose(
                            pst[:gn, c * 128:(c + 1) * 128], klf[:, c, g0 - t0:g0 - t0 + gn], k.ident[:, :]),
                            reads=[Bklf, k.Bconst], writes=[Bpst])
                    sg_, Bsg_ = stg[gi % 2]
                    cp(k, sg_[:gn, :], pst[:gn, 0:256], [Bpst], [Bsg_])
                    act(k, ktok[:gn, gi, :], sg_[:gn, :], AF.Copy, [Bsg_], [Bktok])
                    dma(k, k.dout["o_lat"][g0:g0 + gn, :], sg_[:gn, :], [Bsg_], [k.Bw], q="sync")
                    pst2, Bpst2 = next_ps(k)
                    k.P.op("tensor", lambda e, g0=g0, gn=gn, pst2=pst2: e.transpose(
                        pst2[:gn, 0:64], krf[0:64, g0:g0 + gn], k.ident[0:64, 0:64]),
                        reads=[Bkrf, k.Bconst], writes=[Bpst2])
                    sr_, Bsr_ = stgr[gi % 2]
                    cp(k, sr_[:gn, :], pst2[:gn, 0:64], [Bpst2], [Bsr_])
                    dma(k, k.dout["o_rope"][g0:g0 + gn, :], sr_[:gn, :], [Bsr_], [k.Bw], q="sync")
                if t0 == 1536:
                    dma(k, cc_ins[0][:, 0:2048], klatT[:, 0, 0:NF], [BklatT], [Bccin], q="gpsimd", asyn=True)
                    dma(k, cc_ins[1][:, 0:2048], klatT[:, 1, 0:NF], [BklatT], [Bccin], q="gpsimd", asyn=True)
                    dma(k, cc_ins[2][:, 0:2560].rearrange("p (a r) -> p a r", a=10), ktok[:, 0:10, :], [Bktok], [Bccin], q="gpsimd", asyn=True)
                    dma(k, cc_ins[3][:, 0:1536].rearrange("p (a r) -> p a r", a=6), ktok[:, 10:16, :], [Bktok], [Bccin], q="gpsimd", asyn=True)
                    dma(k, cc_ins[0][0:64, 2048:2560], kropeT[0:64, 0:512], [BkropeT], [Bccin], q="gpsimd", asyn=True)
                    dma(k, cc_ins[1][0:64, 2048:2560], kropeT[0:64, 512:1024], [BkropeT], [Bccin], q="gpsimd", asyn=True)
                    dma(k, cc_ins[3][0:64, 1536:2560], kropeT[0:64, 1024:2048], [BkropeT], [Bccin], q="gpsimd", asyn=True)
                    for ci in range(4):
                        P.dma("gpsimd", lambda e, ci=ci: e.collective_compute(
                            "AllGather", ALU.bypass, replica_groups=[[0, 1], [2, 3], [4, 5], [6, 7]],
                            ins=[cc_ins[ci]], outs=[cc_outs[ci]]), reads=[Bccin], writes=[Bccout], inc=1, asyn=True)
                    dma(k, pklatT[:, 0, :], cc_outs[0][0:128, 0:2048], [Bccout], [Bprev], q="gpsimd", asyn=True)
                    dma(k, pklatT[:, 1, :], cc_outs[1][0:128, 0:2048], [Bccout], [Bprev], q="gpsimd", asyn=True)
                    dma(k, pktok[:, 0:10, :], cc_outs[2][0:128, 0:2560].rearrange("p (a r) -> p a r", a=10), [Bccout], [Bprev], q="gpsimd", asyn=True)
                    dma(k, pktok[:, 10:16, :], cc_outs[3][0:128, 0:1536].rearrange("p (a r) -> p a r", a=6), [Bccout], [Bprev], q="gpsimd", asyn=True)
                    dma(k, pkropeT[0:64, 0:512], cc_outs[0][0:64, 2048:2560], [Bccout], [Bprev], q="gpsimd", asyn=True)
                    dma(k, pkropeT[0:64, 512:1024], cc_outs[1][0:64, 2048:2560], [Bccout], [Bprev], q="gpsimd", asyn=True)
                    dma(k, pkropeT[0:64, 1024:2048], cc_outs[3][0:64, 1536:2560], [Bccout], [Bprev], q="gpsimd", asyn=True)
            P.barrier(skip_async=True)
            P.emit()
        sw.close()

        if k.mla_stop == "A":
            return
        def own_set(g, q0=0, maskfix=False):
            g0, gn = GROUPS[g]
            return (lambda rc, g0=g0, gn=gn: klatT[:, rc, g0:g0 + gn], kropeT[:, g0:g0 + gn], ktok[:, g, :], gn, None,
                    q0, maskfix, [BklatT, BkropeT, Bktok])

        def prev_set(g):
            return (lambda rc, g=g: pklatT[:, rc, g * 128:(g + 1) * 128], pkropeT[:, g * 128:(g + 1) * 128],
                    pktok[:, g, :], 128, k.prevbias[:, 0:1], 0, False, [Bprev])

        scc = ExitStack()
        pasts = []
        pT_, BpT_ = _sb(k, scc, "m_pastT0", [128, 2, PAST], BF16)
        pk_, Bpk_ = _sb(k, scc, "m_ptok0", [128, 16, 256], BF16)
        pR_, BpR_ = _sb(k, scc, "m_pastR0", [128, PAST], BF16)
        P.op("vector", lambda e, pR_=pR_: e.memset(pR_[64:128, :], 0.0), writes=[BpR_])
        pasts.append((pT_, BpT_, pk_, Bpk_, pR_, BpR_))

        def load_past(s_i):
            pT_, BpT_, pk_, Bpk_, pR_, BpR_ = pasts[s_i % 2]
            dma(k, pT_[:, :, :], k.din["cache_latT"][s_i].rearrange("(a p) t -> p a t", p=128), [k.Bw], [BpT_],
                q="gpsimd")
            cl = k.din["cache_lat"][s_i].rearrange("(t p) r -> p t r", p=128)
            for q2 in range(2):
                dma(k, pk_[:, 8 * q2:8 * q2 + 8, :], cl[:, 8 * q2:8 * q2 + 8, :], [k.Bw], [Bpk_], q="gpsimd")
            dma(k, pR_[0:64, :], k.din["cache_ropeT"][s_i], [k.Bw], [BpR_], q="gpsimd")


        with ExitStack() as sc:
            qnope, Bqnope = _sb(k, sc, "m_qnope", [128, T], BF16)
            qr, Bqr = _sb(k, sc, "m_qr", [128, T], BF16)
            P.op("vector", lambda e: e.memset(qr[64:128, :], 0.0), writes=[Bqr])
            QR_ZERO = True
            qlat, Bqlat = _sb(k, sc, "m_qlat", [128, 2, T], BF16)
            k.ps_n = 3
            k.ps_rr = 0
            posets = [(k.ps[3], k.ps[4], k.ps[7], [k.psb[3], k.psb[4], k.psb[7]]),
                      (k.ps[5], k.ps[6], k.ps[7], [k.psb[5], k.psb[6], k.psb[7]])]
            olats = [(olat, Bolat), (olat2, Bolat2)]
            acnt = [0]

            def ov_out(h, c0, n, ol, Bol):
                psv, Bpsv = next_ps(k)
                for rc in range(2):
                    mm(k, psv[:, :n], wuv[:, rc, h, :], ol[:, rc, :n], rc == 0, rc == 1, [Bwuv, Bol], [Bpsv],
                       signal=(rc == 1))
                act(k, ov[:, h, c0:c0 + n], psv[:, :n], AF.Copy, [Bpsv], [k.Bxn])

            load_past(0)
            pend_ov = []
            Bqn_t = [Buf(f"qnope_t{i}") for i in range(len(TILES))]
            Bqr_t = [Buf(f"qr_t{i}") for i in range(len(TILES))]
            Bql_t = [Buf(f"qlat_t{i}") for i in range(len(TILES))]

            def q_stage(h, ti):
                t0, tn = TILES[ti]
                psn, Bpsn = next_ps(k)
                for kc in range(3):
                    mm(k, psn[:, :tn], wuq[:, kc, h * 192:h * 192 + 128], cqn[:, kc, t0:t0 + tn], kc == 0, kc == 2,
                       [Bwuq, Bcqn], [Bpsn], signal=(kc == 2))
                act(k, qnope[:, t0:t0 + tn], psn[:, :tn], AF.Copy, [Bpsn], [Bqn_t[ti]])
                psa, Bpsa = next_ps(k)
                for kc in range(3):
                    mm(k, psa[0:32, :tn], wuq[:, kc, h * 192 + 128:h * 192 + 160], cqn[:, kc, t0:t0 + tn], kc == 0,
                       kc == 2, [Bwuq, Bcqn], [Bpsa], signal=(kc == 2))
                psb, Bpsb = next_ps(k)
                for kc in range(3):
                    mm(k, psb[0:32, :tn], wuq[:, kc, h * 192 + 160:h * 192 + 192], cqn[:, kc, t0:t0 + tn], kc == 0,
                       kc == 2, [Bwuq, Bcqn], [Bpsb], signal=(kc == 2))
                rope(psa, Bpsa, psb, Bpsb, qr, Bqr_t[ti], t0, tn)
                for rc in range(2):
                    psl, Bpsl = next_ps(k)
                    mm(k, psl[:, :tn], wukT[:, h, rc * 128:(rc + 1) * 128], qnope[:, t0:t0 + tn], True, True,
                       [BwukT, Bqn_t[ti]], [Bpsl])
                    act(k, qlat[:, rc, t0:t0 + tn], psl[:, :tn], AF.Copy, [Bpsl], [Bql_t[ti]])

            q_stage(0, 0)
            for h in range(8):
                for qt in range(4):
                    q_stage(h, qt + 1)
                    c0 = qt * 512
                    ks = [own_set(16)]
                    ks += [own_set(4 * qt + j, q0=128 * j, maskfix=True) for j in range(1, 4)]
                    ks += [own_set(4 * qt, q0=0, maskfix=True)]
                    ks += [own_set(g) for g in range(4 * qt)]
                    ks += [prev_set(g) for g in range(16)]
                    ol, Bol = olats[acnt[0] % 2]
                    attend(k, M, lambda rc, c0=c0: qlat[:, rc, c0:c0 + 512], qr[:, c0:c0 + 512], 512, ks, ol, Bol,
                           [Bql_t[qt], Bqr_t[qt]], posets[acnt[0] % 2])
                    if pend_ov:
                        ov_out(*pend_ov.pop())
                    pend_ov.append((h, c0, 512, ol, Bol))
                    acnt[0] += 1
                for rc in range(2):
                    cp(k, qlat_s[:, rc, :, h, :], qlat[:, rc, C_SAMP:T].rearrange("p (s q) -> p s q", s=4), [Bql_t[4]], [Bqs])
                cp(k, qr_s[:, :, h, :], qr[:, C_SAMP:T].rearrange("p (s q) -> p s q", s=4), [Bqr_t[4]], [Bqs])
                if h + 1 < 8:
                    q_stage(h + 1, 0)
                ol, Bol = olats[acnt[0] % 2]
                attend(k, M, lambda rc: qlat[:, rc, C_META:C_META + 16], qr[:, C_META:C_META + 16], 16, [own_set(16)],
                       ol, Bol, [Bql_t[4], Bqr_t[4]], posets[acnt[0] % 2])
                if pend_ov:
                    ov_out(*pend_ov.pop())
                pend_ov.append((h, C_META, 16, ol, Bol))
                acnt[0] += 1
            if pend_ov:
                ov_out(*pend_ov.pop())
            P.barrier()
            P.emit()
        if k.mla_stop == "C":
            return
        with ExitStack() as sc2:
            for i in range(1, 2):
                pT_, BpT_ = _sb(k, sc2, f"m_pastT{i}", [128, 2, PAST], BF16)
                pk_, Bpk_ = _sb(k, sc2, f"m_ptok{i}", [128, 16, 256], BF16)
                pR_, BpR_ = _sb(k, sc2, f"m_pastR{i}", [128, PAST], BF16)
                P.op("vector", lambda e, pR_=pR_: e.memset(pR_[64:128, :], 0.0), writes=[BpR_])
                pasts.append((pT_, BpT_, pk_, Bpk_, pR_, BpR_))
            k.ps_n = 3
            k.ps_rr = 0
            posets2 = [(k.ps[3], k.ps[4], k.ps[7], [k.psb[3], k.psb[4], k.psb[7]]),
                       (k.ps[5], k.ps[6], k.ps[7], [k.psb[5], k.psb[6], k.psb[7]])]
            olats2 = [(olat, Bolat), (olat2, Bolat2)]
            skl, Bskl = _sb(k, sc2, "m_skl", [128, 2, NSAMP], BF16)
            skt, Bskt = _sb(k, sc2, "m_skt", [128, 4, 256], BF16)
            cp(k, skl[:, :, :], klatT[:, :, C_SAMP:T], [BklatT], [Bskl])
            cp(k, skt[:, :, :], ktok[:, 17:21, :], [Bktok], [Bskt])
            for c in range(8):
                dma(k, k.h[:, c, :], h_scr[:, c, :], [Bhs], [k.Bh, BklatT, Bktok], q="sync")

            for s_i in range(4):
                if s_i + 1 < 4:
                    load_past(s_i + 1)
                pastT, BpastT, ptok, Bptok, pastR, BpastR = pasts[s_i % 2]
                ks = [(lambda rc, g=g: pastT[:, rc, g * 128:(g + 1) * 128], pastR[:, g * 128:(g + 1) * 128],
                       ptok[:, g, :], 128, None, 0, False, [BpastT, BpastR, Bptok]) for g in range(16)]
                ks += [(lambda rc, s_i=s_i: skl[:, rc, 32 * s_i:32 * s_i + 32],
                        kropeT[:, C_SAMP + 32 * s_i:C_SAMP + 32 * s_i + 32], skt[:, s_i, :], 32, None, 0, False,
                        [Bskl, BkropeT, Bskt])]
                ol, Bol = olats2[s_i % 2]
                attend(k, M, lambda rc, s_i=s_i: qlat_s[:, rc, s_i].rearrange("p h q -> p (h q)"),
                       qr_s[:, s_i].rearrange("p h q -> p (h q)"), 256, ks, ol, Bol, [Bqs], posets2[s_i % 2])
                psv, Bpsv = next_ps(k)
                for h in range(8):
                    for rc in range(2):
                        mm(k, psv[:, h * 32:(h + 1) * 32], wuv[:, rc, h, :], ol[:, rc, h * 32:(h + 1) * 32], rc == 0,
                           rc == 1, [Bwuv, Bol], [Bpsv], signal=(rc == 1 and h == 7))
                act(k, ov[:, :, C_SAMP + 32 * s_i:C_SAMP + 32 * s_i + 32],
                    psv[:, 0:256].rearrange("p (h q) -> p h q", h=8), AF.Copy, [Bpsv], [k.Bxn])
            P.barrier()
            P.emit()
        if k.mla_stop in ("C2", "C2prep", "C2dma", "C2t2"):
            return
        scc.close()
        k.ps_n = 8
        k.ps_rr = 0
        with ExitStack() as sd:
            wo, Bwo = _sb(k, sd, "m_wo", [128, 8, D], BF16)
            dma(k, wo[:, :, :], wo_d, [k.Bw], [Bwo], q="gpsimd")
            for d in range(8):
                for (t0, tn) in TILES:
                    psd, Bpsd = next_ps(k)
                    for c in range(8):
                        mm(k, psd[:, :tn], wo[:, c, d * 128:(d + 1) * 128], ov[:, c, t0:t0 + tn], c == 0, c == 7,
                           [Bwo, k.Bxn], [Bpsd], signal=(c == 7))
                    tt(k, k.h[:, d, t0:t0 + tn], k.h[:, d, t0:t0 + tn], psd[:, :tn], ALU.add, [k.Bh, Bpsd], [k.Bh])
            P.barrier()
            P.emit()


def final_phase(k):
    P = k.P
    with ExitStack() as st:
        rstd, Brstd = _sb(k, st, "f_rstd", [128, T], F32)
        sq0, Bsq0 = _sb(k, st, "f_sq0", [128, 512], F32)
        sq1, Bsq1 = _sb(k, st, "f_sq1", [128, 512], F32)
        yb = [_sb(k, st, f"f_y{i}", [128, 512], F32) for i in range(2)]
        nw = k.normw["final"]
        i = 0
        for (t0, tn) in TILES:
            rms_rstd(k, k.h, k.Bh, 8, D, rstd, Brstd, [sq0, sq1], [Bsq0, Bsq1], tiles=[(t0, tn)])
            for c in range(8):
                y, By = yb[i % 2]
                i += 1
                stt(k, y[:, :tn], k.h[:, c, t0:t0 + tn], nw[:, c:c + 1], rstd[:, t0:t0 + tn], ALU.mult, ALU.mult,
                    [k.Bh, Brstd, k.Bconst], [By])
                dma(k, k.dout["o_y"][:, c, t0:t0 + tn], y[:, :tn], [By], [k.Bw], q="sync" if i % 2 else "gpsimd")
        P.barrier()
        P.emit()


W_NAMES = ["ffn1_norm", "ffn1_w_gate", "ffn1_w_up", "ffn1_w_down", "mix_norm", "gla_w_in", "gla_w_gate_up",
           "gla_b_gate", "gla_head_norm", "gla_w_out", "mla_w_down", "mla_q_norm", "mla_w_uq", "mla_kv_norm",
           "mla_w_uk", "mla_w_uv", "mla_w_out", "ffn2_norm", "ffn2_w_gate", "ffn2_w_up", "ffn2_w_down",
           "final_norm"]
W_SHAPES = {
    "ffn1_norm": [2, D], "ffn1_w_gate": [2, D, DFF], "ffn1_w_up": [2, D, DFF], "ffn1_w_down": [2, DFF, D],
    "mix_norm": [2, D], "gla_w_in": [1, D, 3088], "gla_w_gate_up": [1, 16, 512], "gla_b_gate": [1, 512],
    "gla_head_norm": [1, 256], "gla_w_out": [1, D, D], "mla_w_down": [1, D, 704], "mla_q_norm": [1, 384],
    "mla_w_uq": [1, 384, 1536], "mla_kv_norm": [1, 256], "mla_w_uk": [1, 256, 8, 128],
    "mla_w_uv": [1, 256, 8, 128], "mla_w_out": [1, D, D], "ffn2_norm": [2, D], "ffn2_w_gate": [2, D, DFF],
    "ffn2_w_up": [2, D, DFF], "ffn2_w_down": [2, DFF, D], "final_norm": [D],
}


def build(stage=99, mla_stop=None):
    nc = bass.Bass("TRN2", target_bir_lowering=False)
    k = K()
    k.mla_stop = mla_stop
    k.nc = nc
    k.din = {}
    for n in W_NAMES:
        k.din[n] = nc.dram_tensor(n, W_SHAPES[n], F32, kind="ExternalInput").ap()
    k.din["xT"] = nc.dram_tensor("xT", [128, 8, T], F32, kind="ExternalInput").ap()
    k.din["consts"] = nc.dram_tensor("consts", [128, NCONST], F32, kind="ExternalInput").ap()
    k.din["state"] = nc.dram_tensor("state", [4, 4, 128, 256], F32, kind="ExternalInput").ap()
    k.din["rope"] = nc.dram_tensor("rope", [32, 2, T], F32, kind="ExternalInput").ap()
    k.din["w_ukT"] = nc.dram_tensor("w_ukT", [128, 8, 256], F32, kind="ExternalInput").ap()
    k.din["cache_lat"] = nc.dram_tensor("cache_lat", [4, PAST, 256], F32, kind="ExternalInput").ap()
    k.din["cache_ropeT"] = nc.dram_tensor("cache_ropeT", [4, 64, PAST], F32, kind="ExternalInput").ap()
    k.din["cache_latT"] = nc.dram_tensor("cache_latT", [4, 256, PAST], F32, kind="ExternalInput").ap()
    k.dout = {}
    k.dout["o_y"] = nc.dram_tensor("o_y", [128, 8, T], F32, kind="ExternalOutput").ap()
    k.dout["o_lat"] = nc.dram_tensor("o_lat", [T, 256], F32, kind="ExternalOutput").ap()
    k.dout["o_rope"] = nc.dram_tensor("o_rope", [T, 64], F32, kind="ExternalOutput").ap()
    k.dout["o_sp"] = nc.dram_tensor("o_sp", [4, 128, 256], F32, kind="ExternalOutput").ap()
    k.dout["o_ss"] = nc.dram_tensor("o_ss", [4, 4, 128, 256], F32, kind="ExternalOutput").ap()
    if stage < 99:
        k.dout["dbg_h"] = nc.dram_tensor("dbg_h", [128, 8, T], F32, kind="ExternalOutput").ap()

    with ExitStack() as st:
        P = Prog(nc, st)
        k.P = P
        k.h, k.Bh = _sb(k, st, "h", [128, 8, T], F32)
        k.xn, k.Bxn = _sb(k, st, "xn", [128, 8, T], BF16)
        k.Bxnt = [Buf(f"xn_t{i}") for i in range(len(TILES))]
        k.cst, k.Bconst = _sb(k, st, "cst", [128, NCONST], F32)
        k.ones_f = k.cst[:, 0:128]
        k.ones_b, _ = _sb(k, st, "ones_b", [128, 128], BF16)
        k.ident = k.cst[:, 128:256]
        k.tri = k.cst[:, 256:384]
        k.revtri = k.cst[:, 384:512]
        k.mask01 = k.cst[:, 512:640]
        k.TRI = {64: k.cst[:, 256:384], 32: k.cst[:, 1536:1664]}
        k.REV = {64: k.cst[:, 384:512], 32: k.cst[:, 1664:1792]}
        k.MASK4 = {64: k.cst[:, 1024:1536].rearrange("p (h n) -> p h n", h=4),
                   32: k.cst[:, 2048:2560].rearrange("p (h n) -> p h n", h=4)}
        k.fA = k.cst[:, 707:708]
        k.fB = k.cst[:, 708:709]
        k.prevbias = k.cst[:, 709:710]
        k.qnorm = k.cst[:, 696:699]
        k.kvnorm = k.cst[:, 699:701]
        k.hnorm = k.cst[:, 701:703]
        k.Bw = Buf("dram_w")
        k.ps, k.psb = [], []
        for i in range(8):
            t = st.enter_context(nc.psum_tensor(f"ps{i}", [128, 512], F32))
            k.ps.append(t)
            k.psb.append(Buf(f"ps{i}"))
        k.ps_rr = 0
        k.ps_n = 8
        dma(k, k.cst[:, :], k.din["consts"], [k.Bw], [k.Bconst])
        for c in range(8):
            dma(k, k.h[:, c, :], k.din["xT"][:, c, :], [k.Bw], [k.Bh], q="sync" if c % 2 == 0 else "gpsimd")
        cp(k, k.ones_b[:, :], k.cst[:, 0:128], [k.Bconst], [k.Bconst])
        k.normw = {}
        specs = [("ffn1_0", "ffn1_norm", 0), ("ffn1_1", "ffn1_norm", 1), ("mix_0", "mix_norm", 0),
                 ("mix_1", "mix_norm", 1), ("ffn2_0", "ffn2_norm", 0), ("ffn2_1", "ffn2_norm", 1),
                 ("final", "final_norm", None)]
        for i, (nm, src, li) in enumerate(specs):
            k.normw[nm] = k.cst[:, 640 + 8 * i:648 + 8 * i]
        P.barrier()
        P.emit()

        ffn_phase(k, 0, 1)
        if stage >= 2:
            gla_phase(k)
        if stage >= 3:
            ffn_phase(k, 0, 2)
            ffn_phase(k, 1, 1)
        if stage >= 4:
            mla_phase(k)
        if stage >= 5:
            ffn_phase(k, 1, 2, final=True)
        if stage < 99:
            for c in range(8):
                dma(k, k.dout["dbg_h"][:, c, :], k.h[:, c, :], [k.Bh], [k.Bw])
        P.barrier()
        P.emit()
    return nc, k


NORM_SPECS = [("ffn1_norm", 0), ("ffn1_norm", 1), ("mix_norm", 0), ("mix_norm", 1), ("ffn2_norm", 0),
              ("ffn2_norm", 1), ("final_norm", None)]


def make_consts(inputs, core):
    c = np.zeros((128, NCONST), np.float32)
    c[:, 0:128] = 1.0
    c[:, 128:256] = np.eye(128, dtype=np.float32)
    j = np.arange(128)[:, None]
    i = np.arange(128)[None, :]
    same = (j // 64) == (i // 64)
    c[:, 256:384] = np.where(same & (j <= i), -1.0 / 16.0, 0.0)
    c[:, 384:512] = np.where(same & (j > i), -1.0 / 16.0, 0.0)
    c[:, 512:640] = np.where(same & (j <= i), 1.0, 0.0)
    for n, (src, li) in enumerate(NORM_SPECS):
        v = np.asarray(inputs[src], np.float32)
        v = v if li is None else v[li]
        c[:, 640 + 8 * n:648 + 8 * n] = v.reshape(8, 128).T
    c[:, 696:699] = np.asarray(inputs["mla_q_norm"], np.float32)[0].reshape(3, 128).T
    c[:, 699:701] = np.asarray(inputs["mla_kv_norm"], np.float32)[0].reshape(2, 128).T
    c[:, 701:703] = np.asarray(inputs["gla_head_norm"], np.float32)[0].reshape(2, 128).T
    for hh in range(4):
        c[:, 1024 + 128 * hh:1152 + 128 * hh] = c[:, 512:640]
    same32 = (j // 32) == (i // 32)
    c[:, 1536:1664] = np.where(same32 & (j <= i), -1.0 / 16.0, 0.0)
    c[:, 1664:1792] = np.where(same32 & (j > i), -1.0 / 16.0, 0.0)
    for hh in range(4):
        c[:, 2048 + 128 * hh:2176 + 128 * hh] = np.where(same32 & (j <= i), 1.0, 0.0)
    half = core % 2
    c[:, 707] = 1.0 if half == 0 else 0.0
    c[:, 708] = 1.0 if half == 1 else 0.0
    c[:, 709] = 0.0 if half == 1 else NEG
    return c


def rope_table(half):
    pos = np.concatenate([NMETA + half * NF + np.arange(NF), np.arange(NMETA), np.tile(PAST + np.arange(32), 4)])
    inv = (np.float32(10000.0) ** (-np.arange(32, dtype=np.float32) / np.float32(32))).astype(np.float32)
    ang = (pos.astype(np.float32)[None, :] * inv[:, None]).astype(np.float32)
    return np.ascontiguousarray(np.stack([np.cos(ang), np.sin(ang)], 1).astype(np.float32))


def core_inputs(inputs, core):
    b, half = core // 2, core % 2
    xp = np.asarray(inputs["x_prompt"], np.float32)
    xs = np.asarray(inputs["x_sample"], np.float32)
    meta = np.asarray(inputs["meta_tokens"], np.float32)
    rows = np.concatenate([xp[b, half * NF:(half + 1) * NF], meta, xs[4 * core:4 * core + 4].reshape(NSAMP, D)], 0)
    xT = np.ascontiguousarray(rows.T.reshape(8, 128, T).transpose(1, 0, 2))
    m = {"xT": xT, "consts": make_consts(inputs, core)}
    m["state"] = np.ascontiguousarray(np.asarray(inputs["state_gla"], np.float32)[0, 4 * core:4 * core + 4])
    m["cache_lat"] = np.ascontiguousarray(np.asarray(inputs["cache_mla_latent"], np.float32)[0, 4 * core:4 * core + 4])
    m["cache_latT"] = np.ascontiguousarray(m["cache_lat"].transpose(0, 2, 1))
    m["cache_ropeT"] = np.ascontiguousarray(
        np.asarray(inputs["cache_mla_rope"], np.float32)[0, 4 * core:4 * core + 4].transpose(0, 2, 1))
    m["w_ukT"] = np.ascontiguousarray(np.asarray(inputs["mla_w_uk"], np.float32)[0].transpose(2, 1, 0))
    m["rope"] = rope_table(half)
    for n in W_NAMES:
        m[n] = np.ascontiguousarray(np.asarray(inputs[n], np.float32))
    return m


def assemble(results):
    y_p = np.zeros((4, 4096, D), np.float32)
    y_s = np.zeros((32, 32, D), np.float32)
    sp = np.zeros((1, 4, 4, 128, 256), np.float32)
    ss = np.zeros((1, 32, 4, 128, 256), np.float32)
    lat_p = np.zeros((1, 4, NMETA + 4096, 256), np.float32)
    rope_p = np.zeros((1, 4, NMETA + 4096, 64), np.float32)
    lat_s = np.zeros((1, 32, 32, 256), np.float32)
    rope_s = np.zeros((1, 32, 32, 64), np.float32)
    for c in range(8):
        r = results[c]
        b, half = c // 2, c % 2
        yT = r["o_y"].transpose(1, 0, 2).reshape(D, T).T
        y_p[b, half * NF:(half + 1) * NF] = yT[:NF]
        y_s[4 * c:4 * c + 4] = yT[C_SAMP:].reshape(4, 32, D)
        if half == 1:
            sp[0, b] = r["o_sp"]
        ss[0, 4 * c:4 * c + 4] = r["o_ss"]
        lat_p[0, b, NMETA + half * NF:NMETA + (half + 1) * NF] = r["o_lat"][:NF]
        rope_p[0, b, NMETA + half * NF:NMETA + (half + 1) * NF] = r["o_rope"][:NF]
        if half == 0:
            lat_p[0, b, :NMETA] = r["o_lat"][C_META:C_META + NMETA]
            rope_p[0, b, :NMETA] = r["o_rope"][C_META:C_META + NMETA]
        lat_s[0, 4 * c:4 * c + 4] = r["o_lat"][C_SAMP:].reshape(4, 32, 256)
        rope_s[0, 4 * c:4 * c + 4] = r["o_rope"][C_SAMP:].reshape(4, 32, 64)
    return (y_p, y_s, sp, ss, lat_p, rope_p, lat_s, rope_s)


def kernel(**inputs):
    nc, k = build()
    in_maps = [core_inputs(inputs, c) for c in range(8)]
    res = run_bass_kernel_spmd(nc, in_maps, core_ids=list(range(8)))
    return assemble(res.results)
```

```python
import numpy as np
from contextlib import ExitStack
import concourse.bass as bass
import concourse.mybir as mybir
from concourse.bass_utils import run_bass_kernel_spmd

F32 = mybir.dt.float32
BF16 = mybir.dt.bfloat16
AF = mybir.ActivationFunctionType
ALU = mybir.AluOpType

ENGS = ("tensor", "vector", "scalar", "gpsimd", "sync")

D = 1024
DFF = 2816
NF = 2048
NMETA = 16
NSAMP = 128
T = NF + NMETA + NSAMP
C_META = NF
C_SAMP = NF + NMETA
TILES = [(0, 512), (512, 512), (1024, 512), (1536, 512), (2048, 144)]
EPS = 1e-6
PAST = 2048
NEG = -30000.0
NCONST = 2560


class Buf:
    __slots__ = ("name", "lw", "lw_eng", "rd")

    def __init__(self, name):
        self.name = name
        self.lw = None
        self.lw_eng = None
        self.rd = {}


class Prog:
    def __init__(self, nc, stack, n_dma_sems=24, epoch=1000):
        self.nc = nc
        self.stack = stack
        self.ops = {e: [] for e in ENGS}
        self.sig = {e: 0 for e in ENGS}
        self.ep = {e: 0 for e in ENGS}
        self.sems = {}
        self.waited = {e: {} for e in ENGS}
        self.epoch = epoch
        self.n_dma = n_dma_sems
        self.dma_cnt = [0] * n_dma_sems
        self.dma_last = [None] * n_dma_sems
        self.dma_rr = 0
        self.nops = 0
        self.old = []
        self.async_idx = set()

    def _sem(self, name):
        if name not in self.sems:
            self.sems[name] = self.stack.enter_context(self.nc.semaphore(name))
        return name

    def _cursem(self, eng):
        if self.sig[eng] >= self.epoch:
            self.old.append((f"s_{eng}_{self.ep[eng]}", self.sig[eng]))
            self.ep[eng] += 1
            self.sig[eng] = 0
        return self._sem(f"s_{eng}_{self.ep[eng]}")

    def _wait(self, eng, tok):
        sem, val = tok
        if self.waited[eng].get(sem, 0) >= val:
            return
        self.waited[eng][sem] = val
        self.ops[eng].append(("wait", sem, val))

    def _deps(self, eng, me, reads, writes):
        for b in reads:
            if b.lw is not None:
                self._wait(eng, b.lw)
        for b in writes:
            if b.lw is not None and b.lw_eng != me:
                self._wait(eng, b.lw)
            for teng, tok in b.rd.items():
                if teng != me:
                    self._wait(eng, tok)

    def _post(self, me, tok, reads, writes):
        for b in reads:
            b.rd[me] = tok
        for b in writes:
            b.lw = tok
            b.lw_eng = me
            b.rd = {}

    def op(self, eng, fn, reads=(), writes=(), signal=True):
        mysem = self._cursem(eng)
        self._deps(eng, eng, reads, writes)
        tok = (mysem, self.sig[eng] + 1)
        if signal:
            self.sig[eng] += 1
        self.ops[eng].append(("op", fn, mysem if signal else None, 1))
        self._post(eng, tok, reads, writes)
        self.nops += 1
        return tok

    def dma(self, q, fn, reads=(), writes=(), inc=16, asyn=False):
        i = self.dma_rr
        self.dma_rr = (i + 1) % self.n_dma
        sem = self._sem(f"s_dma_{i}")
        me = f"dma{i}"
        if self.dma_last[i] is not None:
            self._wait(q, self.dma_last[i])
        self._deps(q, me, reads, writes)
        self.dma_cnt[i] += inc
        tok = (sem, self.dma_cnt[i])
        self.dma_last[i] = tok
        if asyn:
            self.async_idx.add(i)
        else:
            self.async_idx.discard(i)
        self.ops[q].append(("op", fn, sem, inc))
        self._post(me, tok, reads, writes)
        self.nops += 1
        return tok

    def barrier(self, skip_async=False):
        toks = list(self.old)
        for e in ENGS:
            if self.sig[e] > 0:
                toks.append((f"s_{e}_{self.ep[e]}", self.sig[e]))
        for i in range(self.n_dma):
            if self.dma_last[i] is not None and not (skip_async and i in self.async_idx):
                toks.append(self.dma_last[i])
        for e in ENGS:
            for t in toks:
                self._wait(e, t)

    def emit(self):
        nc = self.nc
        with nc.Block() as block:
            for e in ENGS:
                items = self.ops[e]

                def body(engh, items=items):
                    for it in items:
                        if it[0] == "wait":
                            engh.wait_ge(self.sems[it[1]], it[2])
                        else:
                            ins = it[1](engh)
                            if it[2] is not None:
                                ins.then_inc(self.sems[it[2]], it[3])

                getattr(block, e)(body)
        self.ops = {e: [] for e in ENGS}


class K:
    pass


_UID = [0]


def _sb(k, st, name, shape, dt):
    _UID[0] += 1
    name = f"{name}_{_UID[0]}"
    t = st.enter_context(k.nc.sbuf_tensor(name, shape, dt))
    return t, Buf(name)


def mm(k, out, lhsT, rhs, start, stop, reads, writes, signal=True):
    k.P.op("tensor", lambda e: e.matmul(out, lhsT=lhsT, rhs=rhs, start=start, stop=stop),
           reads=reads, writes=writes, signal=signal)


def act(k, out, in_, func, reads, writes, scale=None, bias=None, eng="scalar"):
    kw = {}
    if scale is not None:
        kw["scale"] = scale
    if bias is not None:
        kw["bias"] = bias
    k.P.op("scalar", lambda e: e.activation(out=out, in_=in_, func=func, **kw), reads=reads, writes=writes)


def tt(k, out, in0, in1, op, reads, writes, eng="vector"):
    k.P.op(eng, lambda e: e.tensor_tensor(out=out, in0=in0, in1=in1, op=op), reads=reads, writes=writes)


def ts(k, out, in0, s1, s2, op0, op1, reads, writes, eng="vector"):
    if op1 is None:
        k.P.op(eng, lambda e: e.tensor_scalar(out=out, in0=in0, scalar1=s1, scalar2=None, op0=op0),
               reads=reads, writes=writes)
    else:
        k.P.op(eng, lambda e: e.tensor_scalar(out=out, in0=in0, scalar1=s1, scalar2=s2, op0=op0, op1=op1),
               reads=reads, writes=writes)


def stt(k, out, in0, scalar, in1, op0, op1, reads, writes):
    k.P.op("vector", lambda e: e.scalar_tensor_tensor(out=out, in0=in0, scalar=scalar, in1=in1, op0=op0, op1=op1),
           reads=reads, writes=writes)


def cp(k, out, in_, reads, writes, eng="vector"):
    k.P.op(eng, lambda e: e.tensor_copy(out=out, in_=in_), reads=reads, writes=writes)


def dma(k, out, in_, reads, writes, q="sync", asyn=False):
    k.P.dma(q, lambda e: e.dma_start(out=out, in_=in_), reads=reads, writes=writes, asyn=asyn)


def next_ps(k):
    i = k.ps_rr
    k.ps_rr = (i + 1) % k.ps_n
    return k.ps[i], k.psb[i]


def rms_rstd(k, src, Bsrc, nchunk, nfeat, rstd, Brstd, sq, Bsq, col0=0, tiles=TILES):
    for (t0, tn) in tiles:
        ps, Bps = next_ps(k)
        for c in range(nchunk):
            s, Bs = sq[c % 2], Bsq[c % 2]
            sb16 = s[:, :].bitcast(BF16)
            act(k, sb16[:, :tn], src[:, c, col0 + t0:col0 + t0 + tn], AF.Square, [Bsrc], [Bs])
            mm(k, ps[:, :tn], k.ones_b[:, :], sb16[:, :tn], c == 0, c == nchunk - 1, [Bs, k.Bconst], [Bps],
               signal=True)
        ts(k, rstd[:, t0:t0 + tn], ps[:, :tn], 1.0 / nfeat, EPS, ALU.mult, ALU.add, [Bps], [Brstd])
        act(k, rstd[:, t0:t0 + tn], rstd[:, t0:t0 + tn], AF.Ln, [Brstd], [Brstd])
        act(k, rstd[:, t0:t0 + tn], rstd[:, t0:t0 + tn], AF.Exp, [Brstd], [Brstd], scale=-0.5)


def make_xn(k, nw, rstd, Brstd, sq, Bsq):
    for ti, (t0, tn) in enumerate(TILES):
        rms_rstd(k, k.h, k.Bh, 8, D, rstd, Brstd, sq, Bsq, tiles=[(t0, tn)])
        for c in range(8):
            stt(k, k.xn[:, c, t0:t0 + tn], k.h[:, c, t0:t0 + tn], nw[:, c:c + 1], rstd[:, t0:t0 + tn],
                ALU.mult, ALU.mult, [k.Bh, Brstd, k.Bconst], [k.Bxn, k.Bxnt[ti]])


def ffn_phase(k, li, which, final=False):
    P = k.P
    wg_d = k.din[f"ffn{which}_w_gate"][li].rearrange("(kc p) f -> p kc f", p=128)
    wu_d = k.din[f"ffn{which}_w_up"][li].rearrange("(kc p) f -> p kc f", p=128)
    wd_d = k.din[f"ffn{which}_w_down"][li].rearrange("(fc p) d -> p fc d", p=128)
    nw = k.normw[f"ffn{which}_{li}"]
    with ExitStack() as st:
        wg = [_sb(k, st, f"wg{i}", [128, 8, 512], BF16) for i in range(2)]
        wu = [_sb(k, st, f"wu{i}", [128, 8, 512], BF16) for i in range(2)]
        wd = [_sb(k, st, f"wd{i}", [128, 4, D], BF16) for i in range(2)]
        a, Ba = _sb(k, st, "ffn_a", [128, 4, T], BF16)
        rstd, Brstd = _sb(k, st, "ffn_rstd", [128, T], F32)
        sq0, Bsq0 = _sb(k, st, "ffn_sq0", [128, 512], F32)
        sq1, Bsq1 = _sb(k, st, "ffn_sq1", [128, 512], F32)
        sg = [_sb(k, st, f"ffn_sg{i}", [128, 512], F32) for i in range(2)]
        groups = [(0, 4), (4, 4), (8, 4), (12, 4), (16, 4), (20, 2)]

        def load(gi):
            f0, nf = groups[gi]
            b = gi % 2
            dma(k, wg[b][0][:, :, 0:nf * 128], wg_d[:, :, f0 * 128:(f0 + nf) * 128], [k.Bw], [wg[b][1]], q="gpsimd")
            dma(k, wu[b][0][:, :, 0:nf * 128], wu_d[:, :, f0 * 128:(f0 + nf) * 128], [k.Bw], [wu[b][1]], q="gpsimd")
            dma(k, wd[b][0][:, 0:nf, :], wd_d[:, f0:f0 + nf, :], [k.Bw], [wd[b][1]], q="gpsimd")

        load(0)
        make_xn(k, nw, rstd, Brstd, [sq0, sq1], [Bsq0, Bsq1])
        for gi, (f0, nf) in enumerate(groups):
            if gi + 1 < len(groups):
                load(gi + 1)
            b = gi % 2
            for fc in range(nf):
                for ti, (t0, tn) in enumerate(TILES):
                    pg, Bpg = next_ps(k)
                    pu, Bpu = next_ps(k)
                    for kc in range(8):
                        mm(k, pg[:, :tn], wg[b][0][:, kc, fc * 128:(fc + 1) * 128], k.xn[:, kc, t0:t0 + tn],
                           kc == 0, kc == 7, [wg[b][1], k.Bxnt[ti]], [Bpg], signal=(kc == 7))
                    for kc in range(8):
                        mm(k, pu[:, :tn], wu[b][0][:, kc, fc * 128:(fc + 1) * 128], k.xn[:, kc, t0:t0 + tn],
                           kc == 0, kc == 7, [wu[b][1], k.Bxnt[ti]], [Bpu], signal=(kc == 7))
                    s, Bs = sg[ti % 2]
                    act(k, s[:, :tn], pg[:, :tn], AF.Silu, [Bpg], [Bs])
                    tt(k, a[:, fc, t0:t0 + tn], s[:, :tn], pu[:, :tn], ALU.mult, [Bs, Bpu], [Ba])
            if final and gi == len(groups) - 1:
                fnw = k.normw["final"]
                yi = [0]

                def down_tile(t0, tn):
                    for d in range(8):
                        pd, Bpd = next_ps(k)
                        for fc in range(nf):
                            mm(k, pd[:, :tn], wd[b][0][:, fc, d * 128:(d + 1) * 128], a[:, fc, t0:t0 + tn],
                               fc == 0, fc == nf - 1, [wd[b][1], Ba], [Bpd], signal=(fc == nf - 1))
                        stt(k, k.h[:, d, t0:t0 + tn], pd[:, :tn], 0.5, k.h[:, d, t0:t0 + tn], ALU.mult, ALU.add,
                            [Bpd, k.Bh], [k.Bh])

                def norm_out(t0, tn):
                    rms_rstd(k, k.h, k.Bh, 8, D, rstd, Brstd, [sq0, sq1], [Bsq0, Bsq1], tiles=[(t0, tn)])
                    for c in range(8):
                        y, By = sg[yi[0] % 2]
                        yi[0] += 1
                        stt(k, y[:, :tn], k.h[:, c, t0:t0 + tn], fnw[:, c:c + 1], rstd[:, t0:t0 + tn], ALU.mult,
                            ALU.mult, [k.Bh, Brstd, k.Bconst], [By])
                        dma(k, k.dout["o_y"][:, c, t0:t0 + tn], y[:, :tn], [By], [k.Bw],
                            q="sync" if yi[0] % 2 else "gpsimd")

                down_tile(*TILES[0])
                for ti2, (t0, tn) in enumerate(TILES):
                    if ti2 + 1 < len(TILES):
                        down_tile(*TILES[ti2 + 1])
                    norm_out(t0, tn)
                continue
            for d in range(8):
                for (t0, tn) in TILES:
                    pd, Bpd = next_ps(k)
                    for fc in range(nf):
                        mm(k, pd[:, :tn], wd[b][0][:, fc, d * 128:(d + 1) * 128], a[:, fc, t0:t0 + tn],
                           fc == 0, fc == nf - 1, [wd[b][1], Ba], [Bpd], signal=(fc == nf - 1))
                    stt(k, k.h[:, d, t0:t0 + tn], pd[:, :tn], 0.5, k.h[:, d, t0:t0 + tn], ALU.mult, ALU.add,
                        [Bpd, k.Bh], [k.Bh])
        P.barrier()
        P.emit()


class NS:
    pass


def ps_view4(ps):
    return ps[:, :].rearrange("p (h n) -> p h n", h=4)


def gla_prep(k, W, Z, c0, n):
    xs = lambda kc: k.xn[:, kc, c0:c0 + n]
    ps, Bps = next_ps(k)
    for kc in range(8):
        mm(k, ps[0:16, :n], W.wgd[:, kc, :], xs(kc), kc == 0, kc == 7, [W.Bwgd, k.Bxn], [Bps], signal=(kc == 7))
    cp(k, Z.gd[0:16, :n], ps[0:16, :n], [Bps], [Z.Bgd])
    for half in range(2):
        psv, Bpsv = next_ps(k)
        for kc in range(8):
            mm(k, psv[:n, :], xs(kc), W.wv[:, kc, half * 512:(half + 1) * 512], kc == 0, kc == 7,
               [W.Bwv, k.Bxn], [Bpsv], signal=(kc == 7))
        act(k, Z.vt[:n, half * 512:(half + 1) * 512], psv[:n, :], AF.Copy, [Bpsv], [Z.Bvt])
    ps2, Bps2 = next_ps(k)
    mm(k, ps2[:n, :], Z.gd[0:17, :n], W.wgu[0:17, :], True, True, [Z.Bgd, W.Bwgu], [Bps2])
    act(k, Z.graw[:n, :], ps2[:n, :], AF.Exp, [Bps2], [Z.Bgraw], scale=-1.0)
    act(k, Z.graw[:n, :], Z.graw[:n, :], AF.Ln, [Z.Bgraw], [Z.Bgraw], bias=1.0)
    psq, Bpsq = next_ps(k)
    psq_v = ps_view4(psq)
    for h in range(4):
        for kc in range(8):
            mm(k, psq_v[:, h, :n], W.wq[:, kc, h * 128:(h + 1) * 128], xs(kc), kc == 0, kc == 7,
               [W.Bwq, k.Bxn], [Bpsq], signal=(kc == 7 and h == 3))
    psk, Bpsk = next_ps(k)
    psk_v = ps_view4(psk)
    for h in range(4):
        for kc in range(8):
            mm(k, psk_v[:, h, :n], W.wk[:, kc, h * 128:(h + 1) * 128], xs(kc), kc == 0, kc == 7,
               [W.Bwk, k.Bxn], [Bpsk], signal=(kc == 7 and h == 3))
    psg, Bpsg = next_ps(k)
    psg_v = ps_view4(psg)
    for h in range(4):
        mm(k, psg_v[:, h, :n], Z.graw[:n, h * 128:(h + 1) * 128], k.TRI[64][:n, :n], True, True,
           [Z.Bgraw, k.Bconst], [Bpsg], signal=(h == 3))
    psw, Bpsw = next_ps(k)
    mm(k, psw[:n, :], k.REV[64][:n, :n], Z.graw[:n, :], True, True, [Z.Bgraw, k.Bconst], [Bpsw])
    act(k, Z.egc[:, :, :n], psg_v[:, :, :n], AF.Exp, [Bpsg], [Z.Begc])
    act(k, Z.engc[:, :, :n], psg_v[:, :, :n], AF.Exp, [Bpsg], [Z.Bengc], scale=-1.0)
    act(k, Z.eW[:n, :], psw[:n, :], AF.Exp, [Bpsw], [Z.BeW])
    pskt, Bpskt = next_ps(k)
    for kc in range(8):
        mm(k, pskt[:n, :], xs(kc), W.wk[:, kc, :], kc == 0, kc == 7, [W.Bwk, k.Bxn], [Bpskt], signal=(kc == 7))
    stt(k, Z.qg[:, :, :n], psq_v[:, :, :n], 128.0 ** -0.5, Z.egc[:, :, :n], ALU.mult, ALU.mult,
        [Bpsq, Z.Begc], [Z.Bqg])
    tt(k, Z.kg[:, :, :n], psk_v[:, :, :n], Z.engc[:, :, :n], ALU.mult, [Bpsk, Z.Bengc], [Z.Bkg])
    tt(k, Z.kd[:n, :], pskt[:n, :], Z.eW[:n, :], ALU.mult, [Bpskt, Z.BeW], [Z.Bkd])
    psa, Bpsa = next_ps(k)
    psa_v = ps_view4(psa)
    for h in range(4):
        mm(k, psa_v[:n, h, :n], Z.kg[:, h, :n], Z.qg[:, h, :n], True, True, [Z.Bkg, Z.Bqg], [Bpsa],
           signal=(h == 3))
    tt(k, Z.atm[:n, :, :n], psa_v[:n, :, :n], k.MASK4[64][:n, :, :n], ALU.mult, [Bpsa, k.Bconst], [Z.Batm])


def gla_blocks(k, W, Z, c0, n, nb, L, o_dst, Bo, ecum_b0=None):
    xs = None
    pso = [ps_view4(k.pso[0]), ps_view4(k.pso[1])]
    for bl in range(nb):
        b0 = bl * L
        for h in range(4):
            for vc in range(2):
                c = h * 2 + vc
                out = pso[c // 4][:, c % 4, b0:b0 + L]
                mm(k, out, W.Sbf[:, h, vc * 128:(vc + 1) * 128], Z.qg[:, h, b0:b0 + L], True, False,
                   [W.BSbf[h], Z.Bqg], [k.Bpso[c // 4]], signal=False)
                mm(k, out, Z.vt[:n, h * 256 + vc * 128:h * 256 + (vc + 1) * 128], Z.atm[:n, h, b0:b0 + L],
                   False, True, [Z.Bvt, Z.Batm], [k.Bpso[c // 4]], signal=True)
            pss, Bpss = next_ps(k)
            mm(k, pss[:, 0:256], Z.kd[b0:b0 + L, h * 128:(h + 1) * 128], Z.vt[b0:b0 + L, h * 256:(h + 1) * 256],
               True, True, [Z.Bkd, Z.Bvt], [Bpss])
            stt(k, W.S[:, h, :], W.S[:, h, :], Z.egc[:, h, b0 + L - 1:b0 + L], pss[:, 0:256], ALU.mult, ALU.add,
                [W.BS[h], Z.Begc, Bpss], [W.BS[h]])
            act(k, W.Sbf[:, h, :], W.S[:, h, :], AF.Copy, [W.BS[h]], [W.BSbf[h]])
            if ecum_b0 is not None:
                gb = ecum_b0 + bl
                ts(k, Z.qt[:, h, b0:b0 + L], Z.qg[:, h, b0:b0 + L], W.ecum[:, gb, h:h + 1], None,
                   ALU.mult, None, [Z.Bqg, W.Becum], [Z.Bqt])
        if ecum_b0 is not None:
            gb = ecum_b0 + bl
            tt(k, W.ecum[:, gb + 1, :], W.ecum[:, gb, :], Z.egc[:, :, b0 + L - 1], ALU.mult,
               [W.Becum, Z.Begc], [W.Becum])
    for half in range(2):
        cp(k, o_dst[:, half * 4:(half + 1) * 4, :n], pso[half][:, :, :n], [k.Bpso[half]], [Bo])
    if ecum_b0 is not None:
        dma(k, k.qtl_scr[:, :, c0:c0 + n], Z.qt[:, :, :n], [Z.Bqt], [k.Bqtl], q="sync")


def gla_subtile(k, W, c0, n, nb, L, o_dst, Bo, ecum_b0=None):
    W.par ^= 1
    Z = W.sets[W.par]
    gla_prep(k, W, Z, c0, n)
    gla_blocks(k, W, Z, c0, n, nb, L, o_dst, Bo, ecum_b0)


def gla_post_r(k, W, c0, n, sr8, Bsr8):
    for c in range(8):
        psr, Bpsr = next_ps(k)
        for kc in range(8):
            mm(k, psr[:, :n], W.wr[:, kc, c * 128:(c + 1) * 128], k.xn[:, kc, c0:c0 + n], kc == 0, kc == 7,
               [W.Bwr, k.Bxn], [Bpsr], signal=(kc == 7))
        act(k, sr8[:, c, :n], psr[:, :n], AF.Silu, [Bpsr], [Bsr8])


def gla_post_norm(k, W, o, Bo, n, sr8, Bsr8):
    for c in range(8):
        act(k, W.sq8[:, c, :n], o[:, c, :n], AF.Square, [Bo], [W.Bsq8])
    for j in range(2):
        psn, Bpsn = next_ps(k)
        psn_v = psn[:, :].rearrange("p (h n) -> p h n", h=2)
        for hh in range(2):
            h = 2 * j + hh
            for vc in range(2):
                mm(k, psn_v[:, hh, :n], k.ones_b[:, :], W.sq8[:, h * 2 + vc, :n], vc == 0, vc == 1,
                   [W.Bsq8, k.Bconst], [Bpsn], signal=(vc == 1 and hh == 1))
        ts(k, W.rstd[:, 2 * j:2 * j + 2, :n], psn_v[:, :, :n], 1.0 / 256.0, EPS, ALU.mult, ALU.add, [Bpsn], [W.Brstd1])
    act(k, W.rstd[:, :, :n], W.rstd[:, :, :n], AF.Ln, [W.Brstd1], [W.Brstd1])
    act(k, W.rstd[:, :, :n], W.rstd[:, :, :n], AF.Exp, [W.Brstd1], [W.Brstd1], scale=-0.5)
    for c in range(8):
        h, vc = c // 2, c % 2
        stt(k, W.tmp[:, :n], o[:, c, :n], k.hnorm[:, vc:vc + 1], W.rstd[:, h, :n], ALU.mult, ALU.mult,
            [Bo, W.Brstd1, k.Bconst], [W.Btmp])
        tt(k, W.y[:, c, :n], W.tmp[:, :n], sr8[:, c, :n], ALU.mult, [W.Btmp, Bsr8], [W.By])


def gla_post_out(k, W, c0, n):
    for d in range(8):
        psd, Bpsd = next_ps(k)
        for c in range(8):
            mm(k, psd[:, :n], W.wout[:, c, d * 128:(d + 1) * 128], W.y[:, c, :n], c == 0, c == 7,
               [W.Bwout, W.By], [Bpsd], signal=(c == 7))
        tt(k, k.h[:, d, c0:c0 + n], k.h[:, d, c0:c0 + n], psd[:, :n], ALU.add, [k.Bh, Bpsd], [k.Bh])


def gla_phase(k):
    P = k.P
    nc = k.nc
    win = k.din["gla_w_in"][0].rearrange("(kc p) f -> p kc f", p=128)
    wout_d = k.din["gla_w_out"][0].rearrange("(kc p) d -> p kc d", p=128)
    o_scr = nc.dram_tensor("gla_o_scr", [128, 8, NF], F32, kind="Internal").ap()
    cc_in = nc.dram_tensor("gla_cc_in", [128, 1024], F32, kind="Internal").ap()
    cc_out = nc.dram_tensor("gla_cc_out", [256, 1024], F32, kind="Internal").ap()
    Bscr, Bccin, Bccout = Buf("o_scr"), Buf("ccin"), Buf("ccout")
    k.ps_n = 6
    k.ps_rr = 0
    k.pso = [k.ps[6], k.ps[7]]
    k.Bpso = [k.psb[6], k.psb[7]]
    with ExitStack() as st:
        k.qtl_scr = nc.dram_tensor("gla_qtl_scr", [128, 4, NF], BF16, kind="Internal").ap()
        k.Bqtl = Buf("qtl_scr")
        osm, Bosm = _sb(k, st, "g_osm", [128, 8, 144], F32)
        W = NS()
        W.S, _ = _sb(k, st, "g_S", [128, 4, 256], F32)
        W.Sbf, _ = _sb(k, st, "g_Sbf", [128, 4, 256], BF16)
        W.BS = [Buf(f"S{h}") for h in range(4)]
        W.BSbf = [Buf(f"Sbf{h}") for h in range(4)]
        W.ecum, W.Becum = _sb(k, st, "g_ecum", [128, 33, 4], F32)
        Sloc, BSloc = _sb(k, st, "g_Sloc", [128, 4, 256], F32)
        with ExitStack() as s0:
            rstd, Brstd = _sb(k, s0, "g_rstd", [128, T], F32)
            sq0, Bsq0 = _sb(k, s0, "g_sq0", [128, 512], F32)
            sq1, Bsq1 = _sb(k, s0, "g_sq1", [128, 512], F32)
            make_xn(k, k.normw["mix_0"], rstd, Brstd, [sq0, sq1], [Bsq0, Bsq1])
            P.barrier()
            P.emit()
        with ExitStack() as p1:
            W.wq, W.Bwq = _sb(k, p1, "g_wq", [128, 8, 512], BF16)
            W.wk, W.Bwk = _sb(k, p1, "g_wk", [128, 8, 512], BF16)
            W.wv, W.Bwv = _sb(k, p1, "g_wv", [128, 8, 1024], BF16)
            W.wgd, W.Bwgd = _sb(k, p1, "g_wgd", [128, 8, 16], BF16)
            W.wgu, W.Bwgu = _sb(k, p1, "g_wgu", [32, 512], F32)
            W.par = 0
            W.sets = []
            for i in range(2):
                Z = NS()
                Z.gd, Z.Bgd = _sb(k, p1, f"g_gd{i}", [32, 128], F32)
                Z.graw, Z.Bgraw = _sb(k, p1, f"g_graw{i}", [128, 512], F32)
                Z.eW, Z.BeW = _sb(k, p1, f"g_eW{i}", [128, 512], F32)
                Z.egc, Z.Begc = _sb(k, p1, f"g_egc{i}", [128, 4, 128], F32)
                Z.engc, Z.Bengc = _sb(k, p1, f"g_engc{i}", [128, 4, 128], F32)
                Z.qg, Z.Bqg = _sb(k, p1, f"g_qg{i}", [128, 4, 128], BF16)
                Z.kg, Z.Bkg = _sb(k, p1, f"g_kg{i}", [128, 4, 128], BF16)
                Z.kd, Z.Bkd = _sb(k, p1, f"g_kd{i}", [128, 512], BF16)
                Z.vt, Z.Bvt = _sb(k, p1, f"g_vt{i}", [128, 1024], BF16)
                Z.atm, Z.Batm = _sb(k, p1, f"g_atm{i}", [128, 4, 128], BF16)
                Z.qt, Z.Bqt = _sb(k, p1, f"g_qt{i}", [128, 4, 128], BF16)
                W.sets.append(Z)
            osb = [_sb(k, p1, f"g_osb{i}", [128, 8, 128], F32) for i in range(2)]
            dma(k, W.wq[:, :, :], win[:, :, 0:512], [k.Bw], [W.Bwq], q="gpsimd")
            dma(k, W.wk[:, :, :], win[:, :, 512:1024], [k.Bw], [W.Bwk], q="gpsimd")
            dma(k, W.wv[:, :, :], win[:, :, 1024:2048], [k.Bw], [W.Bwv], q="gpsimd")
            dma(k, W.wgd[:, :, :], win[:, :, 3072:3088], [k.Bw], [W.Bwgd], q="gpsimd")
            dma(k, W.wgu[0:16, :], k.din["gla_w_gate_up"][0], [k.Bw], [W.Bwgu], q="sync")
            dma(k, W.wgu[16:17, :], k.din["gla_b_gate"][0:1, :], [k.Bw], [W.Bwgu], q="sync")
            for Z in W.sets:
                P.op("vector", lambda e, Z=Z: e.memset(Z.gd[:, :], 1.0), writes=[Z.Bgd])
            P.op("vector", lambda e: e.memset(W.S[:, :, :], 0.0), writes=W.BS)
            P.op("vector", lambda e: e.memset(W.Sbf[:, :, :], 0.0), writes=W.BSbf)
            P.op("vector", lambda e: e.memset(W.ecum[:, :, :], 1.0), writes=[W.Becum])
            gla_subtile(k, W, C_META, NMETA, 1, NMETA, osm[:, :, 0:NMETA], Bosm)
            for h in range(4):
                ts(k, W.S[:, h, :], W.S[:, h, :], k.fA, None, ALU.mult, None, [W.BS[h], k.Bconst], [W.BS[h]])
                act(k, W.Sbf[:, h, :], W.S[:, h, :], AF.Copy, [W.BS[h]], [W.BSbf[h]])
            for sti in range(16):
                ob, Bob = osb[sti % 2]
                gla_subtile(k, W, sti * 128, 128, 2, 64, ob, Bob, ecum_b0=2 * sti)
                dma(k, o_scr[:, :, sti * 128:(sti + 1) * 128], ob[:, :, :], [Bob], [Bscr], q="sync")
            for h in range(4):
                dma(k, cc_in[:, h * 256:(h + 1) * 256], W.S[:, h, :], [W.BS[h]], [Bccin], q="gpsimd")
            P.dma("gpsimd", lambda e: e.collective_compute("AllGather", ALU.bypass,
                                                           replica_groups=[[0, 1], [2, 3], [4, 5], [6, 7]],
                                                           ins=[cc_in], outs=[cc_out]),
                  reads=[Bccin], writes=[Bccout], inc=1)
            for h in range(4):
                cp(k, Sloc[:, h, :], W.S[:, h, :], [W.BS[h]], [BSloc])
            for sq_i in range(4):
                for h in range(4):
                    dma(k, W.S[:, h, :], k.din["state"][sq_i, h], [k.Bw], [W.BS[h]], q="sync")
                    act(k, W.Sbf[:, h, :], W.S[:, h, :], AF.Copy, [W.BS[h]], [W.BSbf[h]])
                cs = NMETA + 32 * sq_i
                gla_subtile(k, W, C_SAMP + 32 * sq_i, 32, 1, 32, osm[:, :, cs:cs + 32], Bosm)
                for h in range(4):
                    dma(k, k.dout["o_ss"][sq_i, h], W.S[:, h, :], [W.BS[h]], [k.Bw], q="sync")
            P.barrier()
            P.emit()
        with ExitStack() as p2:
            W2 = NS()
            W2.wr, W2.Bwr = _sb(k, p2, "g_wr", [128, 8, 1024], BF16)
            W2.wout, W2.Bwout = _sb(k, p2, "g_wout", [128, 8, 1024], BF16)
            W2.rstd, _ = _sb(k, p2, "g2_rstd", [128, 4, 256], F32)
            W2.Brstd = [Buf(f"g2_rstd{h}") for h in range(4)]
            W2.tmp, W2.Btmp = _sb(k, p2, "g2_tmp", [128, 256], F32)
            W2.y, W2.By = _sb(k, p2, "g2_y", [128, 8, 256], BF16)
            W2.sq8, W2.Bsq8 = _sb(k, p2, "g2_sq8", [128, 8, 256], BF16)
            W2.Brstd1 = Buf("g2_rstd_all")
            sr8s = [_sb(k, p2, f"g2_sr8{i}", [128, 8, 256], BF16) for i in range(2)]
            o2 = [_sb(k, p2, f"g2_o{i}", [128, 8, 256], F32) for i in range(2)]
            qts = [_sb(k, p2, f"g2_qtl{i}", [128, 4, 256], BF16) for i in range(2)]
            Srv, BSrv = _sb(k, p2, "g2_Srv", [128, 4, 256], F32)
            Srb, BSrb = _sb(k, p2, "g2_Srb", [128, 4, 256], BF16)
            dma(k, W2.wr[:, :, :], win[:, :, 2048:3072], [k.Bw], [W2.Bwr], q="gpsimd")
            dma(k, W2.wout[:, :, :], wout_d, [k.Bw], [W2.Bwout], q="gpsimd")
            dma(k, Srv[:, :, :], cc_out[0:128, :].rearrange("p (h v) -> p h v", h=4), [Bccout], [BSrv], q="sync")
            ts(k, Srv[:, :, :], Srv[:, :, :], k.fB, None, ALU.mult, None, [BSrv, k.Bconst], [BSrv])
            cp(k, Srb[:, :, :], Srv[:, :, :], [BSrv], [BSrb])
            for h in range(4):
                stt(k, Srv[:, h, :], Srv[:, h, :], W.ecum[:, 32, h:h + 1], Sloc[:, h, :], ALU.mult, ALU.add,
                    [BSrv, W.Becum, BSloc], [BSrv])
            dma(k, k.dout["o_sp"].rearrange("h p v -> p h v"), Srv[:, :, :], [BSrv], [k.Bw], q="sync")
            tiles2 = [(C_META, 144, False)] + [(ti * 256, 256, True) for ti in range(8)]

            def load2(i):
                c0, n, corr = tiles2[i]
                if corr:
                    ob, Bob = o2[i % 2]
                    qt, Bqt = qts[i % 2]
                    dma(k, ob[:, :, :], o_scr[:, :, c0:c0 + 256], [Bscr], [Bob], q="sync")
                    dma(k, qt[:, :, :], k.qtl_scr[:, :, c0:c0 + 256], [k.Bqtl], [Bqt], q="sync")

            load2(0)
            gla_post_r(k, W2, tiles2[0][0], tiles2[0][1], *sr8s[0])
            for i, (c0, n, corr) in enumerate(tiles2):
                if i + 1 < len(tiles2):
                    load2(i + 1)
                if corr:
                    ob, Bob = o2[i % 2]
                    qt, Bqt = qts[i % 2]
                    for h in range(4):
                        for vc in range(2):
                            c = h * 2 + vc
                            psc, Bpsc = next_ps(k)
                            mm(k, psc[:, :256], Srb[:, h, vc * 128:(vc + 1) * 128], qt[:, h, :], True, True,
                               [BSrb, Bqt], [Bpsc])
                            tt(k, ob[:, c, :], ob[:, c, :], psc[:, :256], ALU.add, [Bob, Bpsc], [Bob])
                else:
                    ob, Bob = osm, Bosm
                gla_post_norm(k, W2, ob, Bob, n, *sr8s[i % 2])
                if i + 1 < len(tiles2):
                    gla_post_r(k, W2, tiles2[i + 1][0], tiles2[i + 1][1], *sr8s[(i + 1) % 2])
                gla_post_out(k, W2, c0, n)
            P.barrier()
            P.emit()
    k.ps_n = 8
    k.ps_rr = 0


SCALE = 192.0 ** -0.5


def attend(k, M, qlat, qr, nq, keysets, olat, Bolat, Bq, poset):
    po0, po1, pden, Bpo = poset
    nks = len(keysets)

    def scores(i):
        (klT, krT, ktok, nk, bias, q0, maskfix, Bk) = keysets[i]
        pS, BpS = next_ps(k)
        q_0, q_1, q_r = qlat(0), qlat(1), qr
        if q0 > 0:
            q_0, q_1, q_r = q_0[:, q0:nq], q_1[:, q0:nq], q_r[:, q0:nq]
        mm(k, pS[:nk, q0:nq], klT(0), q_0, True, False, Bk + Bq, [BpS], signal=False)
        mm(k, pS[:nk, q0:nq], klT(1), q_1, False, False, Bk + Bq, [BpS], signal=False)
        mm(k, pS[:nk, q0:nq], krT, q_r, False, True, Bk + Bq, [BpS], signal=True)
        return pS, BpS

    def finish(i, pS, BpS):
        (klT, krT, ktok, nk, bias, q0, maskfix, Bk) = keysets[i]
        pT, BpT = M.pT[i % 3]
        if bias is None:
            act(k, pT[:nk, q0:nq], pS[:nk, q0:nq], AF.Exp, [BpS], [BpT], scale=SCALE)
        else:
            act(k, pT[:nk, q0:nq], pS[:nk, q0:nq], AF.Exp, [BpS, k.Bconst], [BpT], scale=SCALE, bias=bias)
        if maskfix:
            k.P.op("vector", lambda e, pT=pT, q0=q0: e.memset(pT[64:128, q0:q0 + 64], 0.0), writes=[BpT])
        first, last = (i == 0), (i == nks - 1)
        mm(k, po0[:, q0:nq], ktok[:nk, 0:128], pT[:nk, q0:nq], first, last, Bk + [BpT], [Bpo[0]], signal=False)
        mm(k, po1[:, q0:nq], ktok[:nk, 128:256], pT[:nk, q0:nq], first, last, Bk + [BpT], [Bpo[1]], signal=True)
        acc, Bacc = M.acc[0]
        tt(k, acc[:nk, q0:nq], acc[:nk, q0:nq], pT[:nk, q0:nq], ALU.add, [Bacc, BpT], [Bacc],
           eng="vector")

    k.P.op("vector", lambda e: e.memset(M.acc[0][0][:, :nq], 0.0), writes=[M.acc[0][1]])
    LOOK = 2
    pend = [scores(i) for i in range(min(LOOK, nks))]
    for i in range(nks):
        if i + LOOK < nks:
            pend.append(scores(i + LOOK))
        finish(i, *pend.pop(0))
    mm(k, pden[:, :nq], k.ones_f[:, :], M.acc[0][0][:, :nq], True, True, [M.acc[0][1], k.Bconst], [Bpo[2]], signal=True)
    act(k, M.oraw[:, 0, :nq], po0[:, :nq], AF.Copy, [Bpo[0]], [M.Boraw])
    act(k, M.oraw[:, 1, :nq], po1[:, :nq], AF.Copy, [Bpo[1]], [M.Boraw])
    act(k, M.rden[:, :nq], pden[:, :nq], AF.Copy, [Bpo[2]], [M.Brden])
    k.P.op("vector", lambda e: e.reciprocal(out=M.rden[:, :nq], in_=M.rden[:, :nq]), reads=[M.Brden], writes=[M.Brden])
    tt(k, olat[:, 0, :nq], M.oraw[:, 0, :nq], M.rden[:, :nq], ALU.mult, [M.Boraw, M.Brden], [Bolat])
    tt(k, olat[:, 1, :nq], M.oraw[:, 1, :nq], M.rden[:, :nq], ALU.mult, [M.Boraw, M.Brden], [Bolat])


def mla_phase(k):
    P = k.P
    nc = k.nc
    wdn_d = k.din["mla_w_down"][0].rearrange("(kc p) f -> p kc f", p=128)
    wuq_d = k.din["mla_w_uq"][0].rearrange("(kc p) f -> p kc f", p=128)
    wuv_d = k.din["mla_w_uv"][0].rearrange("(rc p) h v -> p rc h v", p=128)
    wo_d = k.din["mla_w_out"][0].rearrange("(c p) d -> p c d", p=128)
    h_scr = nc.dram_tensor("mla_h_scr", [128, 8, T], F32, kind="Internal").ap()
    cc_ins = [nc.dram_tensor(f"mla_cc_in{i}", [128, 2560], BF16, kind="Internal").ap() for i in range(4)]
    cc_outs = [nc.dram_tensor(f"mla_cc_out{i}", [256, 2560], BF16, kind="Internal").ap() for i in range(4)]
    Bhs, Bccin, Bccout = Buf("h_scr"), Buf("ccin2"), Buf("ccout2")
    k.ps_n = 5
    k.ps_rr = 0
    with ExitStack() as st:
        M = NS()
        Bprev = Buf("prev")
        kropeT, BkropeT = _sb(k, st, "m_kropeT", [128, T], BF16)
        pkropeT, _ = _sb(k, st, "m_pkropeT", [128, NF], BF16)
        qlat_s, Bqs = _sb(k, st, "m_qlat_s", [128, 2, 4, 8, 32], BF16)
        qr_s, _ = _sb(k, st, "m_qr_s", [128, 4, 8, 32], BF16)
        M.pT = [_sb(k, st, f"m_pT{i}", [128, 512], BF16) for i in range(3)]
        M.oraw, M.Boraw = _sb(k, st, "m_oraw", [128, 2, 512], F32)
        M.acc = [_sb(k, st, f"m_acc{i}", [128, 512], F32) for i in range(2)]
        M.rden, M.Brden = _sb(k, st, "m_rden", [128, 512], F32)
        olat, Bolat = _sb(k, st, "m_olat", [128, 2, 512], BF16)
        olat2, Bolat2 = _sb(k, st, "m_olat2", [128, 2, 512], BF16)
        t1, Bt1 = _sb(k, st, "m_t1", [32, 512], F32)
        t2, Bt2 = _sb(k, st, "m_t2", [32, 512], F32)
        wuq, Bwuq = _sb(k, st, "m_wuq", [128, 3, 1536], BF16)
        wukT, BwukT = _sb(k, st, "m_wukT", [128, 8, 256], BF16)
        wuv, Bwuv = _sb(k, st, "m_wuv", [128, 2, 8, 128], BF16)
        sw = ExitStack()
        wdn, Bwdn = _sb(k, sw, "m_wdn", [128, 8, 704], BF16)
        dma(k, wdn[:, :, :], wdn_d, [k.Bw], [Bwdn], q="gpsimd")
        dma(k, wuq[:, :, :], wuq_d, [k.Bw], [Bwuq], q="gpsimd")
        dma(k, wukT[:, :, :], k.din["w_ukT"], [k.Bw], [BwukT], q="gpsimd")
        dma(k, wuv[:, :, :, :], wuv_d, [k.Bw], [Bwuv], q="gpsimd")
        P.op("vector", lambda e: e.memset(kropeT[64:128, :], 0.0), writes=[BkropeT])
        P.op("vector", lambda e: e.memset(pkropeT[64:128, :], 0.0), writes=[Bprev])
        with ExitStack() as s0:
            rstd, Brstd = _sb(k, s0, "m_rstd", [128, T], F32)
            sq0, Bsq0 = _sb(k, s0, "m_sq0", [128, 512], F32)
            sq1, Bsq1 = _sb(k, s0, "m_sq1", [128, 512], F32)
            for c in range(8):
                dma(k, h_scr[:, c, :], k.h[:, c, :], [k.Bh], [Bhs], q="sync")
            make_xn(k, k.normw["mix_1"], rstd, Brstd, [sq0, sq1], [Bsq0, Bsq1])
            P.barrier()
            P.emit()
        flat = k.h[:, :, :].rearrange("p c t -> p (c t)")
        off = [0]

        def carve(nf32):
            a = flat[:, off[0]:off[0] + nf32]
            off[0] += nf32
            assert off[0] <= 8 * T
            return a

        tab = carve(2 * T).rearrange("p (a t) -> p a t", a=2)
        Btab = Buf("tab")
        klatT = carve(T).bitcast(BF16).rearrange("p (a t) -> p a t", a=2)
        BklatT = Buf("klatT")
        ktok = carve(21 * 128).bitcast(BF16).rearrange("p (a r) -> p a r", a=21)
        Bktok = Buf("ktok")
        cqn = carve(3 * T // 2).bitcast(BF16).rearrange("p (a t) -> p a t", a=3)
        Bcqn = Buf("cqn")
        pklatT = carve(2048).bitcast(BF16).rearrange("p (a t) -> p a t", a=2)
        pktok = carve(2048).bitcast(BF16).rearrange("p (a r) -> p a r", a=16)
        dma(k, tab[0:32, :, :], k.din["rope"], [k.Bw], [Btab], q="sync")
        ov = k.xn
        GROUPS = [(i * 128, 128) for i in range(16)] + [(C_META, 16)] + [(C_SAMP + 32 * i, 32) for i in range(4)]

        def rope(psa, Bpsa, psb, Bpsb, dst, Bdst, t0, tn):
            cos, sin = tab[0:32, 0, t0:t0 + tn], tab[0:32, 1, t0:t0 + tn]
            tt(k, t1[:, :tn], psa[0:32, :tn], cos, ALU.mult, [Bpsa, Btab], [Bt1])
            tt(k, t2[:, :tn], psb[0:32, :tn], sin, ALU.mult, [Bpsb, Btab], [Bt2])
            tt(k, dst[0:32, t0:t0 + tn], t1[:, :tn], t2[:, :tn], ALU.subtract, [Bt1, Bt2], [Bdst])
            tt(k, t1[:, :tn], psb[0:32, :tn], cos, ALU.mult, [Bpsb, Btab], [Bt1])
            tt(k, t2[:, :tn], psa[0:32, :tn], sin, ALU.mult, [Bpsa, Btab], [Bt2])
            tt(k, dst[32:64, t0:t0 + tn], t1[:, :tn], t2[:, :tn], ALU.add, [Bt1, Bt2], [Bdst])

        with ExitStack() as sa:
            cq, Bcq = _sb(k, sa, "m_cq", [128, 3, 512], F32)
            ckv, Bckv = _sb(k, sa, "m_ckv", [128, 2, 512], F32)
            klf, Bklf = _sb(k, sa, "m_klf", [128, 2, 512], F32)
            krf, Bkrf = _sb(k, sa, "m_krf", [64, T], F32)
            sqa = [_sb(k, sa, f"m_sqa{i}", [128, 512], F32) for i in range(2)]
            rs, Brs = _sb(k, sa, "m_rs", [128, 512], F32)
            stg = [_sb(k, sa, f"m_stg{i}", [128, 256], F32) for i in range(2)]
            stgr = [_sb(k, sa, f"m_stgr{i}", [128, 64], F32) for i in range(2)]
            for (t0, tn) in TILES:
                pss = []
                for c in range(5):
                    ps, Bps = next_ps(k)
                    for kc in range(8):
                        mm(k, ps[:, :tn], wdn[:, kc, c * 128:(c + 1) * 128], k.xn[:, kc, t0:t0 + tn], kc == 0, kc == 7,
                           [Bwdn, k.Bxn], [Bps], signal=(kc == 7))
                    pss.append((ps, Bps))
                for c in range(3):
                    act(k, cq[:, c, :tn], pss[c][0][:, :tn], AF.Copy, [pss[c][1]], [Bcq])
                for c in range(2):
                    act(k, ckv[:, c, :tn], pss[3 + c][0][:, :tn], AF.Copy, [pss[3 + c][1]], [Bckv])
                rms_rstd(k, cq, Bcq, 3, 384, rs, Brs, [sqa[0][0], sqa[1][0]], [sqa[0][1], sqa[1][1]],
                         tiles=[(0, tn)])
                psa, Bpsa = next_ps(k)
                psb, Bpsb = next_ps(k)
                for kc in range(8):
                    mm(k, psa[0:32, :tn], wdn[:, kc, 640:672], k.xn[:, kc, t0:t0 + tn], kc == 0, kc == 7,
                       [Bwdn, k.Bxn], [Bpsa], signal=(kc == 7))
                for kc in range(8):
                    mm(k, psb[0:32, :tn], wdn[:, kc, 672:704], k.xn[:, kc, t0:t0 + tn], kc == 0, kc == 7,
                       [Bwdn, k.Bxn], [Bpsb], signal=(kc == 7))
                for c in range(3):
                    stt(k, cqn[:, c, t0:t0 + tn], cq[:, c, :tn], k.qnorm[:, c:c + 1], rs[:, :tn], ALU.mult, ALU.mult,
                        [Bcq, Brs, k.Bconst], [Bcqn])
                rms_rstd(k, ckv, Bckv, 2, 256, rs, Brs, [sqa[0][0], sqa[1][0]], [sqa[0][1], sqa[1][1]],
                         tiles=[(0, tn)])
                for c in range(2):
                    stt(k, klf[:, c, :tn], ckv[:, c, :tn], k.kvnorm[:, c:c + 1], rs[:, :tn], ALU.mult, ALU.mult,
                        [Bckv, Brs, k.Bconst], [Bklf])
                    cp(k, klatT[:, c, t0:t0 + tn], klf[:, c, :tn], [Bklf], [BklatT])
                rope(psa, Bpsa, psb, Bpsb, krf, Bkrf, t0, tn)
                cp(k, kropeT[0:64, t0:t0 + tn], krf[:, t0:t0 + tn], [Bkrf], [BkropeT])
                for gi, (g0, gn) in enumerate(GROUPS):
                    if not (t0 <= g0 < t0 + tn):
                        continue
                    pst, Bpst = next_ps(k)
                    for c in range(2):
                        k.P.op("tensor", lambda e, c=c, g0=g0, gn=gn, pst=pst, t0=t0: e.transpose(
                            pst[:gn, c * 128:(c + 1) * 128], klf[:, c, g0 - t0:g0 - t0 + gn], k.ident[:, :]),
                            reads=[Bklf, k.Bconst], writes=[Bpst])
                    sg_, Bsg_ = stg[gi % 2]
                    cp(k, sg_[:gn, :], pst[:gn, 0:256], [Bpst], [Bsg_])
                    act(k, ktok[:gn, gi, :], sg_[:gn, :], AF.Copy, [Bsg_], [Bktok])
                    dma(k, k.dout["o_lat"][g0:g0 + gn, :], sg_[:gn, :], [Bsg_], [k.Bw], q="sync")
                    pst2, Bpst2 = next_ps(k)
                    k.P.op("tensor", lambda e, g0=g0, gn=gn, pst2=pst2: e.transpose(
                        pst2[:gn, 0:64], krf[0:64, g0:g0 + gn], k.ident[0:64, 0:64]),
                        reads=[Bkrf, k.Bconst], writes=[Bpst2])
                    sr_, Bsr_ = stgr[gi % 2]
                    cp(k, sr_[:gn, :], pst2[:gn, 0:64], [Bpst2], [Bsr_])
                    dma(k, k.dout["o_rope"][g0:g0 + gn, :], sr_[:gn, :], [Bsr_], [k.Bw], q="sync")
                if t0 == 1536:
                    dma(k, cc_ins[0][:, 0:2048], klatT[:, 0, 0:NF], [BklatT], [Bccin], q="gpsimd", asyn=True)
                    dma(k, cc_ins[1][:, 0:2048], klatT[:, 1, 0:NF], [BklatT], [Bccin], q="gpsimd", asyn=True)
                    dma(k, cc_ins[2][:, 0:2560].rearrange("p (a r) -> p a r", a=10), ktok[:, 0:10, :], [Bktok], [Bccin], q="gpsimd", asyn=True)
                    dma(k, cc_ins[3][:, 0:1536].rearrange("p (a r) -> p a r", a=6), ktok[:, 10:16, :], [Bktok], [Bccin], q="gpsimd", asyn=True)
                    dma(k, cc_ins[0][0:64, 2048:2560], kropeT[0:64, 0:512], [BkropeT], [Bccin], q="gpsimd", asyn=True)
                    dma(k, cc_ins[1][0:64, 2048:2560], kropeT[0:64, 512:1024], [BkropeT], [Bccin], q="gpsimd", asyn=True)
                    dma(k, cc_ins[3][0:64, 1536:2560], kropeT[0:64, 1024:2048], [BkropeT], [Bccin], q="gpsimd", asyn=True)
                    for ci in range(4):
                        P.dma("gpsimd", lambda e, ci=ci: e.collective_compute(
                            "AllGather", ALU.bypass, replica_groups=[[0, 1], [2, 3], [4, 5], [6, 7]],
                            ins=[cc_ins[ci]], outs=[cc_outs[ci]]), reads=[Bccin], writes=[Bccout], inc=1, asyn=True)
                    dma(k, pklatT[:, 0, :], cc_outs[0][0:128, 0:2048], [Bccout], [Bprev], q="gpsimd", asyn=True)
                    dma(k, pklatT[:, 1, :], cc_outs[1][0:128, 0:2048], [Bccout], [Bprev], q="gpsimd", asyn=True)
                    dma(k, pktok[:, 0:10, :], cc_outs[2][0:128, 0:2560].rearrange("p (a r) -> p a r", a=10), [Bccout], [Bprev], q="gpsimd", asyn=True)
                    dma(k, pktok[:, 10:16, :], cc_outs[3][0:128, 0:1536].rearrange("p (a r) -> p a r", a=6), [Bccout], [Bprev], q="gpsimd", asyn=True)
                    dma(k, pkropeT[0:64, 0:512], cc_outs[0][0:64, 2048:2560], [Bccout], [Bprev], q="gpsimd", asyn=True)
                    dma(k, pkropeT[0:64, 512:1024], cc_outs[1][0:64, 2048:2560], [Bccout], [Bprev], q="gpsimd", asyn=True)
                    dma(k, pkropeT[0:64, 1024:2048], cc_outs[3][0:64, 1536:2560], [Bccout], [Bprev], q="gpsimd", asyn=True)
            P.barrier(skip_async=True)
            P.emit()
        sw.close()

        if k.mla_stop == "A":
            return
        def own_set(g, q0=0, maskfix=False):
            g0, gn = GROUPS[g]
            return (lambda rc, g0=g0, gn=gn: klatT[:, rc, g0:g0 + gn], kropeT[:, g0:g0 + gn], ktok[:, g, :], gn, None,
                    q0, maskfix, [BklatT, BkropeT, Bktok])

        def prev_set(g):
            return (lambda rc, g=g: pklatT[:, rc, g * 128:(g + 1) * 128], pkropeT[:, g * 128:(g + 1) * 128],
                    pktok[:, g, :], 128, k.prevbias[:, 0:1], 0, False, [Bprev])

        scc = ExitStack()
        pasts = []
        pT_, BpT_ = _sb(k, scc, "m_pastT0", [128, 2, PAST], BF16)
        pk_, Bpk_ = _sb(k, scc, "m_ptok0", [128, 16, 256], BF16)
        pR_, BpR_ = _sb(k, scc, "m_pastR0", [128, PAST], BF16)
        P.op("vector", lambda e, pR_=pR_: e.memset(pR_[64:128, :], 0.0), writes=[BpR_])
        pasts.append((pT_, BpT_, pk_, Bpk_, pR_, BpR_))

        def load_past(s_i):
            pT_, BpT_, pk_, Bpk_, pR_, BpR_ = pasts[s_i % 2]
            dma(k, pT_[:, :, :], k.din["cache_latT"][s_i].rearrange("(a p) t -> p a t", p=128), [k.Bw], [BpT_],
                q="gpsimd")
            cl = k.din["cache_lat"][s_i].rearrange("(t p) r -> p t r", p=128)
            for q2 in range(2):
                dma(k, pk_[:, 8 * q2:8 * q2 + 8, :], cl[:, 8 * q2:8 * q2 + 8, :], [k.Bw], [Bpk_], q="gpsimd")
            dma(k, pR_[0:64, :], k.din["cache_ropeT"][s_i], [k.Bw], [BpR_], q="gpsimd")


        with ExitStack() as sc:
            qnope, Bqnope = _sb(k, sc, "m_qnope", [128, T], BF16)
            qr, Bqr = _sb(k, sc, "m_qr", [128, T], BF16)
            P.op("vector", lambda e: e.memset(qr[64:128, :], 0.0), writes=[Bqr])
            QR_ZERO = True
            qlat, Bqlat = _sb(k, sc, "m_qlat", [128, 2, T], BF16)
            k.ps_n = 3
            k.ps_rr = 0
            posets = [(k.ps[3], k.ps[4], k.ps[7], [k.psb[3], k.psb[4], k.psb[7]]),
                      (k.ps[5], k.ps[6], k.ps[7], [k.psb[5], k.psb[6], k.psb[7]])]
            olats = [(olat, Bolat), (olat2, Bolat2)]
            acnt = [0]

            def ov_out(h, c0, n, ol, Bol):
                psv, Bpsv = next_ps(k)
                for rc in range(2):
                    mm(k, psv[:, :n], wuv[:, rc, h, :], ol[:, rc, :n], rc == 0, rc == 1, [Bwuv, Bol], [Bpsv],
                       signal=(rc == 1))
                act(k, ov[:, h, c0:c0 + n], psv[:, :n], AF.Copy, [Bpsv], [k.Bxn])

            load_past(0)
            pend_ov = []
            Bqn_t = [Buf(f"qnope_t{i}") for i in range(len(TILES))]
            Bqr_t = [Buf(f"qr_t{i}") for i in range(len(TILES))]
            Bql_t = [Buf(f"qlat_t{i}") for i in range(len(TILES))]

            def q_stage(h, ti):
                t0, tn = TILES[ti]
                psn, Bpsn = next_ps(k)
                for kc in range(3):
                    mm(k, psn[:, :tn], wuq[:, kc, h * 192:h * 192 + 128], cqn[:, kc, t0:t0 + tn], kc == 0, kc == 2,
                       [Bwuq, Bcqn], [Bpsn], signal=(kc == 2))
                act(k, qnope[:, t0:t0 + tn], psn[:, :tn], AF.Copy, [Bpsn], [Bqn_t[ti]])
                psa, Bpsa = next_ps(k)
                for kc in range(3):
                    mm(k, psa[0:32, :tn], wuq[:, kc, h * 192 + 128:h * 192 + 160], cqn[:, kc, t0:t0 + tn], kc == 0,
                       kc == 2, [Bwuq, Bcqn], [Bpsa], signal=(kc == 2))
                psb, Bpsb = next_ps(k)
                for kc in range(3):
                    mm(k, psb[0:32, :tn], wuq[:, kc, h * 192 + 160:h * 192 + 192], cqn[:, kc, t0:t0 + tn], kc == 0,
                       kc == 2, [Bwuq, Bcqn], [Bpsb], signal=(kc == 2))
                rope(psa, Bpsa, psb, Bpsb, qr, Bqr_t[ti], t0, tn)
                for rc in range(2):
                    psl, Bpsl = next_ps(k)
                    mm(k, psl[:, :tn], wukT[:, h, rc * 128:(rc + 1) * 128], qnope[:, t0:t0 + tn], True, True,
                       [BwukT, Bqn_t[ti]], [Bpsl])
                    act(k, qlat[:, rc, t0:t0 + tn], psl[:, :tn], AF.Copy, [Bpsl], [Bql_t[ti]])

            q_stage(0, 0)
            for h in range(8):
                for qt in range(4):
                    q_stage(h, qt + 1)
                    c0 = qt * 512
                    ks = [own_set(16)]
                    ks += [own_set(4 * qt + j, q0=128 * j, maskfix=True) for j in range(1, 4)]
                    ks += [own_set(4 * qt, q0=0, maskfix=True)]
                    ks += [own_set(g) for g in range(4 * qt)]
                    ks += [prev_set(g) for g in range(16)]
                    ol, Bol = olats[acnt[0] % 2]
                    attend(k, M, lambda rc, c0=c0: qlat[:, rc, c0:c0 + 512], qr[:, c0:c0 + 512], 512, ks, ol, Bol,
                           [Bql_t[qt], Bqr_t[qt]], posets[acnt[0] % 2])
                    if pend_ov:
                        ov_out(*pend_ov.pop())
                    pend_ov.append((h, c0, 512, ol, Bol))
                    acnt[0] += 1
                for rc in range(2):
                    cp(k, qlat_s[:, rc, :, h, :], qlat[:, rc, C_SAMP:T].rearrange("p (s q) -> p s q", s=4), [Bql_t[4]], [Bqs])
                cp(k, qr_s[:, :, h, :], qr[:, C_SAMP:T].rearrange("p (s q) -> p s q", s=4), [Bqr_t[4]], [Bqs])
                if h + 1 < 8:
                    q_stage(h + 1, 0)
                ol, Bol = olats[acnt[0] % 2]
                attend(k, M, lambda rc: qlat[:, rc, C_META:C_META + 16], qr[:, C_META:C_META + 16], 16, [own_set(16)],
                       ol, Bol, [Bql_t[4], Bqr_t[4]], posets[acnt[0] % 2])
                if pend_ov:
                    ov_out(*pend_ov.pop())
                pend_ov.append((h, C_META, 16, ol, Bol))
                acnt[0] += 1
            if pend_ov:
                ov_out(*pend_ov.pop())
            P.barrier()
            P.emit()
        if k.mla_stop == "C":
            return
        with ExitStack() as sc2:
            for i in range(1, 2):
                pT_, BpT_ = _sb(k, sc2, f"m_pastT{i}", [128, 2, PAST], BF16)
                pk_, Bpk_ = _sb(k, sc2, f"m_ptok{i}", [128, 16, 256], BF16)
                pR_, BpR_ = _sb(k, sc2, f"m_pastR{i}", [128, PAST], BF16)
                P.op("vector", lambda e, pR_=pR_: e.memset(pR_[64:128, :], 0.0), writes=[BpR_])
                pasts.append((pT_, BpT_, pk_, Bpk_, pR_, BpR_))
            k.ps_n = 3
            k.ps_rr = 0
            posets2 = [(k.ps[3], k.ps[4], k.ps[7], [k.psb[3], k.psb[4], k.psb[7]]),
                       (k.ps[5], k.ps[6], k.ps[7], [k.psb[5], k.psb[6], k.psb[7]])]
            olats2 = [(olat, Bolat), (olat2, Bolat2)]
            skl, Bskl = _sb(k, sc2, "m_skl", [128, 2, NSAMP], BF16)
            skt, Bskt = _sb(k, sc2, "m_skt", [128, 4, 256], BF16)
            cp(k, skl[:, :, :], klatT[:, :, C_SAMP:T], [BklatT], [Bskl])
            cp(k, skt[:, :, :], ktok[:, 17:21, :], [Bktok], [Bskt])
            for c in range(8):
                dma(k, k.h[:, c, :], h_scr[:, c, :], [Bhs], [k.Bh, BklatT, Bktok], q="sync")

            for s_i in range(4):
                if s_i + 1 < 4:
                    load_past(s_i + 1)
                pastT, BpastT, ptok, Bptok, pastR, BpastR = pasts[s_i % 2]
                ks = [(lambda rc, g=g: pastT[:, rc, g * 128:(g + 1) * 128], pastR[:, g * 128:(g + 1) * 128],
                       ptok[:, g, :], 128, None, 0, False, [BpastT, BpastR, Bptok]) for g in range(16)]
                ks += [(lambda rc, s_i=s_i: skl[:, rc, 32 * s_i:32 * s_i + 32],
                        kropeT[:, C_SAMP + 32 * s_i:C_SAMP + 32 * s_i + 32], skt[:, s_i, :], 32, None, 0, False,
                        [Bskl, BkropeT, Bskt])]
                ol, Bol = olats2[s_i % 2]
                attend(k, M, lambda rc, s_i=s_i: qlat_s[:, rc, s_i].rearrange("p h q -> p (h q)"),
                       qr_s[:, s_i].rearrange("p h q -> p (h q)"), 256, ks, ol, Bol, [Bqs], posets2[s_i % 2])
                psv, Bpsv = next_ps(k)
                for h in range(8):
                    for rc in range(2):
                        mm(k, psv[:, h * 32:(h + 1) * 32], wuv[:, rc, h, :], ol[:, rc, h * 32:(h + 1) * 32], rc == 0,
                           rc == 1, [Bwuv, Bol], [Bpsv], signal=(rc == 1 and h == 7))
                act(k, ov[:, :, C_SAMP + 32 * s_i:C_SAMP + 32 * s_i + 32],
                    psv[:, 0:256].rearrange("p (h q) -> p h q", h=8), AF.Copy, [Bpsv], [k.Bxn])
            P.barrier()
            P.emit()
        if k.mla_stop in ("C2", "C2prep", "C2dma", "C2t2"):
            return
        scc.close()
        k.ps_n = 8
        k.ps_rr = 0
        with ExitStack() as sd:
            wo, Bwo = _sb(k, sd, "m_wo", [128, 8, D], BF16)
            dma(k, wo[:, :, :], wo_d, [k.Bw], [Bwo], q="gpsimd")
            for d in range(8):
                for (t0, tn) in TILES:
                    psd, Bpsd = next_ps(k)
                    for c in range(8):
                        mm(k, psd[:, :tn], wo[:, c, d * 128:(d + 1) * 128], ov[:, c, t0:t0 + tn], c == 0, c == 7,
                           [Bwo, k.Bxn], [Bpsd], signal=(c == 7))
                    tt(k, k.h[:, d, t0:t0 + tn], k.h[:, d, t0:t0 + tn], psd[:, :tn], ALU.add, [k.Bh, Bpsd], [k.Bh])
            P.barrier()
            P.emit()


def final_phase(k):
    P = k.P
    with ExitStack() as st:
        rstd, Brstd = _sb(k, st, "f_rstd", [128, T], F32)
        sq0, Bsq0 = _sb(k, st, "f_sq0", [128, 512], F32)
        sq1, Bsq1 = _sb(k, st, "f_sq1", [128, 512], F32)
        yb = [_sb(k, st, f"f_y{i}", [128, 512], F32) for i in range(2)]
        nw = k.normw["final"]
        i = 0
        for (t0, tn) in TILES:
            rms_rstd(k, k.h, k.Bh, 8, D, rstd, Brstd, [sq0, sq1], [Bsq0, Bsq1], tiles=[(t0, tn)])
            for c in range(8):
                y, By = yb[i % 2]
                i += 1
                stt(k, y[:, :tn], k.h[:, c, t0:t0 + tn], nw[:, c:c + 1], rstd[:, t0:t0 + tn], ALU.mult, ALU.mult,
                    [k.Bh, Brstd, k.Bconst], [By])
                dma(k, k.dout["o_y"][:, c, t0:t0 + tn], y[:, :tn], [By], [k.Bw], q="sync" if i % 2 else "gpsimd")
        P.barrier()
        P.emit()


W_NAMES = ["ffn1_norm", "ffn1_w_gate", "ffn1_w_up", "ffn1_w_down", "mix_norm", "gla_w_in", "gla_w_gate_up",
           "gla_b_gate", "gla_head_norm", "gla_w_out", "mla_w_down", "mla_q_norm", "mla_w_uq", "mla_kv_norm",
           "mla_w_uk", "mla_w_uv", "mla_w_out", "ffn2_norm", "ffn2_w_gate", "ffn2_w_up", "ffn2_w_down",
           "final_norm"]
W_SHAPES = {
    "ffn1_norm": [2, D], "ffn1_w_gate": [2, D, DFF], "ffn1_w_up": [2, D, DFF], "ffn1_w_down": [2, DFF, D],
    "mix_norm": [2, D], "gla_w_in": [1, D, 3088], "gla_w_gate_up": [1, 16, 512], "gla_b_gate": [1, 512],
    "gla_head_norm": [1, 256], "gla_w_out": [1, D, D], "mla_w_down": [1, D, 704], "mla_q_norm": [1, 384],
    "mla_w_uq": [1, 384, 1536], "mla_kv_norm": [1, 256], "mla_w_uk": [1, 256, 8, 128],
    "mla_w_uv": [1, 256, 8, 128], "mla_w_out": [1, D, D], "ffn2_norm": [2, D], "ffn2_w_gate": [2, D, DFF],
    "ffn2_w_up": [2, D, DFF], "ffn2_w_down": [2, DFF, D], "final_norm": [D],
}


def build(stage=99, mla_stop=None):
    nc = bass.Bass("TRN2", target_bir_lowering=False)
    k = K()
    k.mla_stop = mla_stop
    k.nc = nc
    k.din = {}
    for n in W_NAMES:
        k.din[n] = nc.dram_tensor(n, W_SHAPES[n], F32, kind="ExternalInput").ap()
    k.din["xT"] = nc.dram_tensor("xT", [128, 8, T], F32, kind="ExternalInput").ap()
    k.din["consts"] = nc.dram_tensor("consts", [128, NCONST], F32, kind="ExternalInput").ap()
    k.din["state"] = nc.dram_tensor("state", [4, 4, 128, 256], F32, kind="ExternalInput").ap()
    k.din["rope"] = nc.dram_tensor("rope", [32, 2, T], F32, kind="ExternalInput").ap()
    k.din["w_ukT"] = nc.dram_tensor("w_ukT", [128, 8, 256], F32, kind="ExternalInput").ap()
    k.din["cache_lat"] = nc.dram_tensor("cache_lat", [4, PAST, 256], F32, kind="ExternalInput").ap()
    k.din["cache_ropeT"] = nc.dram_tensor("cache_ropeT", [4, 64, PAST], F32, kind="ExternalInput").ap()
    k.din["cache_latT"] = nc.dram_tensor("cache_latT", [4, 256, PAST], F32, kind="ExternalInput").ap()
    k.dout = {}
    k.dout["o_y"] = nc.dram_tensor("o_y", [128, 8, T], F32, kind="ExternalOutput").ap()
    k.dout["o_lat"] = nc.dram_tensor("o_lat", [T, 256], F32, kind="ExternalOutput").ap()
    k.dout["o_rope"] = nc.dram_tensor("o_rope", [T, 64], F32, kind="ExternalOutput").ap()
    k.dout["o_sp"] = nc.dram_tensor("o_sp", [4, 128, 256], F32, kind="ExternalOutput").ap()
    k.dout["o_ss"] = nc.dram_tensor("o_ss", [4, 4, 128, 256], F32, kind="ExternalOutput").ap()
    if stage < 99:
        k.dout["dbg_h"] = nc.dram_tensor("dbg_h", [128, 8, T], F32, kind="ExternalOutput").ap()

    with ExitStack() as st:
        P = Prog(nc, st)
        k.P = P
        k.h, k.Bh = _sb(k, st, "h", [128, 8, T], F32)
        k.xn, k.Bxn = _sb(k, st, "xn", [128, 8, T], BF16)
        k.Bxnt = [Buf(f"xn_t{i}") for i in range(len(TILES))]
        k.cst, k.Bconst = _sb(k, st, "cst", [128, NCONST], F32)
        k.ones_f = k.cst[:, 0:128]
        k.ones_b, _ = _sb(k, st, "ones_b", [128, 128], BF16)
        k.ident = k.cst[:, 128:256]
        k.tri = k.cst[:, 256:384]
        k.revtri = k.cst[:, 384:512]
        k.mask01 = k.cst[:, 512:640]
        k.TRI = {64: k.cst[:, 256:384], 32: k.cst[:, 1536:1664]}
        k.REV = {64: k.cst[:, 384:512], 32: k.cst[:, 1664:1792]}
        k.MASK4 = {64: k.cst[:, 1024:1536].rearrange("p (h n) -> p h n", h=4),
                   32: k.cst[:, 2048:2560].rearrange("p (h n) -> p h n", h=4)}
        k.fA = k.cst[:, 707:708]
        k.fB = k.cst[:, 708:709]
        k.prevbias = k.cst[:, 709:710]
        k.qnorm = k.cst[:, 696:699]
        k.kvnorm = k.cst[:, 699:701]
        k.hnorm = k.cst[:, 701:703]
        k.Bw = Buf("dram_w")
        k.ps, k.psb = [], []
        for i in range(8):
            t = st.enter_context(nc.psum_tensor(f"ps{i}", [128, 512], F32))
            k.ps.append(t)
            k.psb.append(Buf(f"ps{i}"))
        k.ps_rr = 0
        k.ps_n = 8
        dma(k, k.cst[:, :], k.din["consts"], [k.Bw], [k.Bconst])
        for c in range(8):
            dma(k, k.h[:, c, :], k.din["xT"][:, c, :], [k.Bw], [k.Bh], q="sync")
        cp(k, k.ones_b[:, :], k.cst[:, 0:128], [k.Bconst], [k.Bconst])
        k.normw = {}
        specs = [("ffn1_0", "ffn1_norm", 0), ("ffn1_1", "ffn1_norm", 1), ("mix_0", "mix_norm", 0),
                 ("mix_1", "mix_norm", 1), ("ffn2_0", "ffn2_norm", 0), ("ffn2_1", "ffn2_norm", 1),
                 ("final", "final_norm", None)]
        for i, (nm, src, li) in enumerate(specs):
            k.normw[nm] = k.cst[:, 640 + 8 * i:648 + 8 * i]
        P.barrier()
        P.emit()

        ffn_phase(k, 0, 1)
        if stage >= 2:
            gla_phase(k)
        if stage >= 3:
            ffn_phase(k, 0, 2)
            ffn_phase(k, 1, 1)
        if stage >= 4:
            mla_phase(k)
        if stage >= 5:
            ffn_phase(k, 1, 2, final=True)
        if stage < 99:
            for c in range(8):
                dma(k, k.dout["dbg_h"][:, c, :], k.h[:, c, :], [k.Bh], [k.Bw])
        P.barrier()
        P.emit()
    return nc, k


NORM_SPECS = [("ffn1_norm", 0), ("ffn1_norm", 1), ("mix_norm", 0), ("mix_norm", 1), ("ffn2_norm", 0),
              ("ffn2_norm", 1), ("final_norm", None)]


def make_consts(inputs, core):
    c = np.zeros((128, NCONST), np.float32)
    c[:, 0:128] = 1.0
    c[:, 128:256] = np.eye(128, dtype=np.float32)
    j = np.arange(128)[:, None]
    i = np.arange(128)[None, :]
    same = (j // 64) == (i // 64)
    c[:, 256:384] = np.where(same & (j <= i), -1.0 / 16.0, 0.0)
    c[:, 384:512] = np.where(same & (j > i), -1.0 / 16.0, 0.0)
    c[:, 512:640] = np.where(same & (j <= i), 1.0, 0.0)
    for n, (src, li) in enumerate(NORM_SPECS):
        v = np.asarray(inputs[src], np.float32)
        v = v if li is None else v[li]
        c[:, 640 + 8 * n:648 + 8 * n] = v.reshape(8, 128).T
    c[:, 696:699] = np.asarray(inputs["mla_q_norm"], np.float32)[0].reshape(3, 128).T
    c[:, 699:701] = np.asarray(inputs["mla_kv_norm"], np.float32)[0].reshape(2, 128).T
    c[:, 701:703] = np.asarray(inputs["gla_head_norm"], np.float32)[0].reshape(2, 128).T
    for hh in range(4):
        c[:, 1024 + 128 * hh:1152 + 128 * hh] = c[:, 512:640]
    same32 = (j // 32) == (i // 32)
    c[:, 1536:1664] = np.where(same32 & (j <= i), -1.0 / 16.0, 0.0)
    c[:, 1664:1792] = np.where(same32 & (j > i), -1.0 / 16.0, 0.0)
    for hh in range(4):
        c[:, 2048 + 128 * hh:2176 + 128 * hh] = np.where(same32 & (j <= i), 1.0, 0.0)
    half = core % 2
    c[:, 707] = 1.0 if half == 0 else 0.0
    c[:, 708] = 1.0 if half == 1 else 0.0
    c[:, 709] = 0.0 if half == 1 else NEG
    return c


def rope_table(half):
    pos = np.concatenate([NMETA + half * NF + np.arange(NF), np.arange(NMETA), np.tile(PAST + np.arange(32), 4)])
    inv = (np.float32(10000.0) ** (-np.arange(32, dtype=np.float32) / np.float32(32))).astype(np.float32)
    ang = (pos.astype(np.float32)[None, :] * inv[:, None]).astype(np.float32)
    return np.ascontiguousarray(np.stack([np.cos(ang), np.sin(ang)], 1).astype(np.float32))


def core_inputs(inputs, core):
    b, half = core // 2, core % 2
    xp = np.asarray(inputs["x_prompt"], np.float32)
    xs = np.asarray(inputs["x_sample"], np.float32)
    meta = np.asarray(inputs["meta_tokens"], np.float32)
    rows = np.concatenate([xp[b, half * NF:(half + 1) * NF], meta, xs[4 * core:4 * core + 4].reshape(NSAMP, D)], 0)
    xT = np.ascontiguousarray(rows.T.reshape(8, 128, T).transpose(1, 0, 2))
    m = {"xT": xT, "consts": make_consts(inputs, core)}
    m["state"] = np.ascontiguousarray(np.asarray(inputs["state_gla"], np.float32)[0, 4 * core:4 * core + 4])
    m["cache_lat"] = np.ascontiguousarray(np.asarray(inputs["cache_mla_latent"], np.float32)[0, 4 * core:4 * core + 4])
    m["cache_latT"] = np.ascontiguousarray(m["cache_lat"].transpose(0, 2, 1))
    m["cache_ropeT"] = np.ascontiguousarray(
        np.asarray(inputs["cache_mla_rope"], np.float32)[0, 4 * core:4 * core + 4].transpose(0, 2, 1))
    m["w_ukT"] = np.ascontiguousarray(np.asarray(inputs["mla_w_uk"], np.float32)[0].transpose(2, 1, 0))
    m["rope"] = rope_table(half)
    for n in W_NAMES:
        m[n] = np.ascontiguousarray(np.asarray(inputs[n], np.float32))
    return m


def assemble(results):
    y_p = np.zeros((4, 4096, D), np.float32)
    y_s = np.zeros((32, 32, D), np.float32)
    sp = np.zeros((1, 4, 4, 128, 256), np.float32)
    ss = np.zeros((1, 32, 4, 128, 256), np.float32)
    lat_p = np.zeros((1, 4, NMETA + 4096, 256), np.float32)
    rope_p = np.zeros((1, 4, NMETA + 4096, 64), np.float32)
    lat_s = np.zeros((1, 32, 32, 256), np.float32)
    rope_s = np.zeros((1, 32, 32, 64), np.float32)
    for c in range(8):
        r = results[c]
        b, half = c // 2, c % 2
        yT = r["o_y"].transpose(1, 0, 2).reshape(D, T).T
        y_p[b, half * NF:(half + 1) * NF] = yT[:NF]
        y_s[4 * c:4 * c + 4] = yT[C_SAMP:].reshape(4, 32, D)
        if half == 1:
            sp[0, b] = r["o_sp"]
        ss[0, 4 * c:4 * c + 4] = r["o_ss"]
        lat_p[0, b, NMETA + half * NF:NMETA + (half + 1) * NF] = r["o_lat"][:NF]
        rope_p[0, b, NMETA + half * NF:NMETA + (half + 1) * NF] = r["o_rope"][:NF]
        if half == 0:
            lat_p[0, b, :NMETA] = r["o_lat"][C_META:C_META + NMETA]
            rope_p[0, b, :NMETA] = r["o_rope"][C_META:C_META + NMETA]
        lat_s[0, 4 * c:4 * c + 4] = r["o_lat"][C_SAMP:].reshape(4, 32, 256)
        rope_s[0, 4 * c:4 * c + 4] = r["o_rope"][C_SAMP:].reshape(4, 32, 64)
    return (y_p, y_s, sp, ss, lat_p, rope_p, lat_s, rope_s)


def kernel(**inputs):
    nc, k = build()
    in_maps = [core_inputs(inputs, c) for c in range(8)]
    res = run_bass_kernel_spmd(nc, in_maps, core_ids=list(range(8)))
    return assemble(res.results)
```

```python
import numpy as np
from contextlib import ExitStack
import concourse.bass as bass
import concourse.mybir as mybir
from concourse.bass_utils import run_bass_kernel_spmd

F32 = mybir.dt.float32
BF16 = mybir.dt.bfloat16
AF = mybir.ActivationFunctionType
ALU = mybir.AluOpType

ENGS = ("tensor", "vector", "scalar", "gpsimd", "sync")

D = 1024
DFF = 2816
NF = 2048
NMETA = 16
NSAMP = 128
T = NF + NMETA + NSAMP
C_META = NF
C_SAMP = NF + NMETA
TILES = [(0, 512), (512, 512), (1024, 512), (1536, 512), (2048, 144)]
EPS = 1e-6
PAST = 2048
NEG = -30000.0
NCONST = 2560


class Buf:
    __slots__ = ("name", "lw", "lw_eng", "rd")

    def __init__(self, name):
        self.name = name
        self.lw = None
        self.lw_eng = None
        self.rd = {}


class Prog:
    def __init__(self, nc, stack, n_dma_sems=24, epoch=1000):
        self.nc = nc
        self.stack = stack
        self.ops = {e: [] for e in ENGS}
        self.sig = {e: 0 for e in ENGS}
        self.ep = {e: 0 for e in ENGS}
        self.sems = {}
        self.waited = {e: {} for e in ENGS}
        self.epoch = epoch
        self.n_dma = n_dma_sems
        self.dma_cnt = [0] * n_dma_sems
        self.dma_last = [None] * n_dma_sems
        self.dma_rr = 0
        self.nops = 0
        self.old = []
        self.async_idx = set()

    def _sem(self, name):
        if name not in self.sems:
            self.sems[name] = self.stack.enter_context(self.nc.semaphore(name))
        return name

    def _cursem(self, eng):
        if self.sig[eng] >= self.epoch:
            self.old.append((f"s_{eng}_{self.ep[eng]}", self.sig[eng]))
            self.ep[eng] += 1
            self.sig[eng] = 0
        return self._sem(f"s_{eng}_{self.ep[eng]}")

    def _wait(self, eng, tok):
        sem, val = tok
        if self.waited[eng].get(sem, 0) >= val:
            return
        self.waited[eng][sem] = val
        self.ops[eng].append(("wait", sem, val))

    def _deps(self, eng, me, reads, writes):
        for b in reads:
            if b.lw is not None:
                self._wait(eng, b.lw)
        for b in writes:
            if b.lw is not None and b.lw_eng != me:
                self._wait(eng, b.lw)
            for teng, tok in b.rd.items():
                if teng != me:
                    self._wait(eng, tok)

    def _post(self, me, tok, reads, writes):
        for b in reads:
            b.rd[me] = tok
        for b in writes:
            b.lw = tok
            b.lw_eng = me
            b.rd = {}

    def op(self, eng, fn, reads=(), writes=(), signal=True):
        mysem = self._cursem(eng)
        self._deps(eng, eng, reads, writes)
        tok = (mysem, self.sig[eng] + 1)
        if signal:
            self.sig[eng] += 1
        self.ops[eng].append(("op", fn, mysem if signal else None, 1))
        self._post(eng, tok, reads, writes)
        self.nops += 1
        return tok

    def dma(self, q, fn, reads=(), writes=(), inc=16, asyn=False):
        i = self.dma_rr
        self.dma_rr = (i + 1) % self.n_dma
        sem = self._sem(f"s_dma_{i}")
        me = f"dma{i}"
        if self.dma_last[i] is not None:
            self._wait(q, self.dma_last[i])
        self._deps(q, me, reads, writes)
        self.dma_cnt[i] += inc
        tok = (sem, self.dma_cnt[i])
        self.dma_last[i] = tok
        if asyn:
            self.async_idx.add(i)
        else:
            self.async_idx.discard(i)
        self.ops[q].append(("op", fn, sem, inc))
        self._post(me, tok, reads, writes)
        self.nops += 1
        return tok

    def barrier(self, skip_async=False):
        toks = list(self.old)
        for e in ENGS:
            if self.sig[e] > 0:
                toks.append((f"s_{e}_{self.ep[e]}", self.sig[e]))
        for i in range(self.n_dma):
            if self.dma_last[i] is not None and not (skip_async and i in self.async_idx):
                toks.append(self.dma_last[i])
        for e in ENGS:
            for t in toks:
                self._wait(e, t)

    def emit(self):
        nc = self.nc
        with nc.Block() as block:
            for e in ENGS:
                items = self.ops[e]

                def body(engh, items=items):
                    for it in items:
                        if it[0] == "wait":
                            engh.wait_ge(self.sems[it[1]], it[2])
                        else:
                            ins = it[1](engh)
                            if it[2] is not None:
                                ins.then_inc(self.sems[it[2]], it[3])

                getattr(block, e)(body)
        self.ops = {e: [] for e in ENGS}


class K:
    pass


_UID = [0]


def _sb(k, st, name, shape, dt):
    _UID[0] += 1
    name = f"{name}_{_UID[0]}"
    t = st.enter_context(k.nc.sbuf_tensor(name, shape, dt))
    return t, Buf(name)


def mm(k, out, lhsT, rhs, start, stop, reads, writes, signal=True):
    k.P.op("tensor", lambda e: e.matmul(out, lhsT=lhsT, rhs=rhs, start=start, stop=stop),
           reads=reads, writes=writes, signal=signal)


def act(k, out, in_, func, reads, writes, scale=None, bias=None, eng="scalar"):
    kw = {}
    if scale is not None:
        kw["scale"] = scale
    if bias is not None:
        kw["bias"] = bias
    k.P.op("scalar", lambda e: e.activation(out=out, in_=in_, func=func, **kw), reads=reads, writes=writes)


def tt(k, out, in0, in1, op, reads, writes, eng="vector"):
    k.P.op(eng, lambda e: e.tensor_tensor(out=out, in0=in0, in1=in1, op=op), reads=reads, writes=writes)


def ts(k, out, in0, s1, s2, op0, op1, reads, writes, eng="vector"):
    if op1 is None:
        k.P.op(eng, lambda e: e.tensor_scalar(out=out, in0=in0, scalar1=s1, scalar2=None, op0=op0),
               reads=reads, writes=writes)
    else:
        k.P.op(eng, lambda e: e.tensor_scalar(out=out, in0=in0, scalar1=s1, scalar2=s2, op0=op0, op1=op1),
               reads=reads, writes=writes)


def stt(k, out, in0, scalar, in1, op0, op1, reads, writes):
    k.P.op("vector", lambda e: e.scalar_tensor_tensor(out=out, in0=in0, scalar=scalar, in1=in1, op0=op0, op1=op1),
           reads=reads, writes=writes)


def cp(k, out, in_, reads, writes, eng="vector"):
    k.P.op(eng, lambda e: e.tensor_copy(out=out, in_=in_), reads=reads, writes=writes)


def dma(k, out, in_, reads, writes, q="sync", asyn=False):
    k.P.dma(q, lambda e: e.dma_start(out=out, in_=in_), reads=reads, writes=writes, asyn=asyn)


def next_ps(k):
    i = k.ps_rr
    k.ps_rr = (i + 1) % k.ps_n
    return k.ps[i], k.psb[i]


def rms_rstd(k, src, Bsrc, nchunk, nfeat, rstd, Brstd, sq, Bsq, col0=0, tiles=TILES):
    for (t0, tn) in tiles:
        ps, Bps = next_ps(k)
        for c in range(nchunk):
            s, Bs = sq[c % 2], Bsq[c % 2]
            sb16 = s[:, :].bitcast(BF16)
            act(k, sb16[:, :tn], src[:, c, col0 + t0:col0 + t0 + tn], AF.Square, [Bsrc], [Bs])
            mm(k, ps[:, :tn], k.ones_b[:, :], sb16[:, :tn], c == 0, c == nchunk - 1, [Bs, k.Bconst], [Bps],
               signal=True)
        ts(k, rstd[:, t0:t0 + tn], ps[:, :tn], 1.0 / nfeat, EPS, ALU.mult, ALU.add, [Bps], [Brstd])
        act(k, rstd[:, t0:t0 + tn], rstd[:, t0:t0 + tn], AF.Ln, [Brstd], [Brstd])
        act(k, rstd[:, t0:t0 + tn], rstd[:, t0:t0 + tn], AF.Exp, [Brstd], [Brstd], scale=-0.5)


def make_xn(k, nw, rstd, Brstd, sq, Bsq):
    for ti, (t0, tn) in enumerate(TILES):
        rms_rstd(k, k.h, k.Bh, 8, D, rstd, Brstd, sq, Bsq, tiles=[(t0, tn)])
        for c in range(8):
            stt(k, k.xn[:, c, t0:t0 + tn], k.h[:, c, t0:t0 + tn], nw[:, c:c + 1], rstd[:, t0:t0 + tn],
                ALU.mult, ALU.mult, [k.Bh, Brstd, k.Bconst], [k.Bxn, k.Bxnt[ti]])


def ffn_phase(k, li, which, final=False):
    P = k.P
    wg_d = k.din[f"ffn{which}_w_gate"][li].rearrange("(kc p) f -> p kc f", p=128)
    wu_d = k.din[f"ffn{which}_w_up"][li].rearrange("(kc p) f -> p kc f", p=128)
    wd_d = k.din[f"ffn{which}_w_down"][li].rearrange("(fc p) d -> p fc d", p=128)
    nw = k.normw[f"ffn{which}_{li}"]
    with ExitStack() as st:
        wg = [_sb(k, st, f"wg{i}", [128, 8, 512], BF16) for i in range(2)]
        wu = [_sb(k, st, f"wu{i}", [128, 8, 512], BF16) for i in range(2)]
        wd = [_sb(k, st, f"wd{i}", [128, 4, D], BF16) for i in range(2)]
        a, Ba = _sb(k, st, "ffn_a", [128, 4, T], BF16)
        rstd, Brstd = _sb(k, st, "ffn_rstd", [128, T], F32)
        sq0, Bsq0 = _sb(k, st, "ffn_sq0", [128, 512], F32)
        sq1, Bsq1 = _sb(k, st, "ffn_sq1", [128, 512], F32)
        sg = [_sb(k, st, f"ffn_sg{i}", [128, 512], F32) for i in range(2)]
        groups = [(0, 4), (4, 4), (8, 4), (12, 4), (16, 4), (20, 2)]

        def load(gi):
            f0, nf = groups[gi]
            b = gi % 2
            dma(k, wg[b][0][:, :, 0:nf * 128], wg_d[:, :, f0 * 128:(f0 + nf) * 128], [k.Bw], [wg[b][1]], q="gpsimd")
            dma(k, wu[b][0][:, :, 0:nf * 128], wu_d[:, :, f0 * 128:(f0 + nf) * 128], [k.Bw], [wu[b][1]], q="gpsimd")
            dma(k, wd[b][0][:, 0:nf, :], wd_d[:, f0:f0 + nf, :], [k.Bw], [wd[b][1]], q="gpsimd")

        load(0)
        make_xn(k, nw, rstd, Brstd, [sq0, sq1], [Bsq0, Bsq1])
        for gi, (f0, nf) in enumerate(groups):
            if gi + 1 < len(groups):
                load(gi + 1)
            b = gi % 2
            if gi == 0:
                order = [(fc, ti) for ti in range(len(TILES)) for fc in range(nf)]
            else:
                order = [(fc, ti) for fc in range(nf) for ti in range(len(TILES))]
            for fc, ti in order:
                t0, tn = TILES[ti]
                pg, Bpg = next_ps(k)
                pu, Bpu = next_ps(k)
                for kc in range(8):
                    mm(k, pg[:, :tn], wg[b][0][:, kc, fc * 128:(fc + 1) * 128], k.xn[:, kc, t0:t0 + tn],
                       kc == 0, kc == 7, [wg[b][1], k.Bxnt[ti]], [Bpg], signal=(kc == 7))
                for kc in range(8):
                    mm(k, pu[:, :tn], wu[b][0][:, kc, fc * 128:(fc + 1) * 128], k.xn[:, kc, t0:t0 + tn],
                       kc == 0, kc == 7, [wu[b][1], k.Bxnt[ti]], [Bpu], signal=(kc == 7))
                s, Bs = sg[(fc + ti) % 2]
                act(k, s[:, :tn], pg[:, :tn], AF.Silu, [Bpg], [Bs])
                tt(k, a[:, fc, t0:t0 + tn], s[:, :tn], pu[:, :tn], ALU.mult, [Bs, Bpu], [Ba])
            if final and gi == len(groups) - 1:
                fnw = k.normw["final"]
                yi = [0]

                def down_tile(t0, tn):
                    for d in range(8):
                        pd, Bpd = next_ps(k)
                        for fc in range(nf):
                            mm(k, pd[:, :tn], wd[b][0][:, fc, d * 128:(d + 1) * 128], a[:, fc, t0:t0 + tn],
                               fc == 0, fc == nf - 1, [wd[b][1], Ba], [Bpd], signal=(fc == nf - 1))
                        stt(k, k.h[:, d, t0:t0 + tn], pd[:, :tn], 0.5, k.h[:, d, t0:t0 + tn], ALU.mult, ALU.add,
                            [Bpd, k.Bh], [k.Bh])

                def norm_out(t0, tn):
                    rms_rstd(k, k.h, k.Bh, 8, D, rstd, Brstd, [sq0, sq1], [Bsq0, Bsq1], tiles=[(t0, tn)])
                    for c in range(8):
                        y, By = sg[yi[0] % 2]
                        yi[0] += 1
                        stt(k, y[:, :tn], k.h[:, c, t0:t0 + tn], fnw[:, c:c + 1], rstd[:, t0:t0 + tn], ALU.mult,
                            ALU.mult, [k.Bh, Brstd, k.Bconst], [By])
                        dma(k, k.dout["o_y"][:, c, t0:t0 + tn], y[:, :tn], [By], [k.Bw],
                            q="sync" if yi[0] % 2 else "gpsimd")

                down_tile(*TILES[0])
                for ti2, (t0, tn) in enumerate(TILES):
                    if ti2 + 1 < len(TILES):
                        down_tile(*TILES[ti2 + 1])
                    norm_out(t0, tn)
                continue
            for d in range(8):
                for (t0, tn) in TILES:
                    pd, Bpd = next_ps(k)
                    for fc in range(nf):
                        mm(k, pd[:, :tn], wd[b][0][:, fc, d * 128:(d + 1) * 128], a[:, fc, t0:t0 + tn],
                           fc == 0, fc == nf - 1, [wd[b][1], Ba], [Bpd], signal=(fc == nf - 1))
                    stt(k, k.h[:, d, t0:t0 + tn], pd[:, :tn], 0.5, k.h[:, d, t0:t0 + tn], ALU.mult, ALU.add,
                        [Bpd, k.Bh], [k.Bh])
        P.barrier()
        P.emit()


class NS:
    pass


def ps_view4(ps):
    return ps[:, :].rearrange("p (h n) -> p h n", h=4)


def gla_prep(k, W, Z, c0, n):
    xs = lambda kc: k.xn[:, kc, c0:c0 + n]
    ps, Bps = next_ps(k)
    for kc in range(8):
        mm(k, ps[0:16, :n], W.wgd[:, kc, :], xs(kc), kc == 0, kc == 7, [W.Bwgd, k.Bxn], [Bps], signal=(kc == 7))
    cp(k, Z.gd[0:16, :n], ps[0:16, :n], [Bps], [Z.Bgd])
    for half in range(2):
        psv, Bpsv = next_ps(k)
        for kc in range(8):
            mm(k, psv[:n, :], xs(kc), W.wv[:, kc, half * 512:(half + 1) * 512], kc == 0, kc == 7,
               [W.Bwv, k.Bxn], [Bpsv], signal=(kc == 7))
        act(k, Z.vt[:n, half * 512:(half + 1) * 512], psv[:n, :], AF.Copy, [Bpsv], [Z.Bvt])
    ps2, Bps2 = next_ps(k)
    mm(k, ps2[:n, :], Z.gd[0:17, :n], W.wgu[0:17, :], True, True, [Z.Bgd, W.Bwgu], [Bps2])
    act(k, Z.graw[:n, :], ps2[:n, :], AF.Exp, [Bps2], [Z.Bgraw], scale=-1.0)
    act(k, Z.graw[:n, :], Z.graw[:n, :], AF.Ln, [Z.Bgraw], [Z.Bgraw], bias=1.0)
    psq, Bpsq = next_ps(k)
    psq_v = ps_view4(psq)
    for h in range(4):
        for kc in range(8):
            mm(k, psq_v[:, h, :n], W.wq[:, kc, h * 128:(h + 1) * 128], xs(kc), kc == 0, kc == 7,
               [W.Bwq, k.Bxn], [Bpsq], signal=(kc == 7 and h == 3))
    psk, Bpsk = next_ps(k)
    psk_v = ps_view4(psk)
    for h in range(4):
        for kc in range(8):
            mm(k, psk_v[:, h, :n], W.wk[:, kc, h * 128:(h + 1) * 128], xs(kc), kc == 0, kc == 7,
               [W.Bwk, k.Bxn], [Bpsk], signal=(kc == 7 and h == 3))
    psg, Bpsg = next_ps(k)
    psg_v = ps_view4(psg)
    for h in range(4):
        mm(k, psg_v[:, h, :n], Z.graw[:n, h * 128:(h + 1) * 128], k.TRI[64][:n, :n], True, True,
           [Z.Bgraw, k.Bconst], [Bpsg], signal=(h == 3))
    psw, Bpsw = next_ps(k)
    mm(k, psw[:n, :], k.REV[64][:n, :n], Z.graw[:n, :], True, True, [Z.Bgraw, k.Bconst], [Bpsw])
    act(k, Z.egc[:, :, :n], psg_v[:, :, :n], AF.Exp, [Bpsg], [Z.Begc])
    act(k, Z.engc[:, :, :n], psg_v[:, :, :n], AF.Exp, [Bpsg], [Z.Bengc], scale=-1.0)
    act(k, Z.eW[:n, :], psw[:n, :], AF.Exp, [Bpsw], [Z.BeW])
    pskt, Bpskt = next_ps(k)
    for kc in range(8):
        mm(k, pskt[:n, :], xs(kc), W.wk[:, kc, :], kc == 0, kc == 7, [W.Bwk, k.Bxn], [Bpskt], signal=(kc == 7))
    stt(k, Z.qg[:, :, :n], psq_v[:, :, :n], 128.0 ** -0.5, Z.egc[:, :, :n], ALU.mult, ALU.mult,
        [Bpsq, Z.Begc], [Z.Bqg])
    tt(k, Z.kg[:, :, :n], psk_v[:, :, :n], Z.engc[:, :, :n], ALU.mult, [Bpsk, Z.Bengc], [Z.Bkg])
    tt(k, Z.kd[:n, :], pskt[:n, :], Z.eW[:n, :], ALU.mult, [Bpskt, Z.BeW], [Z.Bkd])
    psa, Bpsa = next_ps(k)
    psa_v = ps_view4(psa)
    for h in range(4):
        mm(k, psa_v[:n, h, :n], Z.kg[:, h, :n], Z.qg[:, h, :n], True, True, [Z.Bkg, Z.Bqg], [Bpsa],
           signal=(h == 3))
    tt(k, Z.atm[:n, :, :n], psa_v[:n, :, :n], k.MASK4[64][:n, :, :n], ALU.mult, [Bpsa, k.Bconst], [Z.Batm])


def gla_blocks(k, W, Z, c0, n, nb, L, o_dst, Bo, ecum_b0=None):
    xs = None
    pso = [ps_view4(k.pso[0]), ps_view4(k.pso[1])]
    for bl in range(nb):
        b0 = bl * L
        for h in range(4):
            for vc in range(2):
                c = h * 2 + vc
                out = pso[c // 4][:, c % 4, b0:b0 + L]
                mm(k, out, W.Sbf[:, h, vc * 128:(vc + 1) * 128], Z.qg[:, h, b0:b0 + L], True, False,
                   [W.BSbf[h], Z.Bqg], [k.Bpso[c // 4]], signal=False)
                mm(k, out, Z.vt[:n, h * 256 + vc * 128:h * 256 + (vc + 1) * 128], Z.atm[:n, h, b0:b0 + L],
                   False, True, [Z.Bvt, Z.Batm], [k.Bpso[c // 4]], signal=True)
            pss, Bpss = next_ps(k)
            mm(k, pss[:, 0:256], Z.kd[b0:b0 + L, h * 128:(h + 1) * 128], Z.vt[b0:b0 + L, h * 256:(h + 1) * 256],
               True, True, [Z.Bkd, Z.Bvt], [Bpss])
            stt(k, W.S[:, h, :], W.S[:, h, :], Z.egc[:, h, b0 + L - 1:b0 + L], pss[:, 0:256], ALU.mult, ALU.add,
                [W.BS[h], Z.Begc, Bpss], [W.BS[h]])
            act(k, W.Sbf[:, h, :], W.S[:, h, :], AF.Copy, [W.BS[h]], [W.BSbf[h]])
            if ecum_b0 is not None:
                gb = ecum_b0 + bl
                ts(k, Z.qt[:, h, b0:b0 + L], Z.qg[:, h, b0:b0 + L], W.ecum[:, gb, h:h + 1], None,
                   ALU.mult, None, [Z.Bqg, W.Becum], [Z.Bqt])
        if ecum_b0 is not None:
            gb = ecum_b0 + bl
            tt(k, W.ecum[:, gb + 1, :], W.ecum[:, gb, :], Z.egc[:, :, b0 + L - 1], ALU.mult,
               [W.Becum, Z.Begc], [W.Becum])
    for half in range(2):
        cp(k, o_dst[:, half * 4:(half + 1) * 4, :n], pso[half][:, :, :n], [k.Bpso[half]], [Bo])
    if ecum_b0 is not None:
        dma(k, k.qtl_scr[:, :, c0:c0 + n], Z.qt[:, :, :n], [Z.Bqt], [k.Bqtl], q="sync")


def gla_subtile(k, W, c0, n, nb, L, o_dst, Bo, ecum_b0=None):
    W.par ^= 1
    Z = W.sets[W.par]
    gla_prep(k, W, Z, c0, n)
    gla_blocks(k, W, Z, c0, n, nb, L, o_dst, Bo, ecum_b0)


def gla_post_r(k, W, c0, n, sr8, Bsr8):
    for c in range(8):
        psr, Bpsr = next_ps(k)
        for kc in range(8):
            mm(k, psr[:, :n], W.wr[:, kc, c * 128:(c + 1) * 128], k.xn[:, kc, c0:c0 + n], kc == 0, kc == 7,
               [W.Bwr, k.Bxn], [Bpsr], signal=(kc == 7))
        act(k, sr8[:, c, :n], psr[:, :n], AF.Silu, [Bpsr], [Bsr8])


def gla_post_norm(k, W, o, Bo, n, sr8, Bsr8):
    for c in range(8):
        act(k, W.sq8[:, c, :n], o[:, c, :n], AF.Square, [Bo], [W.Bsq8])
    for j in range(2):
        psn, Bpsn = next_ps(k)
        psn_v = psn[:, :].rearrange("p (h n) -> p h n", h=2)
        for hh in range(2):
            h = 2 * j + hh
            for vc in range(2):
                mm(k, psn_v[:, hh, :n], k.ones_b[:, :], W.sq8[:, h * 2 + vc, :n], vc == 0, vc == 1,
                   [W.Bsq8, k.Bconst], [Bpsn], signal=(vc == 1 and hh == 1))
        ts(k, W.rstd[:, 2 * j:2 * j + 2, :n], psn_v[:, :, :n], 1.0 / 256.0, EPS, ALU.mult, ALU.add, [Bpsn], [W.Brstd1])
    act(k, W.rstd[:, :, :n], W.rstd[:, :, :n], AF.Ln, [W.Brstd1], [W.Brstd1])
    act(k, W.rstd[:, :, :n], W.rstd[:, :, :n], AF.Exp, [W.Brstd1], [W.Brstd1], scale=-0.5)
    for c in range(8):
        h, vc = c // 2, c % 2
        stt(k, W.tmp[:, :n], o[:, c, :n], k.hnorm[:, vc:vc + 1], W.rstd[:, h, :n], ALU.mult, ALU.mult,
            [Bo, W.Brstd1, k.Bconst], [W.Btmp])
        tt(k, W.y[:, c, :n], W.tmp[:, :n], sr8[:, c, :n], ALU.mult, [W.Btmp, Bsr8], [W.By])


def gla_post_out(k, W, c0, n):
    for d in range(8):
        psd, Bpsd = next_ps(k)
        for c in range(8):
            mm(k, psd[:, :n], W.wout[:, c, d * 128:(d + 1) * 128], W.y[:, c, :n], c == 0, c == 7,
               [W.Bwout, W.By], [Bpsd], signal=(c == 7))
        tt(k, k.h[:, d, c0:c0 + n], k.h[:, d, c0:c0 + n], psd[:, :n], ALU.add, [k.Bh, Bpsd], [k.Bh])


def gla_phase(k):
    P = k.P
    nc = k.nc
    win = k.din["gla_w_in"][0].rearrange("(kc p) f -> p kc f", p=128)
    wout_d = k.din["gla_w_out"][0].rearrange("(kc p) d -> p kc d", p=128)
    o_scr = nc.dram_tensor("gla_o_scr", [128, 8, NF], F32, kind="Internal").ap()
    cc_in = nc.dram_tensor("gla_cc_in", [128, 1024], F32, kind="Internal").ap()
    cc_out = nc.dram_tensor("gla_cc_out", [256, 1024], F32, kind="Internal").ap()
    Bscr, Bccin, Bccout = Buf("o_scr"), Buf("ccin"), Buf("ccout")
    k.ps_n = 6
    k.ps_rr = 0
    k.pso = [k.ps[6], k.ps[7]]
    k.Bpso = [k.psb[6], k.psb[7]]
    with ExitStack() as st:
        k.qtl_scr = nc.dram_tensor("gla_qtl_scr", [128, 4, NF], BF16, kind="Internal").ap()
        k.Bqtl = Buf("qtl_scr")
        osm, Bosm = _sb(k, st, "g_osm", [128, 8, 144], F32)
        W = NS()
        W.S, _ = _sb(k, st, "g_S", [128, 4, 256], F32)
        W.Sbf, _ = _sb(k, st, "g_Sbf", [128, 4, 256], BF16)
        W.BS = [Buf(f"S{h}") for h in range(4)]
        W.BSbf = [Buf(f"Sbf{h}") for h in range(4)]
        W.ecum, W.Becum = _sb(k, st, "g_ecum", [128, 33, 4], F32)
        Sloc, BSloc = _sb(k, st, "g_Sloc", [128, 4, 256], F32)
        with ExitStack() as s0:
            rstd, Brstd = _sb(k, s0, "g_rstd", [128, T], F32)
            sq0, Bsq0 = _sb(k, s0, "g_sq0", [128, 512], F32)
            sq1, Bsq1 = _sb(k, s0, "g_sq1", [128, 512], F32)
            make_xn(k, k.normw["mix_0"], rstd, Brstd, [sq0, sq1], [Bsq0, Bsq1])
            P.barrier()
            P.emit()
        with ExitStack() as p1:
            W.wq, W.Bwq = _sb(k, p1, "g_wq", [128, 8, 512], BF16)
            W.wk, W.Bwk = _sb(k, p1, "g_wk", [128, 8, 512], BF16)
            W.wv, W.Bwv = _sb(k, p1, "g_wv", [128, 8, 1024], BF16)
            W.wgd, W.Bwgd = _sb(k, p1, "g_wgd", [128, 8, 16], BF16)
            W.wgu, W.Bwgu = _sb(k, p1, "g_wgu", [32, 512], F32)
            W.par = 0
            W.sets = []
            for i in range(2):
                Z = NS()
                Z.gd, Z.Bgd = _sb(k, p1, f"g_gd{i}", [32, 128], F32)
                Z.graw, Z.Bgraw = _sb(k, p1, f"g_graw{i}", [128, 512], F32)
                Z.eW, Z.BeW = _sb(k, p1, f"g_eW{i}", [128, 512], F32)
                Z.egc, Z.Begc = _sb(k, p1, f"g_egc{i}", [128, 4, 128], F32)
                Z.engc, Z.Bengc = _sb(k, p1, f"g_engc{i}", [128, 4, 128], F32)
                Z.qg, Z.Bqg = _sb(k, p1, f"g_qg{i}", [128, 4, 128], BF16)
                Z.kg, Z.Bkg = _sb(k, p1, f"g_kg{i}", [128, 4, 128], BF16)
                Z.kd, Z.Bkd = _sb(k, p1, f"g_kd{i}", [128, 512], BF16)
                Z.vt, Z.Bvt = _sb(k, p1, f"g_vt{i}", [128, 1024], BF16)
                Z.atm, Z.Batm = _sb(k, p1, f"g_atm{i}", [128, 4, 128], BF16)
                Z.qt, Z.Bqt = _sb(k, p1, f"g_qt{i}", [128, 4, 128], BF16)
                W.sets.append(Z)
            osb = [_sb(k, p1, f"g_osb{i}", [128, 8, 128], F32) for i in range(2)]
            dma(k, W.wq[:, :, :], win[:, :, 0:512], [k.Bw], [W.Bwq], q="gpsimd")
            dma(k, W.wk[:, :, :], win[:, :, 512:1024], [k.Bw], [W.Bwk], q="gpsimd")
            dma(k, W.wv[:, :, :], win[:, :, 1024:2048], [k.Bw], [W.Bwv], q="gpsimd")
            dma(k, W.wgd[:, :, :], win[:, :, 3072:3088], [k.Bw], [W.Bwgd], q="gpsimd")
            dma(k, W.wgu[0:16, :], k.din["gla_w_gate_up"][0], [k.Bw], [W.Bwgu], q="sync")
            dma(k, W.wgu[16:17, :], k.din["gla_b_gate"][0:1, :], [k.Bw], [W.Bwgu], q="sync")
            for Z in W.sets:
                P.op("vector", lambda e, Z=Z: e.memset(Z.gd[:, :], 1.0), writes=[Z.Bgd])
            P.op("vector", lambda e: e.memset(W.S[:, :, :], 0.0), writes=W.BS)
            P.op("vector", lambda e: e.memset(W.Sbf[:, :, :], 0.0), writes=W.BSbf)
            P.op("vector", lambda e: e.memset(W.ecum[:, :, :], 1.0), writes=[W.Becum])
            gla_subtile(k, W, C_META, NMETA, 1, NMETA, osm[:, :, 0:NMETA], Bosm)
            for h in range(4):
                ts(k, W.S[:, h, :], W.S[:, h, :], k.fA, None, ALU.mult, None, [W.BS[h], k.Bconst], [W.BS[h]])
                act(k, W.Sbf[:, h, :], W.S[:, h, :], AF.Copy, [W.BS[h]], [W.BSbf[h]])
            for sti in range(16):
                ob, Bob = osb[sti % 2]
                gla_subtile(k, W, sti * 128, 128, 2, 64, ob, Bob, ecum_b0=2 * sti)
                dma(k, o_scr[:, :, sti * 128:(sti + 1) * 128], ob[:, :, :], [Bob], [Bscr], q="sync")
            for h in range(4):
                dma(k, cc_in[:, h * 256:(h + 1) * 256], W.S[:, h, :], [W.BS[h]], [Bccin], q="gpsimd")
            P.dma("gpsimd", lambda e: e.collective_compute("AllGather", ALU.bypass,
                                                           replica_groups=[[0, 1], [2, 3], [4, 5], [6, 7]],
                                                           ins=[cc_in], outs=[cc_out]),
                  reads=[Bccin], writes=[Bccout], inc=1)
            for h in range(4):
                cp(k, Sloc[:, h, :], W.S[:, h, :], [W.BS[h]], [BSloc])
            for sq_i in range(4):
                for h in range(4):
                    dma(k, W.S[:, h, :], k.din["state"][sq_i, h], [k.Bw], [W.BS[h]], q="sync")
                    act(k, W.Sbf[:, h, :], W.S[:, h, :], AF.Copy, [W.BS[h]], [W.BSbf[h]])
                cs = NMETA + 32 * sq_i
                gla_subtile(k, W, C_SAMP + 32 * sq_i, 32, 1, 32, osm[:, :, cs:cs + 32], Bosm)
                for h in range(4):
                    dma(k, k.dout["o_ss"][sq_i, h], W.S[:, h, :], [W.BS[h]], [k.Bw], q="sync")
            P.barrier()
            P.emit()
        with ExitStack() as p2:
            W2 = NS()
            W2.wr, W2.Bwr = _sb(k, p2, "g_wr", [128, 8, 1024], BF16)
            W2.wout, W2.Bwout = _sb(k, p2, "g_wout", [128, 8, 1024], BF16)
            W2.rstd, _ = _sb(k, p2, "g2_rstd", [128, 4, 256], F32)
            W2.Brstd = [Buf(f"g2_rstd{h}") for h in range(4)]
            W2.tmp, W2.Btmp = _sb(k, p2, "g2_tmp", [128, 256], F32)
            W2.y, W2.By = _sb(k, p2, "g2_y", [128, 8, 256], BF16)
            W2.sq8, W2.Bsq8 = _sb(k, p2, "g2_sq8", [128, 8, 256], BF16)
            W2.Brstd1 = Buf("g2_rstd_all")
            sr8s = [_sb(k, p2, f"g2_sr8{i}", [128, 8, 256], BF16) for i in range(2)]
            o2 = [_sb(k, p2, f"g2_o{i}", [128, 8, 256], F32) for i in range(2)]
            qts = [_sb(k, p2, f"g2_qtl{i}", [128, 4, 256], BF16) for i in range(2)]
            Srv, BSrv = _sb(k, p2, "g2_Srv", [128, 4, 256], F32)
            Srb, BSrb = _sb(k, p2, "g2_Srb", [128, 4, 256], BF16)
            dma(k, W2.wr[:, :, :], win[:, :, 2048:3072], [k.Bw], [W2.Bwr], q="gpsimd")
            dma(k, W2.wout[:, :, :], wout_d, [k.Bw], [W2.Bwout], q="gpsimd")
            dma(k, Srv[:, :, :], cc_out[0:128, :].rearrange("p (h v) -> p h v", h=4), [Bccout], [BSrv], q="sync")
            ts(k, Srv[:, :, :], Srv[:, :, :], k.fB, None, ALU.mult, None, [BSrv, k.Bconst], [BSrv])
            cp(k, Srb[:, :, :], Srv[:, :, :], [BSrv], [BSrb])
            for h in range(4):
                stt(k, Srv[:, h, :], Srv[:, h, :], W.ecum[:, 32, h:h + 1], Sloc[:, h, :], ALU.mult, ALU.add,
                    [BSrv, W.Becum, BSloc], [BSrv])
            dma(k, k.dout["o_sp"].rearrange("h p v -> p h v"), Srv[:, :, :], [BSrv], [k.Bw], q="sync")
            tiles2 = [(C_META, 144, False)] + [(ti * 256, 256, True) for ti in range(8)]

            def load2(i):
                c0, n, corr = tiles2[i]
                if corr:
                    ob, Bob = o2[i % 2]
                    qt, Bqt = qts[i % 2]
                    dma(k, ob[:, :, :], o_scr[:, :, c0:c0 + 256], [Bscr], [Bob], q="sync")
                    dma(k, qt[:, :, :], k.qtl_scr[:, :, c0:c0 + 256], [k.Bqtl], [Bqt], q="sync")

            load2(0)
            gla_post_r(k, W2, tiles2[0][0], tiles2[0][1], *sr8s[0])
            for i, (c0, n, corr) in enumerate(tiles2):
                if i + 1 < len(tiles2):
                    load2(i + 1)
                if corr:
                    ob, Bob = o2[i % 2]
                    qt, Bqt = qts[i % 2]
                    for h in range(4):
                        for vc in range(2):
                            c = h * 2 + vc
                            psc, Bpsc = next_ps(k)
                            mm(k, psc[:, :256], Srb[:, h, vc * 128:(vc + 1) * 128], qt[:, h, :], True, True,
                               [BSrb, Bqt], [Bpsc])
                            tt(k, ob[:, c, :], ob[:, c, :], psc[:, :256], ALU.add, [Bob, Bpsc], [Bob])
                else:
                    ob, Bob = osm, Bosm
                gla_post_norm(k, W2, ob, Bob, n, *sr8s[i % 2])
                if i + 1 < len(tiles2):
                    gla_post_r(k, W2, tiles2[i + 1][0], tiles2[i + 1][1], *sr8s[(i + 1) % 2])
                gla_post_out(k, W2, c0, n)
            P.barrier()
            P.emit()
    k.ps_n = 8
    k.ps_rr = 0


SCALE = 192.0 ** -0.5


def attend(k, M, qlat, qr, nq, keysets, olat, Bolat, Bq, poset):
    po0, po1, pden, Bpo = poset
    nks = len(keysets)

    def scores(i):
        (klT, krT, ktok, nk, bias, q0, maskfix, Bk) = keysets[i]
        pS, BpS = next_ps(k)
        q_0, q_1, q_r = qlat(0), qlat(1), qr
        if q0 > 0:
            q_0, q_1, q_r = q_0[:, q0:nq], q_1[:, q0:nq], q_r[:, q0:nq]
        mm(k, pS[:nk, q0:nq], klT(0), q_0, True, False, Bk + Bq, [BpS], signal=False)
        mm(k, pS[:nk, q0:nq], klT(1), q_1, False, False, Bk + Bq, [BpS], signal=False)
        mm(k, pS[:nk, q0:nq], krT, q_r, False, True, Bk + Bq, [BpS], signal=True)
        return pS, BpS

    def finish(i, pS, BpS):
        (klT, krT, ktok, nk, bias, q0, maskfix, Bk) = keysets[i]
        pT, BpT = M.pT[i % 3]
        if bias is None:
            act(k, pT[:nk, q0:nq], pS[:nk, q0:nq], AF.Exp, [BpS], [BpT], scale=SCALE)
        else:
            act(k, pT[:nk, q0:nq], pS[:nk, q0:nq], AF.Exp, [BpS, k.Bconst], [BpT], scale=SCALE, bias=bias)
        if maskfix:
            k.P.op("vector", lambda e, pT=pT, q0=q0: e.memset(pT[64:128, q0:q0 + 64], 0.0), writes=[BpT])
        first, last = (i == 0), (i == nks - 1)
        mm(k, po0[:, q0:nq], ktok[:nk, 0:128], pT[:nk, q0:nq], first, last, Bk + [BpT], [Bpo[0]], signal=False)
        mm(k, po1[:, q0:nq], ktok[:nk, 128:256], pT[:nk, q0:nq], first, last, Bk + [BpT], [Bpo[1]], signal=True)
        acc, Bacc = M.acc[0]
        tt(k, acc[:nk, q0:nq], acc[:nk, q0:nq], pT[:nk, q0:nq], ALU.add, [Bacc, BpT], [Bacc],
           eng="vector")

    k.P.op("vector", lambda e: e.memset(M.acc[0][0][:, :nq], 0.0), writes=[M.acc[0][1]])
    LOOK = 2
    pend = [scores(i) for i in range(min(LOOK, nks))]
    for i in range(nks):
        if i + LOOK < nks:
            pend.append(scores(i + LOOK))
        finish(i, *pend.pop(0))
    mm(k, pden[:, :nq], k.ones_f[:, :], M.acc[0][0][:, :nq], True, True, [M.acc[0][1], k.Bconst], [Bpo[2]], signal=True)
    act(k, M.oraw[:, 0, :nq], po0[:, :nq], AF.Copy, [Bpo[0]], [M.Boraw])
    act(k, M.oraw[:, 1, :nq], po1[:, :nq], AF.Copy, [Bpo[1]], [M.Boraw])
    act(k, M.rden[:, :nq], pden[:, :nq], AF.Copy, [Bpo[2]], [M.Brden])
    k.P.op("vector", lambda e: e.reciprocal(out=M.rden[:, :nq], in_=M.rden[:, :nq]), reads=[M.Brden], writes=[M.Brden])
    tt(k, olat[:, 0, :nq], M.oraw[:, 0, :nq], M.rden[:, :nq], ALU.mult, [M.Boraw, M.Brden], [Bolat])
    tt(k, olat[:, 1, :nq], M.oraw[:, 1, :nq], M.rden[:, :nq], ALU.mult, [M.Boraw, M.Brden], [Bolat])


def mla_phase(k):
    P = k.P
    nc = k.nc
    wdn_d = k.din["mla_w_down"][0].rearrange("(kc p) f -> p kc f", p=128)
    wuq_d = k.din["mla_w_uq"][0].rearrange("(kc p) f -> p kc f", p=128)
    wuv_d = k.din["mla_w_uv"][0].rearrange("(rc p) h v -> p rc h v", p=128)
    wo_d = k.din["mla_w_out"][0].rearrange("(c p) d -> p c d", p=128)
    h_scr = nc.dram_tensor("mla_h_scr", [128, 8, T], F32, kind="Internal").ap()
    cc_ins = [nc.dram_tensor(f"mla_cc_in{i}", [128, 2560], BF16, kind="Internal").ap() for i in range(4)]
    cc_outs = [nc.dram_tensor(f"mla_cc_out{i}", [256, 2560], BF16, kind="Internal").ap() for i in range(4)]
    Bhs, Bccin, Bccout = Buf("h_scr"), Buf("ccin2"), Buf("ccout2")
    k.ps_n = 5
    k.ps_rr = 0
    with ExitStack() as st:
        M = NS()
        Bprev = Buf("prev")
        kropeT, BkropeT = _sb(k, st, "m_kropeT", [128, T], BF16)
        pkropeT, _ = _sb(k, st, "m_pkropeT", [128, NF], BF16)
        qlat_s, Bqs = _sb(k, st, "m_qlat_s", [128, 2, 4, 8, 32], BF16)
        qr_s, _ = _sb(k, st, "m_qr_s", [128, 4, 8, 32], BF16)
        M.pT = [_sb(k, st, f"m_pT{i}", [128, 512], BF16) for i in range(3)]
        M.oraw, M.Boraw = _sb(k, st, "m_oraw", [128, 2, 512], F32)
        M.acc = [_sb(k, st, f"m_acc{i}", [128, 512], F32) for i in range(2)]
        M.rden, M.Brden = _sb(k, st, "m_rden", [128, 512], F32)
        olat, Bolat = _sb(k, st, "m_olat", [128, 2, 512], BF16)
        olat2, Bolat2 = _sb(k, st, "m_olat2", [128, 2, 512], BF16)
        t1, Bt1 = _sb(k, st, "m_t1", [32, 512], F32)
        t2, Bt2 = _sb(k, st, "m_t2", [32, 512], F32)
        wuq, Bwuq = _sb(k, st, "m_wuq", [128, 3, 1536], BF16)
        wukT, BwukT = _sb(k, st, "m_wukT", [128, 8, 256], BF16)
        wuv, Bwuv = _sb(k, st, "m_wuv", [128, 2, 8, 128], BF16)
        sw = ExitStack()
        wdn, Bwdn = _sb(k, sw, "m_wdn", [128, 8, 704], BF16)
        dma(k, wdn[:, :, :], wdn_d, [k.Bw], [Bwdn], q="gpsimd")
        dma(k, wuq[:, :, :], wuq_d, [k.Bw], [Bwuq], q="gpsimd")
        dma(k, wukT[:, :, :], k.din["w_ukT"], [k.Bw], [BwukT], q="gpsimd")
        dma(k, wuv[:, :, :, :], wuv_d, [k.Bw], [Bwuv], q="gpsimd")
        P.op("vector", lambda e: e.memset(kropeT[64:128, :], 0.0), writes=[BkropeT])
        P.op("vector", lambda e: e.memset(pkropeT[64:128, :], 0.0), writes=[Bprev])
        with ExitStack() as s0:
            rstd, Brstd = _sb(k, s0, "m_rstd", [128, T], F32)
            sq0, Bsq0 = _sb(k, s0, "m_sq0", [128, 512], F32)
            sq1, Bsq1 = _sb(k, s0, "m_sq1", [128, 512], F32)
            for c in range(8):
                dma(k, h_scr[:, c, :], k.h[:, c, :], [k.Bh], [Bhs], q="sync")
            make_xn(k, k.normw["mix_1"], rstd, Brstd, [sq0, sq1], [Bsq0, Bsq1])
            P.barrier()
            P.emit()
        flat = k.h[:, :, :].rearrange("p c t -> p (c t)")
        off = [0]

        def carve(nf32):
            a = flat[:, off[0]:off[0] + nf32]
            off[0] += nf32
            assert off[0] <= 8 * T
            return a

        tab = carve(2 * T).rearrange("p (a t) -> p a t", a=2)
        Btab = Buf("tab")
        klatT = carve(T).bitcast(BF16).rearrange("p (a t) -> p a t", a=2)
        BklatT = Buf("klatT")
        ktok = carve(21 * 128).bitcast(BF16).rearrange("p (a r) -> p a r", a=21)
        Bktok = Buf("ktok")
        cqn = carve(3 * T // 2).bitcast(BF16).rearrange("p (a t) -> p a t", a=3)
        Bcqn = Buf("cqn")
        pklatT = carve(2048).bitcast(BF16).rearrange("p (a t) -> p a t", a=2)
        pktok = carve(2048).bitcast(BF16).rearrange("p (a r) -> p a r", a=16)
        dma(k, tab[0:32, :, :], k.din["rope"], [k.Bw], [Btab], q="sync")
        ov = k.xn
        GROUPS = [(i * 128, 128) for i in range(16)] + [(C_META, 16)] + [(C_SAMP + 32 * i, 32) for i in range(4)]

        def rope(psa, Bpsa, psb, Bpsb, dst, Bdst, t0, tn):
            cos, sin = tab[0:32, 0, t0:t0 + tn], tab[0:32, 1, t0:t0 + tn]
            tt(k, t1[:, :tn], psa[0:32, :tn], cos, ALU.mult, [Bpsa, Btab], [Bt1])
            tt(k, t2[:, :tn], psb[0:32, :tn], sin, ALU.mult, [Bpsb, Btab], [Bt2])
            tt(k, dst[0:32, t0:t0 + tn], t1[:, :tn], t2[:, :tn], ALU.subtract, [Bt1, Bt2], [Bdst])
            tt(k, t1[:, :tn], psb[0:32, :tn], cos, ALU.mult, [Bpsb, Btab], [Bt1])
            tt(k, t2[:, :tn], psa[0:32, :tn], sin, ALU.mult, [Bpsa, Btab], [Bt2])
            tt(k, dst[32:64, t0:t0 + tn], t1[:, :tn], t2[:, :tn], ALU.add, [Bt1, Bt2], [Bdst])

        with ExitStack() as sa:
            cq, Bcq = _sb(k, sa, "m_cq", [128, 3, 512], F32)
            ckv, Bckv = _sb(k, sa, "m_ckv", [128, 2, 512], F32)
            klf, Bklf = _sb(k, sa, "m_klf", [128, 2, 512], F32)
            krf, Bkrf = _sb(k, sa, "m_krf", [64, T], F32)
            sqa = [_sb(k, sa, f"m_sqa{i}", [128, 512], F32) for i in range(2)]
            rs, Brs = _sb(k, sa, "m_rs", [128, 512], F32)
            stg = [_sb(k, sa, f"m_stg{i}", [128, 256], F32) for i in range(2)]
            stgr = [_sb(k, sa, f"m_stgr{i}", [128, 64], F32) for i in range(2)]
            for (t0, tn) in TILES:
                pss = []
                for c in range(5):
                    ps, Bps = next_ps(k)
                    for kc in range(8):
                        mm(k, ps[:, :tn], wdn[:, kc, c * 128:(c + 1) * 128], k.xn[:, kc, t0:t0 + tn], kc == 0, kc == 7,
                           [Bwdn, k.Bxn], [Bps], signal=(kc == 7))
                    pss.append((ps, Bps))
                for c in range(3):
                    act(k, cq[:, c, :tn], pss[c][0][:, :tn], AF.Copy, [pss[c][1]], [Bcq])
                for c in range(2):
                    act(k, ckv[:, c, :tn], pss[3 + c][0][:, :tn], AF.Copy, [pss[3 + c][1]], [Bckv])
                rms_rstd(k, cq, Bcq, 3, 384, rs, Brs, [sqa[0][0], sqa[1][0]], [sqa[0][1], sqa[1][1]],
                         tiles=[(0, tn)])
                psa, Bpsa = next_ps(k)
                psb, Bpsb = next_ps(k)
                for kc in range(8):
                    mm(k, psa[0:32, :tn], wdn[:, kc, 640:672], k.xn[:, kc, t0:t0 + tn], kc == 0, kc == 7,
                       [Bwdn, k.Bxn], [Bpsa], signal=(kc == 7))
                for kc in range(8):
                    mm(k, psb[0:32, :tn], wdn[:, kc, 672:704], k.xn[:, kc, t0:t0 + tn], kc == 0, kc == 7,
                       [Bwdn, k.Bxn], [Bpsb], signal=(kc == 7))
                for c in range(3):
                    stt(k, cqn[:, c, t0:t0 + tn], cq[:, c, :tn], k.qnorm[:, c:c + 1], rs[:, :tn], ALU.mult, ALU.mult,
                        [Bcq, Brs, k.Bconst], [Bcqn])
                rms_rstd(k, ckv, Bckv, 2, 256, rs, Brs, [sqa[0][0], sqa[1][0]], [sqa[0][1], sqa[1][1]],
                         tiles=[(0, tn)])
                for c in range(2):
                    stt(k, klf[:, c, :tn], ckv[:, c, :tn], k.kvnorm[:, c:c + 1], rs[:, :tn], ALU.mult, ALU.mult,
                        [Bckv, Brs, k.Bconst], [Bklf])
                    cp(k, klatT[:, c, t0:t0 + tn], klf[:, c, :tn], [Bklf], [BklatT])
                rope(psa, Bpsa, psb, Bpsb, krf, Bkrf, t0, tn)
                cp(k, kropeT[0:64, t0:t0 + tn], krf[:, t0:t0 + tn], [Bkrf], [BkropeT])
                for gi, (g0, gn) in enumerate(GROUPS):
                    if not (t0 <= g0 < t0 + tn):
                        continue
                    pst, Bpst = next_ps(k)
                    for c in range(2):
                        k.P.op("tensor", lambda e, c=c, g0=g0, gn=gn, pst=pst, t0=t0: e.transpose(
                            pst[:gn, c * 128:(c + 1) * 128], klf[:, c, g0 - t0:g0 - t0 + gn], k.ident[:, :]),
                            reads=[Bklf, k.Bconst], writes=[Bpst])
                    sg_, Bsg_ = stg[gi % 2]
                    cp(k, sg_[:gn, :], pst[:gn, 0:256], [Bpst], [Bsg_])
                    act(k, ktok[:gn, gi, :], sg_[:gn, :], AF.Copy, [Bsg_], [Bktok])
                    dma(k, k.dout["o_lat"][g0:g0 + gn, :], sg_[:gn, :], [Bsg_], [k.Bw], q="sync")
                    pst2, Bpst2 = next_ps(k)
                    k.P.op("tensor", lambda e, g0=g0, gn=gn, pst2=pst2: e.transpose(
                        pst2[:gn, 0:64], krf[0:64, g0:g0 + gn], k.ident[0:64, 0:64]),
                        reads=[Bkrf, k.Bconst], writes=[Bpst2])
                    sr_, Bsr_ = stgr[gi % 2]
                    cp(k, sr_[:gn, :], pst2[:gn, 0:64], [Bpst2], [Bsr_])
                    dma(k, k.dout["o_rope"][g0:g0 + gn, :], sr_[:gn, :], [Bsr_], [k.Bw], q="sync")
                if t0 == 1536:
                    dma(k, cc_ins[0][:, 0:2048], klatT[:, 0, 0:NF], [BklatT], [Bccin], q="gpsimd", asyn=True)
                    dma(k, cc_ins[1][:, 0:2048], klatT[:, 1, 0:NF], [BklatT], [Bccin], q="gpsimd", asyn=True)
                    dma(k, cc_ins[2][:, 0:2560].rearrange("p (a r) -> p a r", a=10), ktok[:, 0:10, :], [Bktok], [Bccin], q="gpsimd", asyn=True)
                    dma(k, cc_ins[3][:, 0:1536].rearrange("p (a r) -> p a r", a=6), ktok[:, 10:16, :], [Bktok], [Bccin], q="gpsimd", asyn=True)
                    dma(k, cc_ins[0][0:64, 2048:2560], kropeT[0:64, 0:512], [BkropeT], [Bccin], q="gpsimd", asyn=True)
                    dma(k, cc_ins[1][0:64, 2048:2560], kropeT[0:64, 512:1024], [BkropeT], [Bccin], q="gpsimd", asyn=True)
                    dma(k, cc_ins[3][0:64, 1536:2560], kropeT[0:64, 1024:2048], [BkropeT], [Bccin], q="gpsimd", asyn=True)
                    for ci in range(4):
                        P.dma("gpsimd", lambda e, ci=ci: e.collective_compute(
                            "AllGather", ALU.bypass, replica_groups=[[0, 1], [2, 3], [4, 5], [6, 7]],
                            ins=[cc_ins[ci]], outs=[cc_outs[ci]]), reads=[Bccin], writes=[Bccout], inc=1, asyn=True)
                    dma(k, pklatT[:, 0, :], cc_outs[0][0:128, 0:2048], [Bccout], [Bprev], q="gpsimd", asyn=True)
                    dma(k, pklatT[:, 1, :], cc_outs[1][0:128, 0:2048], [Bccout], [Bprev], q="gpsimd", asyn=True)
                    dma(k, pktok[:, 0:10, :], cc_outs[2][0:128, 0:2560].rearrange("p (a r) -> p a r", a=10), [Bccout], [Bprev], q="gpsimd", asyn=True)
                    dma(k, pktok[:, 10:16, :], cc_outs[3][0:128, 0:1536].rearrange("p (a r) -> p a r", a=6), [Bccout], [Bprev], q="gpsimd", asyn=True)
                    dma(k, pkropeT[0:64, 0:512], cc_outs[0][0:64, 2048:2560], [Bccout], [Bprev], q="gpsimd", asyn=True)
                    dma(k, pkropeT[0:64, 512:1024], cc_outs[1][0:64, 2048:2560], [Bccout], [Bprev], q="gpsimd", asyn=True)
                    dma(k, pkropeT[0:64, 1024:2048], cc_outs[3][0:64, 1536:2560], [Bccout], [Bprev], q="gpsimd", asyn=True)
            P.barrier(skip_async=True)
            P.emit()
        sw.close()

        if k.mla_stop == "A":
            return
        def own_set(g, q0=0, maskfix=False):
            g0, gn = GROUPS[g]
            return (lambda rc, g0=g0, gn=gn: klatT[:, rc, g0:g0 + gn], kropeT[:, g0:g0 + gn], ktok[:, g, :], gn, None,
                    q0, maskfix, [BklatT, BkropeT, Bktok])

        def prev_set(g):
            return (lambda rc, g=g: pklatT[:, rc, g * 128:(g + 1) * 128], pkropeT[:, g * 128:(g + 1) * 128],
                    pktok[:, g, :], 128, k.prevbias[:, 0:1], 0, False, [Bprev])

        scc = ExitStack()
        pasts = []
        pT_, BpT_ = _sb(k, scc, "m_pastT0", [128, 2, PAST], BF16)
        pk_, Bpk_ = _sb(k, scc, "m_ptok0", [128, 16, 256], BF16)
        pR_, BpR_ = _sb(k, scc, "m_pastR0", [128, PAST], BF16)
        P.op("vector", lambda e, pR_=pR_: e.memset(pR_[64:128, :], 0.0), writes=[BpR_])
        pasts.append((pT_, BpT_, pk_, Bpk_, pR_, BpR_))

        def load_past(s_i):
            pT_, BpT_, pk_, Bpk_, pR_, BpR_ = pasts[s_i % 2]
            dma(k, pT_[:, :, :], k.din["cache_latT"][s_i].rearrange("(a p) t -> p a t", p=128), [k.Bw], [BpT_],
                q="gpsimd")
            cl = k.din["cache_lat"][s_i].rearrange("(t p) r -> p t r", p=128)
            for q2 in range(2):
                dma(k, pk_[:, 8 * q2:8 * q2 + 8, :], cl[:, 8 * q2:8 * q2 + 8, :], [k.Bw], [Bpk_], q="gpsimd")
            dma(k, pR_[0:64, :], k.din["cache_ropeT"][s_i], [k.Bw], [BpR_], q="gpsimd")


        with ExitStack() as sc:
            qnope, Bqnope = _sb(k, sc, "m_qnope", [128, T], BF16)
            qr, Bqr = _sb(k, sc, "m_qr", [128, T], BF16)
            P.op("vector", lambda e: e.memset(qr[64:128, :], 0.0), writes=[Bqr])
            QR_ZERO = True
            qlat, Bqlat = _sb(k, sc, "m_qlat", [128, 2, T], BF16)
            k.ps_n = 3
            k.ps_rr = 0
            posets = [(k.ps[3], k.ps[4], k.ps[7], [k.psb[3], k.psb[4], k.psb[7]]),
                      (k.ps[5], k.ps[6], k.ps[7], [k.psb[5], k.psb[6], k.psb[7]])]
            olats = [(olat, Bolat), (olat2, Bolat2)]
            acnt = [0]

            def ov_out(h, c0, n, ol, Bol):
                psv, Bpsv = next_ps(k)
                for rc in range(2):
                    mm(k, psv[:, :n], wuv[:, rc, h, :], ol[:, rc, :n], rc == 0, rc == 1, [Bwuv, Bol], [Bpsv],
                       signal=(rc == 1))
                act(k, ov[:, h, c0:c0 + n], psv[:, :n], AF.Copy, [Bpsv], [k.Bxn])

            load_past(0)
            pend_ov = []
            Bqn_t = [Buf(f"qnope_t{i}") for i in range(len(TILES))]
            Bqr_t = [Buf(f"qr_t{i}") for i in range(len(TILES))]
            Bql_t = [Buf(f"qlat_t{i}") for i in range(len(TILES))]

            def q_stage(h, ti):
                t0, tn = TILES[ti]
                psn, Bpsn = next_ps(k)
                for kc in range(3):
                    mm(k, psn[:, :tn], wuq[:, kc, h * 192:h * 192 + 128], cqn[:, kc, t0:t0 + tn], kc == 0, kc == 2,
                       [Bwuq, Bcqn], [Bpsn], signal=(kc == 2))
                act(k, qnope[:, t0:t0 + tn], psn[:, :tn], AF.Copy, [Bpsn], [Bqn_t[ti]])
                psa, Bpsa = next_ps(k)
                for kc in range(3):
                    mm(k, psa[0:32, :tn], wuq[:, kc, h * 192 + 128:h * 192 + 160], cqn[:, kc, t0:t0 + tn], kc == 0,
                       kc == 2, [Bwuq, Bcqn], [Bpsa], signal=(kc == 2))
                psb, Bpsb = next_ps(k)
                for kc in range(3):
                    mm(k, psb[0:32, :tn], wuq[:, kc, h * 192 + 160:h * 192 + 192], cqn[:, kc, t0:t0 + tn], kc == 0,
                       kc == 2, [Bwuq, Bcqn], [Bpsb], signal=(kc == 2))
                rope(psa, Bpsa, psb, Bpsb, qr, Bqr_t[ti], t0, tn)
                for rc in range(2):
                    psl, Bpsl = next_ps(k)
                    mm(k, psl[:, :tn], wukT[:, h, rc * 128:(rc + 1) * 128], qnope[:, t0:t0 + tn], True, True,
                       [BwukT, Bqn_t[ti]], [Bpsl])
                    act(k, qlat[:, rc, t0:t0 + tn], psl[:, :tn], AF.Copy, [Bpsl], [Bql_t[ti]])

            q_stage(0, 0)
            for h in range(8):
                for qt in range(4):
                    q_stage(h, qt + 1)
                    c0 = qt * 512
                    ks = [own_set(16)]
                    ks += [own_set(4 * qt + j, q0=128 * j, maskfix=True) for j in range(1, 4)]
                    ks += [own_set(4 * qt, q0=0, maskfix=True)]
                    ks += [own_set(g) for g in range(4 * qt)]
                    ks += [prev_set(g) for g in range(16)]
                    ol, Bol = olats[acnt[0] % 2]
                    attend(k, M, lambda rc, c0=c0: qlat[:, rc, c0:c0 + 512], qr[:, c0:c0 + 512], 512, ks, ol, Bol,
                           [Bql_t[qt], Bqr_t[qt]], posets[acnt[0] % 2])
                    if pend_ov:
                        ov_out(*pend_ov.pop())
                    pend_ov.append((h, c0, 512, ol, Bol))
                    acnt[0] += 1
                for rc in range(2):
                    cp(k, qlat_s[:, rc, :, h, :], qlat[:, rc, C_SAMP:T].rearrange("p (s q) -> p s q", s=4), [Bql_t[4]], [Bqs])
                cp(k, qr_s[:, :, h, :], qr[:, C_SAMP:T].rearrange("p (s q) -> p s q", s=4), [Bqr_t[4]], [Bqs])
                if h + 1 < 8:
                    q_stage(h + 1, 0)
                ol, Bol = olats[acnt[0] % 2]
                attend(k, M, lambda rc: qlat[:, rc, C_META:C_META + 16], qr[:, C_META:C_META + 16], 16, [own_set(16)],
                       ol, Bol, [Bql_t[4], Bqr_t[4]], posets[acnt[0] % 2])
                if pend_ov:
                    ov_out(*pend_ov.pop())
                pend_ov.append((h, C_META, 16, ol, Bol))
                acnt[0] += 1
            if pend_ov:
                ov_out(*pend_ov.pop())
            P.barrier()
            P.emit()
        if k.mla_stop == "C":
            return
        with ExitStack() as sc2:
            for i in range(1, 2):
                pT_, BpT_ = _sb(k, sc2, f"m_pastT{i}", [128, 2, PAST], BF16)
                pk_, Bpk_ = _sb(k, sc2, f"m_ptok{i}", [128, 16, 256], BF16)
                pR_, BpR_ = _sb(k, sc2, f"m_pastR{i}", [128, PAST], BF16)
                P.op("vector", lambda e, pR_=pR_: e.memset(pR_[64:128, :], 0.0), writes=[BpR_])
                pasts.append((pT_, BpT_, pk_, Bpk_, pR_, BpR_))
            k.ps_n = 3
            k.ps_rr = 0
            posets2 = [(k.ps[3], k.ps[4], k.ps[7], [k.psb[3], k.psb[4], k.psb[7]]),
                       (k.ps[5], k.ps[6], k.ps[7], [k.psb[5], k.psb[6], k.psb[7]])]
            olats2 = [(olat, Bolat), (olat2, Bolat2)]
            skl, Bskl = _sb(k, sc2, "m_skl", [128, 2, NSAMP], BF16)
            skt, Bskt = _sb(k, sc2, "m_skt", [128, 4, 256], BF16)
            cp(k, skl[:, :, :], klatT[:, :, C_SAMP:T], [BklatT], [Bskl])
            cp(k, skt[:, :, :], ktok[:, 17:21, :], [Bktok], [Bskt])
            for c in range(8):
                dma(k, k.h[:, c, :], h_scr[:, c, :], [Bhs], [k.Bh, BklatT, Bktok], q="sync")

            for s_i in range(4):
                if s_i + 1 < 4:
                    load_past(s_i + 1)
                pastT, BpastT, ptok, Bptok, pastR, BpastR = pasts[s_i % 2]
                ks = [(lambda rc, g=g: pastT[:, rc, g * 128:(g + 1) * 128], pastR[:, g * 128:(g + 1) * 128],
                       ptok[:, g, :], 128, None, 0, False, [BpastT, BpastR, Bptok]) for g in range(16)]
                ks += [(lambda rc, s_i=s_i: skl[:, rc, 32 * s_i:32 * s_i + 32],
                        kropeT[:, C_SAMP + 32 * s_i:C_SAMP + 32 * s_i + 32], skt[:, s_i, :], 32, None, 0, False,
                        [Bskl, BkropeT, Bskt])]
                ol, Bol = olats2[s_i % 2]
                attend(k, M, lambda rc, s_i=s_i: qlat_s[:, rc, s_i].rearrange("p h q -> p (h q)"),
                       qr_s[:, s_i].rearrange("p h q -> p (h q)"), 256, ks, ol, Bol, [Bqs], posets2[s_i % 2])
                psv, Bpsv = next_ps(k)
                for h in range(8):
                    for rc in range(2):
                        mm(k, psv[:, h * 32:(h + 1) * 32], wuv[:, rc, h, :], ol[:, rc, h * 32:(h + 1) * 32], rc == 0,
                           rc == 1, [Bwuv, Bol], [Bpsv], signal=(rc == 1 and h == 7))
                act(k, ov[:, :, C_SAMP + 32 * s_i:C_SAMP + 32 * s_i + 32],
                    psv[:, 0:256].rearrange("p (h q) -> p h q", h=8), AF.Copy, [Bpsv], [k.Bxn])
            P.barrier()
            P.emit()
        if k.mla_stop in ("C2", "C2prep", "C2dma", "C2t2"):
            return
        scc.close()
        k.ps_n = 8
        k.ps_rr = 0
        with ExitStack() as sd:
            wo, Bwo = _sb(k, sd, "m_wo", [128, 8, D], BF16)
            dma(k, wo[:, :, :], wo_d, [k.Bw], [Bwo], q="gpsimd")
            for d in range(8):
                for (t0, tn) in TILES:
                    psd, Bpsd = next_ps(k)
                    for c in range(8):
                        mm(k, psd[:, :tn], wo[:, c, d * 128:(d + 1) * 128], ov[:, c, t0:t0 + tn], c == 0, c == 7,
                           [Bwo, k.Bxn], [Bpsd], signal=(c == 7))
                    tt(k, k.h[:, d, t0:t0 + tn], k.h[:, d, t0:t0 + tn], psd[:, :tn], ALU.add, [k.Bh, Bpsd], [k.Bh])
            P.barrier()
            P.emit()


def final_phase(k):
    P = k.P
    with ExitStack() as st:
        rstd, Brstd = _sb(k, st, "f_rstd", [128, T], F32)
        sq0, Bsq0 = _sb(k, st, "f_sq0", [128, 512], F32)
        sq1, Bsq1 = _sb(k, st, "f_sq1", [128, 512], F32)
        yb = [_sb(k, st, f"f_y{i}", [128, 512], F32) for i in range(2)]
        nw = k.normw["final"]
        i = 0
        for (t0, tn) in TILES:
            rms_rstd(k, k.h, k.Bh, 8, D, rstd, Brstd, [sq0, sq1], [Bsq0, Bsq1], tiles=[(t0, tn)])
            for c in range(8):
                y, By = yb[i % 2]
                i += 1
                stt(k, y[:, :tn], k.h[:, c, t0:t0 + tn], nw[:, c:c + 1], rstd[:, t0:t0 + tn], ALU.mult, ALU.mult,
                    [k.Bh, Brstd, k.Bconst], [By])
                dma(k, k.dout["o_y"][:, c, t0:t0 + tn], y[:, :tn], [By], [k.Bw], q="sync" if i % 2 else "gpsimd")
        P.barrier()
        P.emit()


W_NAMES = ["ffn1_norm", "ffn1_w_gate", "ffn1_w_up", "ffn1_w_down", "mix_norm", "gla_w_in", "gla_w_gate_up",
           "gla_b_gate", "gla_head_norm", "gla_w_out", "mla_w_down", "mla_q_norm", "mla_w_uq", "mla_kv_norm",
           "mla_w_uk", "mla_w_uv", "mla_w_out", "ffn2_norm", "ffn2_w_gate", "ffn2_w_up", "ffn2_w_down",
           "final_norm"]
W_SHAPES = {
    "ffn1_norm": [2, D], "ffn1_w_gate": [2, D, DFF], "ffn1_w_up": [2, D, DFF], "ffn1_w_down": [2, DFF, D],
    "mix_norm": [2, D], "gla_w_in": [1, D, 3088], "gla_w_gate_up": [1, 16, 512], "gla_b_gate": [1, 512],
    "gla_head_norm": [1, 256], "gla_w_out": [1, D, D], "mla_w_down": [1, D, 704], "mla_q_norm": [1, 384],
    "mla_w_uq": [1, 384, 1536], "mla_kv_norm": [1, 256], "mla_w_uk": [1, 256, 8, 128],
    "mla_w_uv": [1, 256, 8, 128], "mla_w_out": [1, D, D], "ffn2_norm": [2, D], "ffn2_w_gate": [2, D, DFF],
    "ffn2_w_up": [2, D, DFF], "ffn2_w_down": [2, DFF, D], "final_norm": [D],
}


def build(stage=99, mla_stop=None):
    nc = bass.Bass("TRN2", target_bir_lowering=False)
    k = K()
    k.mla_stop = mla_stop
    k.nc = nc
    k.din = {}
    for n in W_NAMES:
        k.din[n] = nc.dram_tensor(n, W_SHAPES[n], F32, kind="ExternalInput").ap()
    k.din["xT"] = nc.dram_tensor("xT", [128, 8, T], F32, kind="ExternalInput").ap()
    k.din["consts"] = nc.dram_tensor("consts", [128, NCONST], F32, kind="ExternalInput").ap()
    k.din["state"] = nc.dram_tensor("state", [4, 4, 128, 256], F32, kind="ExternalInput").ap()
    k.din["rope"] = nc.dram_tensor("rope", [32, 2, T], F32, kind="ExternalInput").ap()
    k.din["w_ukT"] = nc.dram_tensor("w_ukT", [128, 8, 256], F32, kind="ExternalInput").ap()
    k.din["cache_lat"] = nc.dram_tensor("cache_lat", [4, PAST, 256], F32, kind="ExternalInput").ap()
    k.din["cache_ropeT"] = nc.dram_tensor("cache_ropeT", [4, 64, PAST], F32, kind="ExternalInput").ap()
    k.din["cache_latT"] = nc.dram_tensor("cache_latT", [4, 256, PAST], F32, kind="ExternalInput").ap()
    k.dout = {}
    k.dout["o_y"] = nc.dram_tensor("o_y", [128, 8, T], F32, kind="ExternalOutput").ap()
    k.dout["o_lat"] = nc.dram_tensor("o_lat", [T, 256], F32, kind="ExternalOutput").ap()
    k.dout["o_rope"] = nc.dram_tensor("o_rope", [T, 64], F32, kind="ExternalOutput").ap()
    k.dout["o_sp"] = nc.dram_tensor("o_sp", [4, 128, 256], F32, kind="ExternalOutput").ap()
    k.dout["o_ss"] = nc.dram_tensor("o_ss", [4, 4, 128, 256], F32, kind="ExternalOutput").ap()
    if stage < 99:
        k.dout["dbg_h"] = nc.dram_tensor("dbg_h", [128, 8, T], F32, kind="ExternalOutput").ap()

    with ExitStack() as st:
        P = Prog(nc, st)
        k.P = P
        k.h, k.Bh = _sb(k, st, "h", [128, 8, T], F32)
        k.xn, k.Bxn = _sb(k, st, "xn", [128, 8, T], BF16)
        k.Bxnt = [Buf(f"xn_t{i}") for i in range(len(TILES))]
        k.cst, k.Bconst = _sb(k, st, "cst", [128, NCONST], F32)
        k.ones_f = k.cst[:, 0:128]
        k.ones_b, _ = _sb(k, st, "ones_b", [128, 128], BF16)
        k.ident = k.cst[:, 128:256]
        k.tri = k.cst[:, 256:384]
        k.revtri = k.cst[:, 384:512]
        k.mask01 = k.cst[:, 512:640]
        k.TRI = {64: k.cst[:, 256:384], 32: k.cst[:, 1536:1664]}
        k.REV = {64: k.cst[:, 384:512], 32: k.cst[:, 1664:1792]}
        k.MASK4 = {64: k.cst[:, 1024:1536].rearrange("p (h n) -> p h n", h=4),
                   32: k.cst[:, 2048:2560].rearrange("p (h n) -> p h n", h=4)}
        k.fA = k.cst[:, 707:708]
        k.fB = k.cst[:, 708:709]
        k.prevbias = k.cst[:, 709:710]
        k.qnorm = k.cst[:, 696:699]
        k.kvnorm = k.cst[:, 699:701]
        k.hnorm = k.cst[:, 701:703]
        k.Bw = Buf("dram_w")
        k.ps, k.psb = [], []
        for i in range(8):
            t = st.enter_context(nc.psum_tensor(f"ps{i}", [128, 512], F32))
            k.ps.append(t)
            k.psb.append(Buf(f"ps{i}"))
        k.ps_rr = 0
        k.ps_n = 8
        dma(k, k.cst[:, :], k.din["consts"], [k.Bw], [k.Bconst])
        for c in range(8):
            dma(k, k.h[:, c, :], k.din["xT"][:, c, :], [k.Bw], [k.Bh], q="sync" if c % 2 == 0 else "gpsimd")
        cp(k, k.ones_b[:, :], k.cst[:, 0:128], [k.Bconst], [k.Bconst])
        k.normw = {}
        specs = [("ffn1_0", "ffn1_norm", 0), ("ffn1_1", "ffn1_norm", 1), ("mix_0", "mix_norm", 0),
                 ("mix_1", "mix_norm", 1), ("ffn2_0", "ffn2_norm", 0), ("ffn2_1", "ffn2_norm", 1),
                 ("final", "final_norm", None)]
        for i, (nm, src, li) in enumerate(specs):
            k.normw[nm] = k.cst[:, 640 + 8 * i:648 + 8 * i]
        P.barrier()
        P.emit()

        ffn_phase(k, 0, 1)
        if stage >= 2:
            gla_phase(k)
        if stage >= 3:
            ffn_phase(k, 0, 2)
            ffn_phase(k, 1, 1)
        if stage >= 4:
            mla_phase(k)
        if stage >= 5:
            ffn_phase(k, 1, 2, final=True)
        if stage < 99:
            for c in range(8):
                dma(k, k.dout["dbg_h"][:, c, :], k.h[:, c, :], [k.Bh], [k.Bw])
        P.barrier()
        P.emit()
    return nc, k


NORM_SPECS = [("ffn1_norm", 0), ("ffn1_norm", 1), ("mix_norm", 0), ("mix_norm", 1), ("ffn2_norm", 0),
              ("ffn2_norm", 1), ("final_norm", None)]


def make_consts(inputs, core):
    c = np.zeros((128, NCONST), np.float32)
    c[:, 0:128] = 1.0
    c[:, 128:256] = np.eye(128, dtype=np.float32)
    j = np.arange(128)[:, None]
    i = np.arange(128)[None, :]
    same = (j // 64) == (i // 64)
    c[:, 256:384] = np.where(same & (j <= i), -1.0 / 16.0, 0.0)
    c[:, 384:512] = np.where(same & (j > i), -1.0 / 16.0, 0.0)
    c[:, 512:640] = np.where(same & (j <= i), 1.0, 0.0)
    for n, (src, li) in enumerate(NORM_SPECS):
        v = np.asarray(inputs[src], np.float32)
        v = v if li is None else v[li]
        c[:, 640 + 8 * n:648 + 8 * n] = v.reshape(8, 128).T
    c[:, 696:699] = np.asarray(inputs["mla_q_norm"], np.float32)[0].reshape(3, 128).T
    c[:, 699:701] = np.asarray(inputs["mla_kv_norm"], np.float32)[0].reshape(2, 128).T
    c[:, 701:703] = np.asarray(inputs["gla_head_norm"], np.float32)[0].reshape(2, 128).T
    for hh in range(4):
        c[:, 1024 + 128 * hh:1152 + 128 * hh] = c[:, 512:640]
    same32 = (j // 32) == (i // 32)
    c[:, 1536:1664] = np.where(same32 & (j <= i), -1.0 / 16.0, 0.0)
    c[:, 1664:1792] = np.where(same32 & (j > i), -1.0 / 16.0, 0.0)
    for hh in range(4):
        c[:, 2048 + 128 * hh:2176 + 128 * hh] = np.where(same32 & (j <= i), 1.0, 0.0)
    half = core % 2
    c[:, 707] = 1.0 if half == 0 else 0.0
    c[:, 708] = 1.0 if half == 1 else 0.0
    c[:, 709] = 0.0 if half == 1 else NEG
    return c


def rope_table(half):
    pos = np.concatenate([NMETA + half * NF + np.arange(NF), np.arange(NMETA), np.tile(PAST + np.arange(32), 4)])
    inv = (np.float32(10000.0) ** (-np.arange(32, dtype=np.float32) / np.float32(32))).astype(np.float32)
    ang = (pos.astype(np.float32)[None, :] * inv[:, None]).astype(np.float32)
    return np.ascontiguousarray(np.stack([np.cos(ang), np.sin(ang)], 1).astype(np.float32))


def core_inputs(inputs, core):
    b, half = core // 2, core % 2
    xp = np.asarray(inputs["x_prompt"], np.float32)
    xs = np.asarray(inputs["x_sample"], np.float32)
    meta = np.asarray(inputs["meta_tokens"], np.float32)
    rows = np.concatenate([xp[b, half * NF:(half + 1) * NF], meta, xs[4 * core:4 * core + 4].reshape(NSAMP, D)], 0)
    xT = np.ascontiguousarray(rows.T.reshape(8, 128, T).transpose(1, 0, 2))
    m = {"xT": xT, "consts": make_consts(inputs, core)}
    m["state"] = np.ascontiguousarray(np.asarray(inputs["state_gla"], np.float32)[0, 4 * core:4 * core + 4])
    m["cache_lat"] = np.ascontiguousarray(np.asarray(inputs["cache_mla_latent"], np.float32)[0, 4 * core:4 * core + 4])
    m["cache_latT"] = np.ascontiguousarray(m["cache_lat"].transpose(0, 2, 1))
    m["cache_ropeT"] = np.ascontiguousarray(
        np.asarray(inputs["cache_mla_rope"], np.float32)[0, 4 * core:4 * core + 4].transpose(0, 2, 1))
    m["w_ukT"] = np.ascontiguousarray(np.asarray(inputs["mla_w_uk"], np.float32)[0].transpose(2, 1, 0))
    m["rope"] = rope_table(half)
    for n in W_NAMES:
        m[n] = np.ascontiguousarray(np.asarray(inputs[n], np.float32))
    return m


def assemble(results):
    y_p = np.zeros((4, 4096, D), np.float32)
    y_s = np.zeros((32, 32, D), np.float32)
    sp = np.zeros((1, 4, 4, 128, 256), np.float32)
    ss = np.zeros((1, 32, 4, 128, 256), np.float32)
    lat_p = np.zeros((1, 4, NMETA + 4096, 256), np.float32)
    rope_p = np.zeros((1, 4, NMETA + 4096, 64), np.float32)
    lat_s = np.zeros((1, 32, 32, 256), np.float32)
    rope_s = np.zeros((1, 32, 32, 64), np.float32)
    for c in range(8):
        r = results[c]
        b, half = c // 2, c % 2
        yT = r["o_y"].transpose(1, 0, 2).reshape(D, T).T
        y_p[b, half * NF:(half + 1) * NF] = yT[:NF]
        y_s[4 * c:4 * c + 4] = yT[C_SAMP:].reshape(4, 32, D)
        if half == 1:
            sp[0, b] = r["o_sp"]
        ss[0, 4 * c:4 * c + 4] = r["o_ss"]
        lat_p[0, b, NMETA + half * NF:NMETA + (half + 1) * NF] = r["o_lat"][:NF]
        rope_p[0, b, NMETA + half * NF:NMETA + (half + 1) * NF] = r["o_rope"][:NF]
        if half == 0:
            lat_p[0, b, :NMETA] = r["o_lat"][C_META:C_META + NMETA]
            rope_p[0, b, :NMETA] = r["o_rope"][C_META:C_META + NMETA]
        lat_s[0, 4 * c:4 * c + 4] = r["o_lat"][C_SAMP:].reshape(4, 32, 256)
        rope_s[0, 4 * c:4 * c + 4] = r["o_rope"][C_SAMP:].reshape(4, 32, 64)
    return (y_p, y_s, sp, ss, lat_p, rope_p, lat_s, rope_s)


def kernel(**inputs):
    nc, k = build()
    in_maps = [core_inputs(inputs, c) for c in range(8)]
    res = run_bass_kernel_spmd(nc, in_maps, core_ids=list(range(8)))
    return assemble(res.results)
```
